# Optimizing a Trainium2 kernel written in Bass

```python
import jax, jax.numpy as jnp
from jax import lax
import numpy as np

D_MODEL = 2048
BATCH = 2
SEQ = 4096
DEPTH = 2
DEC_BATCH = 2
DEC_SEQ = 16384
PAST_LEN = 128

CONV_DIM = 1024
CONV_WIDTH = 31
MLSTM_HEADS = 4
MLSTM_HEAD_QK = 256
MLSTM_HEAD_V = 512
MLSTM_QK_DIM = MLSTM_HEADS * MLSTM_HEAD_QK
MLSTM_V_DIM = MLSTM_HEADS * MLSTM_HEAD_V
CHUNK = 128
D_FF = 5504
FFN_CONV_WIDTH = 3
EPS = 1e-6
IN_SIZES = (2 * CONV_DIM, MLSTM_QK_DIM, MLSTM_QK_DIM, MLSTM_V_DIM, MLSTM_V_DIM, 4 * MLSTM_HEADS, 2 * D_MODEL)
D_IN = sum(IN_SIZES)
IN_SPLITS = tuple(int(s) for s in np.cumsum(IN_SIZES)[:-1])

kernel_name = 'gated_conformer_mlstm_encoder'


def _rmsnorm(x, g):
    xf = x.astype(jnp.float32)
    y = xf * lax.rsqrt(jnp.mean(xf * xf, -1, keepdims=True) + EPS)
    return (y * g.astype(jnp.float32)).astype(x.dtype)


def _layernorm(x, g, b):
    xf = x.astype(jnp.float32)
    mu = jnp.mean(xf, -1, keepdims=True)
    xc = xf - mu
    y = xc * lax.rsqrt(jnp.mean(xc * xc, -1, keepdims=True) + EPS)
    return (y * g.astype(jnp.float32) + b.astype(jnp.float32)).astype(x.dtype)


def _depthwise_conv(x, w, b):
    width = w.shape[0]
    pad = (width - 1) // 2
    y = lax.conv_general_dilated(x, w[:, None, :].astype(x.dtype), window_strides=(1,),
                                 padding=[(pad, pad)], dimension_numbers=('NWC', 'WIO', 'NWC'),
                                 feature_group_count=x.shape[-1])
    return y + b.astype(x.dtype)


def _mlstm_chunkwise(q, k, v, ig, lf):
    B, H, S, DK = q.shape
    DV = v.shape[-1]
    nc = S // CHUNK

    def to_chunks(t):
        return jnp.moveaxis(t.reshape(B, H, nc, CHUNK, *t.shape[3:]), 2, 0)

    xs = tuple(to_chunks(t) for t in (q, k, v, ig, lf))
    causal = jnp.tril(jnp.ones((CHUNK, CHUNK), bool))

    def step(carry, inp):
        C, n, m = carry
        qb, kb, vb, igb, lfb = inp
        b = jnp.cumsum(lfb, -1)
        dmat = b[..., :, None] - b[..., None, :] + igb[..., None, :]
        dmat = jnp.where(causal, dmat, -jnp.inf)
        a = b + m[..., None]
        m_t = jnp.maximum(a, jnp.max(dmat, -1))
        w = jnp.exp(dmat - m_t[..., None])
        s = jnp.einsum('bhtd,bhsd->bhts', qb, kb) * w
        inter = jnp.exp(a - m_t)
        num = inter[..., None] * jnp.einsum('bhtd,bhde->bhte', qb, C) + jnp.einsum('bhts,bhse->bhte', s, vb)
        den = inter * jnp.einsum('bhtd,bhd->bht', qb, n) + jnp.sum(s, -1)
        h = num / jnp.maximum(jnp.abs(den), jnp.exp(-m_t))[..., None]
        b_last = b[..., -1]
        g = b_last[..., None] - b + igb
        m_new = jnp.maximum(b_last + m, jnp.max(g, -1))
        decay = jnp.exp(b_last + m - m_new)
        wk = jnp.exp(g - m_new[..., None])[..., None] * kb
        C_new = decay[..., None, None] * C + jnp.einsum('bhsd,bhse->bhde', wk, vb)
        n_new = decay[..., None] * n + jnp.sum(wk, -2)
        return (C_new, n_new, m_new), h

    init = (jnp.zeros((B, H, DK, DV), jnp.float32), jnp.zeros((B, H, DK), jnp.float32),
            jnp.zeros((B, H), jnp.float32))
    _, hc = lax.scan(step, init, xs)
    return jnp.moveaxis(hc, 0, 2).reshape(B, H, S, DV)


def _mixer(xn, w_in, b_gates, conv_dw_w, conv_dw_b, conv_ln_g, conv_ln_b, w_conv_out,
           mlstm_head_g, w_mlstm_out, w_out):
    B, S, _ = xn.shape
    proj = xn @ w_in.astype(xn.dtype)
    conv_in, q, k, v, o_pre, gate_pre, merge_pre = jnp.split(proj, IN_SPLITS, axis=-1)

    u = conv_in[..., :CONV_DIM] * jax.nn.sigmoid(conv_in[..., CONV_DIM:])
    u = _depthwise_conv(u, conv_dw_w, conv_dw_b)
    u = jax.nn.silu(_layernorm(u, conv_ln_g, conv_ln_b))
    y_conv = u @ w_conv_out.astype(u.dtype)

    def heads(t, d):
        return jnp.transpose(t.astype(jnp.float32).reshape(B, S, MLSTM_HEADS, d), (0, 2, 1, 3))
    qh = heads(q, MLSTM_HEAD_QK) * (MLSTM_HEAD_QK ** -0.5)
    kh = heads(k, MLSTM_HEAD_QK)
    vh = heads(v, MLSTM_HEAD_V)
    gts = (gate_pre + b_gates.astype(gate_pre.dtype)).astype(jnp.float32).reshape(B, S, 4, MLSTM_HEADS)
    gts = jnp.transpose(gts, (2, 0, 3, 1))
    ig_f, lf_f, ig_b, lf_b = gts[0], jax.nn.log_sigmoid(gts[1]), gts[2], jax.nn.log_sigmoid(gts[3])
    h_f = _mlstm_chunkwise(qh, kh, vh, ig_f, lf_f)
    flip = lambda t: jnp.flip(t, axis=2)
    h_b = flip(_mlstm_chunkwise(flip(qh), flip(kh), flip(vh), flip(ig_b), flip(lf_b)))
    h = h_f + h_b
    h = h * lax.rsqrt(jnp.mean(h * h, -1, keepdims=True) + EPS)
    h = h * mlstm_head_g.astype(jnp.float32).reshape(MLSTM_HEADS, 1, MLSTM_HEAD_V)
    h = jnp.transpose(h, (0, 2, 1, 3)).reshape(B, S, MLSTM_V_DIM).astype(xn.dtype)
    h = h * jax.nn.sigmoid(o_pre)
    y_mlstm = h @ w_mlstm_out.astype(h.dtype)

    gates = jax.nn.sigmoid(merge_pre)
    mixed = gates[..., :D_MODEL] * y_conv + gates[..., D_MODEL:] * y_mlstm
    return mixed @ w_out.astype(mixed.dtype)


def _ffn(xn, w_up, ffn_dw_w, ffn_dw_b, w_down):
    up = xn @ w_up.astype(xn.dtype)
    gate, val = up[..., :D_FF], up[..., D_FF:]
    hid = jax.nn.gelu(_depthwise_conv(gate, ffn_dw_w, ffn_dw_b), approximate=False) * val
    return hid @ w_down.astype(hid.dtype)


def _trunk(x, norm_mix_g, w_in, b_gates, conv_dw_w, conv_dw_b, conv_ln_g, conv_ln_b, w_conv_out,
           mlstm_head_g, w_mlstm_out, w_out, norm_ffn_g, w_up, ffn_dw_w, ffn_dw_b, w_down, norm_final_g):
    for l in range(DEPTH):
        x = x + _mixer(_rmsnorm(x, norm_mix_g[l]), w_in[l], b_gates[l], conv_dw_w[l], conv_dw_b[l],
                       conv_ln_g[l], conv_ln_b[l], w_conv_out[l], mlstm_head_g[l], w_mlstm_out[l], w_out[l])
        x = x + _ffn(_rmsnorm(x, norm_ffn_g[l]), w_up[l], ffn_dw_w[l], ffn_dw_b[l], w_down[l])
    return _rmsnorm(x, norm_final_g)


def setup_inputs(seed: int = 0) -> dict:
    key = jax.random.key(seed)
    ks = jax.random.split(key, 20)
    nrm = lambda k, shape, s: jax.random.normal(k, shape, jnp.float32) * s
    ig_bias = nrm(ks[4], (DEPTH, 2, MLSTM_HEADS), 0.1)
    fg_bias = jnp.linspace(3.0, 6.0, MLSTM_HEADS, dtype=jnp.float32)[None, None, :] + nrm(ks[5], (DEPTH, 2, MLSTM_HEADS), 0.1)
    b_gates = jnp.stack([ig_bias[:, 0], fg_bias[:, 0], ig_bias[:, 1], fg_bias[:, 1]], axis=1).reshape(DEPTH, 4 * MLSTM_HEADS)
    return {
        'x_prompt': nrm(ks[0], (BATCH, SEQ, D_MODEL), 1.0),
        'x_sample': nrm(ks[1], (DEC_BATCH, DEC_SEQ, D_MODEL), 1.0),
        'norm_mix_g': 1.0 + nrm(ks[2], (DEPTH, D_MODEL), 0.1),
        'w_in': nrm(ks[3], (DEPTH, D_MODEL, D_IN), D_MODEL ** -0.5),
        'b_gates': b_gates,
        'conv_dw_w': nrm(ks[6], (DEPTH, CONV_WIDTH, CONV_DIM), CONV_WIDTH ** -0.5),
        'conv_dw_b': nrm(ks[7], (DEPTH, CONV_DIM), 0.02),
        'conv_ln_g': 1.0 + nrm(ks[8], (DEPTH, CONV_DIM), 0.1),
        'conv_ln_b': nrm(ks[9], (DEPTH, CONV_DIM), 0.02),
        'w_conv_out': nrm(ks[10], (DEPTH, CONV_DIM, D_MODEL), CONV_DIM ** -0.5),
        'mlstm_head_g': 1.0 + nrm(ks[11], (DEPTH, MLSTM_V_DIM), 0.1),
        'w_mlstm_out': nrm(ks[12], (DEPTH, MLSTM_V_DIM, D_MODEL), MLSTM_V_DIM ** -0.5),
        'w_out': nrm(ks[13], (DEPTH, D_MODEL, D_MODEL), D_MODEL ** -0.5),
        'norm_ffn_g': 1.0 + nrm(ks[14], (DEPTH, D_MODEL), 0.1),
        'w_up': nrm(ks[15], (DEPTH, D_MODEL, 2 * D_FF), D_MODEL ** -0.5),
        'ffn_dw_w': nrm(ks[16], (DEPTH, FFN_CONV_WIDTH, D_FF), FFN_CONV_WIDTH ** -0.5),
        'ffn_dw_b': nrm(ks[17], (DEPTH, D_FF), 0.02),
        'w_down': nrm(ks[18], (DEPTH, D_FF, D_MODEL), D_FF ** -0.5),
        'norm_final_g': 1.0 + nrm(ks[19], (D_MODEL,), 0.1),
    }


def reference(x_prompt, x_sample, norm_mix_g, w_in, b_gates, conv_dw_w, conv_dw_b, conv_ln_g, conv_ln_b,
              w_conv_out, mlstm_head_g, w_mlstm_out, w_out, norm_ffn_g, w_up, ffn_dw_w, ffn_dw_b, w_down,
              norm_final_g):
    y_prompt = _trunk(x_prompt, norm_mix_g, w_in, b_gates, conv_dw_w, conv_dw_b, conv_ln_g, conv_ln_b,
                      w_conv_out, mlstm_head_g, w_mlstm_out, w_out, norm_ffn_g, w_up, ffn_dw_w, ffn_dw_b,
                      w_down, norm_final_g)
    y_sample = _trunk(x_sample, norm_mix_g, w_in, b_gates, conv_dw_w, conv_dw_b, conv_ln_g, conv_ln_b,
                      w_conv_out, mlstm_head_g, w_mlstm_out, w_out, norm_ffn_g, w_up, ffn_dw_w, ffn_dw_b,
                      w_down, norm_final_g)
    return (y_prompt, y_sample)
```

```python
import numpy as np
from contextlib import ExitStack

import concourse.bass as bass
import concourse.mybir as mybir
from concourse.bass_utils import run_bass_kernel_spmd

F32 = mybir.dt.float32
BF16 = mybir.dt.bfloat16
AF = mybir.ActivationFunctionType
ALU = mybir.AluOpType

D = 2048
KC = 16
CONV = 1024
CW = 31
HEADS = 4
DK = 256
DV = 512
DFF = 5504
NF = 43
DIN = 12304
DEPTH = 2
EPS = 1e-6
C_A, C_B, C_Q, C_K, C_V, C_O, C_G, C_M = 0, 1024, 2048, 3072, 4096, 6144, 8192, 8208
NEG = -30000.0

ENG = ['pe', 'act', 'dve', 'pool', 'sp']
BLK = {'pe': 'tensor', 'act': 'scalar', 'dve': 'vector', 'pool': 'gpsimd', 'sp': 'sync'}


class Prog:
    def __init__(self, nc, es):
        self.nc = nc
        self.es = es
        self.sem = {e: es.enter_context(nc.semaphore('s_' + e)) for e in ENG}
        self.cnt = {e: 0 for e in ENG}
        self.dsems = {}
        self.ops = {e: [] for e in ENG}
        self.waited = {e: {} for e in ENG}
        self.res = {}
        self.nops = 0

    def semh(self, k):
        if k in self.sem:
            return self.sem[k]
        return self.dsems[k][0]

    def _deps(self, r, w):
        evs = {}

        def add(k, v):
            if evs.get(k, 0) < v:
                evs[k] = v

        for key in r:
            st = self.res.get(key)
            if st and st[0]:
                add(*st[0])
        for key in w:
            st = self.res.get(key)
            if st:
                if st[0]:
                    add(*st[0])
                for k, v in st[1].items():
                    add(k, v)
        return evs

    def _commit(self, ev, r, w):
        k, v = ev
        for key in r:
            st = self.res.setdefault(key, [None, {}])
            if st[1].get(k, 0) < v:
                st[1][k] = v
        for key in w:
            self.res[key] = [ev, {}]

    def _waits(self, e, evs, skip_self_pe=True):
        waits = []
        for k, v in evs.items():
            if skip_self_pe and e == 'pe' and k == 'pe':
                continue
            if self.waited[e].get(k, 0) >= v:
                continue
            self.waited[e][k] = v
            waits.append((k, v))
        return waits

    def op(self, e, fns, r=(), w=()):
        if isinstance(fns, tuple):
            fns = [fns]
        waits = self._waits(e, self._deps(r, w))
        self.cnt[e] += 1
        ev = (e, self.cnt[e])
        self.ops[e].append((waits, fns, (e, 1)))
        self._commit(ev, r, w)
        self.nops += len(fns)

    def dma(self, e, slot, out, in_, r=(), w=()):
        waits = self._waits(e, self._deps(r, w), skip_self_pe=False)
        k = 'd:' + slot
        if k not in self.dsems:
            self.dsems[k] = [self.es.enter_context(self.nc.semaphore('d_' + slot)), 0]
        self.dsems[k][1] += 16
        ev = (k, self.dsems[k][1])
        self.ops[e].append((waits, [('dma_start', dict(out=out, in_=in_))], (k, 16)))
        self._commit(ev, r, w)
        self.nops += 1

    def barrier(self):
        targets = {e: self.cnt[e] for e in ENG}
        for k, (h, c) in self.dsems.items():
            targets[k] = c
        for e in ENG:
            waits = []
            for k, v in targets.items():
                if v > 0 and self.waited[e].get(k, 0) < v:
                    self.waited[e][k] = v
                    waits.append((k, v))
            if waits:
                self.ops[e].append((waits, [], None))
        self.res = {}

    def emit(self):
        nc = self.nc
        with nc.Block() as block:
            for e in ENG:
                ops = self.ops[e]
                if not ops:
                    continue

                def body(eng, ops=ops):
                    for waits, fns, inc in ops:
                        for k, v in waits:
                            eng.wait_ge(self.semh(k), v)
                        ins = None
                        for (nm, kw) in fns:
                            ins = getattr(eng, nm)(**kw)
                        if inc is not None and ins is not None:
                            ins.then_inc(self.semh(inc[0]), inc[1])

                getattr(block, BLK[e])(body)
                self.ops[e] = []


def I(name, **kw):
    return (name, kw)


class Ring:
    def __init__(self, P, slots, name='w'):
        self.P = P
        self.slots = slots
        self.i = 0
        self.name = name

    def load(self, parts):
        s = self.i % len(self.slots)
        self.i += 1
        slot = self.slots[s]
        key = (self.name, s)
        for (src, nk, c0, ncols) in parts:
            self.P.dma('pool', f'{self.name}{s}', slot[:, 0:nk, c0:c0 + ncols],
                       src.rearrange("(kc p) c -> p kc c", p=128), w=[key])
        return slot, key


def tiles_of(nch, per=4):
    out = []
    c = 0
    while c < nch:
        n = min(per, nch - c)
        out.append((c, n))
        c += n
    return out


def build_program(NCH, depth=DEPTH):
    NT = NCH * 128
    nc = bass.Bass("TRN2", target_bir_lowering=False)

    def din(name, shape, dt=F32):
        return nc.dram_tensor(name, list(shape), dt, kind="ExternalInput").ap()

    def dscr(name, shape, dt=F32):
        return nc.dram_tensor(name, list(shape), dt).ap()

    x_in = din("x", [NT, D])
    msk = din("msk", [128, NCH])
    w_in = din("w_in", [depth, D, DIN])
    w_conv_out = din("w_conv_out", [depth, CONV, D])
    w_mlstm_out = din("w_mlstm_out", [depth, D, D])
    w_out = din("w_out", [depth, D, D])
    w_up = din("w_up", [depth, D, 2 * DFF])
    w_down = din("w_down", [depth, DFF, D])
    norm_mix_g = din("norm_mix_g", [depth, D])
    norm_ffn_g = din("norm_ffn_g", [depth, D])
    norm_final_g = din("norm_final_g", [D])
    b_gates = din("b_gates", [depth, 16])
    p_cw = din("p_cw", [depth, 128, 8, CW])
    p_cv = din("p_cv", [depth, 128, 3, 8])
    p_hg = din("p_hg", [depth, 128, 16])
    p_fw = din("p_fw", [depth, 128, NF, 4])
    c_tri = din("c_tri", [4, 128, 128])
    y_out = nc.dram_tensor("y", [NT, D], F32, kind="ExternalOutput").ap()

    uT = dscr("uT", [CONV, NT + 32])
    qTc = dscr("qTc", [NCH, 128, 1024], BF16)
    kTc = dscr("kTc", [NCH, 128, 1024], BF16)
    ktok = dscr("ktok", [NT, 1024], BF16)
    vtok = dscr("vtok", [NT, 2048], BF16)
    gts = dscr("gts", [NT, 16])
    ogT = dscr("ogT", [D, NT], BF16)
    gmT = dscr("gmT", [2 * D, NT], BF16)
    mixT = dscr("mixT", [D, NT])
    hfT = dscr("hfT", [D, NT])
    hbT = dscr("hbT", [D, NT])
    xmid = dscr("xmid", [NT + 2, D])
    x1 = dscr("x1", [NT, D])

    def fm(ap):
        return ap.rearrange("(j p) t -> p j t", p=128)

    es = ExitStack()
    with es:
        P = Prog(nc, es)

        uid = [0]

        def sb(st, name, shape, dt):
            uid[0] += 1
            return st.enter_context(nc.sbuf_tensor(f"{name}_{uid[0]}", list(shape), dt))

        def ps(st, name, shape, dt=F32):
            uid[0] += 1
            return st.enter_context(nc.psum_tensor(f"{name}_{uid[0]}", list(shape), dt))

        identf = sb(es, "identf", [128, 128], F32)
        identb = sb(es, "identb", [128, 128], BF16)
        ones1k = sb(es, "ones1k", [128, 128], F32)
        ones512 = sb(es, "ones512", [128, 128], F32)
        onesb = sb(es, "onesb", [128, 128], BF16)
        tri = sb(es, "tri", [128, 4, 128], F32)
        zer = sb(es, "zer", [128, 2048], F32)
        mcol = sb(es, "mcol", [128, NCH], F32)
        epsc = sb(es, "epsc", [128, 1], F32)

        P.op('pool', I('memset', ap=identf[:], constant=1.0), w=['identf'])
        P.op('pool', I('affine_select', out=identf[:], in_=identf[:], pattern=[[-1, 128]],
                       compare_op=ALU.is_equal, fill=0.0, base=0, channel_multiplier=1),
             r=['identf'], w=['identf'])
        P.op('dve', I('tensor_copy', out=identb[:], in_=identf[:]), r=['identf'], w=['identb'])
        P.op('dve', I('memset', ap=ones1k[:], constant=1.0 / 1024), w=['ones1k'])
        P.op('dve', I('memset', ap=ones512[:], constant=1.0 / 512), w=['ones512'])
        P.op('dve', I('memset', ap=onesb[:], constant=1.0), w=['onesb'])
        P.op('dve', I('memset', ap=zer[:], constant=0.0), w=['zer'])
        P.op('dve', I('memset', ap=epsc[:], constant=EPS), w=['epsc'])
        P.dma('sp', 'c0', tri[:], c_tri.rearrange("a p t -> p a t"), w=['tri'])
        P.dma('sp', 'c1', mcol[:], msk, w=['mcol'])
        P.dma('sp', 'z0', fm(uT[:, 0:16]), zer[:, 0:128].rearrange("p (j t) -> p j t", j=8), r=['zer'])
        P.dma('sp', 'z1', fm(uT[:, NT + 16:NT + 32]), zer[:, 0:128].rearrange("p (j t) -> p j t", j=8), r=['zer'])
        P.dma('sp', 'z2', xmid[0:1, :], zer[0:1, :], r=['zer'])
        P.dma('sp', 'z3', xmid[NT + 1:NT + 2, :], zer[0:1, :], r=['zer'])
        P.barrier()
        P.emit()

        atiles = tiles_of(NCH, 4)

        def rmsnorm_rows(xt, nr, xk, gvec, gk, outt, outk, stat, sqj):
            P.op('dve', I('memset', ap=stat[:, 0:1], constant=0.0), w=['stat0'])
            P.op('act', I('activation', out=sqj[0:nr, :], in_=xt[0:nr, :], func=AF.Square, accum_out=stat[0:nr, 0:1]),
                 r=[xk, 'stat0'], w=['sqj', 'stat0'])
            P.op('act', I('activation', out=stat[0:nr, 1:2], in_=stat[0:nr, 0:1], func=AF.Sqrt, scale=1.0 / D,
                          bias=epsc[0:nr, :]), r=['stat0', 'epsc'], w=['stat1'])
            P.op('dve', I('reciprocal', out=stat[0:nr, 2:3], in_=stat[0:nr, 1:2]), r=['stat1'], w=['stat2'])
            P.op('dve', I('scalar_tensor_tensor', out=outt[0:nr, :], in0=xt[0:nr, :], scalar=stat[0:nr, 2:3],
                          in1=gvec[0:nr, :], op0=ALU.mult, op1=ALU.mult),
                 r=[xk, 'stat2', gk], w=[outk])

        for l in range(depth):
            xsrc = x_in if l == 0 else x1
            xdst = x1 if l < depth - 1 else y_out
            last = (l == depth - 1)
            W_in = w_in[l]

            with ExitStack() as st:
                gB = sb(st, "gB", [128, D], F32)
                wg = sb(st, "wg", [128, KC, 16], BF16)
                bg = sb(st, "bg", [128, 16], F32)
                xin = [sb(st, f"xin{i}", [128, D], F32) for i in range(2)]
                xnb = sb(st, "xnb", [128, D], BF16)
                sqj = sb(st, "sqj", [128, D], BF16)
                stat = sb(st, "stat", [128, 4], F32)
                xnT = sb(st, "xnT", [128, KC, 512], BF16)
                sig = sb(st, "sig", [128, 4, 512], F32)
                ust = [sb(st, f"ust{i}", [128, 4, 512], F32) for i in range(2)]
                bst = [sb(st, f"bst{i}", [128, 4, 512], BF16) for i in range(2)]
                qst = sb(st, "qst", [128, 4, 8, 128], BF16)
                kst = sb(st, "kst", [128, 4, 8, 128], BF16)
                ktk = sb(st, "ktk", [128, 4, 1024], BF16)
                vst = sb(st, "vst", [128, 4, 2048], BF16)
                gst = sb(st, "gst", [128, 4, 16], F32)
                gtmp = sb(st, "gtmp", [128, 16], F32)
                slots = [sb(st, f"ring{i}", [128, KC, 512], BF16) for i in range(4)]
                tp = [ps(st, f"tp{i}", [128, 1024], BF16) for i in range(2)]
                mm = [ps(st, f"mm{i}", [128, 512], F32) for i in range(5)]
                gps = ps(st, "gps", [128, 16], F32)
                ring = Ring(P, slots)
                cnt = dict(mm=0, tp=0, bst=0)

                def next_mm():
                    i = cnt['mm'] % len(mm)
                    cnt['mm'] += 1
                    return mm[i], ('mm', i)

                def next_tp():
                    i = cnt['tp'] % len(tp)
                    cnt['tp'] += 1
                    return tp[i], ('tp', i)

                P.dma('sp', 'gB', gB[:], norm_mix_g[l].partition_broadcast(128), w=['gvec'])
                P.dma('sp', 'bg', bg[:], b_gates[l].partition_broadcast(128), w=['bg'])
                P.dma('pool', 'wg', wg[:], W_in[:, C_G:C_G + 16].rearrange("(kc p) c -> p kc c", p=128), w=['wg'])

                def load_x(chunk):
                    b = chunk % 2
                    P.dma('sp', f'xin{b}', xin[b][:], xsrc[chunk * 128:(chunk + 1) * 128, :], w=[('xin', b)])

                def norm_chunk(chunk, ci):
                    b = chunk % 2
                    rmsnorm_rows(xin[b], 128, ('xin', b), gB, 'gvec', xnb, 'xnb', stat, sqj)
                    for g in range(2):
                        tpt, tpk = next_tp()
                        P.op('pe', [I('transpose', out=tpt[:, k * 128:(k + 1) * 128],
                                      in_=xnb[:, (g * 8 + k) * 128:(g * 8 + k + 1) * 128], identity=identb[:])
                                    for k in range(8)], r=['xnb', 'identb'], w=[tpk])
                        dst = xnT[:, g * 8:(g + 1) * 8, ci * 128:(ci + 1) * 128]
                        src = tpt[:].rearrange("p (k t) -> p k t", k=8)
                        if g == 0:
                            P.op('act', I('copy', out=dst, in_=src), r=[tpk], w=[('xnT', ci)])
                        else:
                            P.op('dve', I('tensor_copy', out=dst, in_=src), r=[tpk], w=[('xnT', ci)])

                def a1_tile(ti, c0, n):
                    nt = n * 128
                    t0 = c0 * 128
                    xk = [('xnT', ci) for ci in range(n)]
                    for ci in range(n):
                        if c0 + ci + 1 < NCH:
                            load_x(c0 + ci + 1)
                        norm_chunk(c0 + ci, ci)

                    def ws_block(col0, evac):
                        slot, skey = ring.load([(W_in[:, col0:col0 + 512], KC, 0, 512)])
                        for ct in range(4):
                            pt, pk = next_mm()
                            P.op('pe', [I('matmul', out=pt[:, 0:nt], lhsT=slot[:, kc, ct * 128:(ct + 1) * 128],
                                          rhs=xnT[:, kc, 0:nt], start=(kc == 0), stop=(kc == KC - 1))
                                        for kc in range(KC)], r=[skey] + xk, w=[pk])
                            evac(ct, pt, pk)

                    for i in range(2):
                        def ev_b(ct, pt, pk):
                            P.op('act', I('activation', out=sig[:, ct, 0:nt], in_=pt[:, 0:nt], func=AF.Sigmoid),
                                 r=[pk], w=[('sig', ct)])
                        ws_block(C_B + i * 512, ev_b)
                        us = ust[i]

                        def ev_a(ct, pt, pk):
                            P.op('dve', I('tensor_tensor', out=us[:, ct, 0:nt], in0=pt[:, 0:nt], in1=sig[:, ct, 0:nt],
                                          op=ALU.mult), r=[pk, ('sig', ct)], w=[('ust', i)])
                        ws_block(C_A + i * 512, ev_a)
                        P.dma('sp', f'ust{i}', fm(uT[i * 512:(i + 1) * 512, 16 + t0:16 + t0 + nt]), us[:, :, 0:nt],
                              r=[('ust', i)])
                    for (cbase, stg, sname, dst, scl) in ((C_Q, qst, 'qst', qTc, DK ** -0.5), (C_K, kst, 'kst', kTc, 1.0)):
                        for i in range(2):
                            def ev_q(ct, pt, pk):
                                P.op('act', I('mul', out=stg[:, 0:n, i * 4 + ct, :],
                                              in_=pt[:, 0:nt].rearrange("p (c t) -> p c t", c=n), mul=scl),
                                     r=[pk], w=[sname])
                            ws_block(cbase + i * 512, ev_q)
                        P.dma('sp', sname, dst[c0:c0 + n].rearrange("c p f -> p c f"),
                              stg[:, 0:n].rearrange("p c j t -> p c (j t)"), r=[sname])
                    for ci in range(n):
                        tpt, tpk = next_tp()
                        P.op('pe', [I('transpose', out=tpt[:, j * 128:(j + 1) * 128], in_=kst[:, ci, j, :], identity=identb[:])
                                    for j in range(8)], r=['kst', 'identb'], w=[tpk])
                        P.op('dve', I('tensor_copy', out=ktk[:, ci, :], in_=tpt[:]), r=[tpk], w=['ktk'])
                    P.dma('sp', 'ktk', ktok[t0:t0 + nt, :].rearrange("(c p) f -> p c f", p=128), ktk[:, 0:n, :], r=['ktk'])
                    for i in range(4):
                        slot, skey = ring.load([(W_in[:, C_V + i * 512:C_V + (i + 1) * 512], KC, 0, 512)])
                        for ci in range(n):
                            pt, pk = next_mm()
                            P.op('pe', [I('matmul', out=pt[:, :], lhsT=xnT[:, kc, ci * 128:(ci + 1) * 128], rhs=slot[:, kc, :],
                                          start=(kc == 0), stop=(kc == KC - 1)) for kc in range(KC)],
                                 r=[skey, ('xnT', ci)], w=[pk])
                            if (i + ci) % 2 == 0:
                                P.op('act', I('copy', out=vst[:, ci, i * 512:(i + 1) * 512], in_=pt[:, :]), r=[pk], w=['vst'])
                            else:
                                P.op('dve', I('tensor_copy', out=vst[:, ci, i * 512:(i + 1) * 512], in_=pt[:, :]), r=[pk], w=['vst'])
                    P.dma('sp', 'vst', vtok[t0:t0 + nt, :].rearrange("(c p) f -> p c f", p=128), vst[:, 0:n, :], r=['vst'])
                    for ci in range(n):
                        P.op('pe', [I('matmul', out=gps[:, :], lhsT=xnT[:, kc, ci * 128:(ci + 1) * 128], rhs=wg[:, kc, :],
                                      start=(kc == 0), stop=(kc == KC - 1)) for kc in range(KC)],
                             r=['wg', ('xnT', ci)], w=['gps'])
                        P.op('dve', I('tensor_tensor', out=gst[:, ci, :], in0=gps[:, :], in1=bg[:], op=ALU.add),
                             r=['gps', 'bg'], w=['gst'])
                        gv = gst[:, ci, :].rearrange("p (a b) -> p a b", a=2)[:, :, 4:8]
                        tv = gtmp[:, :].rearrange("p (a b) -> p a b", a=2)[:, :, 4:8]
                        P.op('act', I('activation', out=tv, in_=gv, func=AF.Exp, scale=-1.0), r=['gst'], w=['gtmp'])
                        P.op('act', I('activation', out=tv, in_=tv, func=AF.Ln, bias=1.0), r=['gtmp'], w=['gtmp'])
                        P.op('dve', I('tensor_scalar', out=gv, in0=tv, scalar1=-1.0, scalar2=None, op0=ALU.mult),
                             r=['gtmp'], w=['gst'])
                    P.dma('sp', 'gst', gts[t0:t0 + nt, :].rearrange("(c p) g -> p c g", p=128), gst[:, 0:n, :], r=['gst'])
                    for (cbase, nblk, dst) in ((C_O, 4, ogT), (C_M, 8, gmT)):
                        for i in range(nblk):
                            bb = cnt['bst'] % 2
                            cnt['bst'] += 1
                            bs = bst[bb]

                            def ev_s(ct, pt, pk):
                                P.op('act', I('activation', out=bs[:, ct, 0:nt], in_=pt[:, 0:nt], func=AF.Sigmoid),
                                     r=[pk], w=[('bst', bb)])
                            ws_block(cbase + i * 512, ev_s)
                            P.dma('sp', f'bst{bb}', fm(dst[i * 512:(i + 1) * 512, t0:t0 + nt]), bs[:, :, 0:nt], r=[('bst', bb)])

                load_x(0)
                for ti, (c0, n) in enumerate(atiles):
                    a1_tile(ti, c0, n)
                P.barrier()
                P.emit()

            with ExitStack() as st:
                cw = sb(st, "cw", [128, 8, CW], F32)
                cv = sb(st, "cv", [128, 3, 8], F32)
                uin = [sb(st, f"uin{i}", [128, 8, 544], F32) for i in range(2)]
                acc = sb(st, "acc", [128, 8, 512], F32)
                ysq = [sb(st, f"ysq{i}", [128, 512], F32) for i in range(2)]
                mean = sb(st, "mean", [128, 512], F32)
                var = sb(st, "var", [128, 512], F32)
                rstd = sb(st, "rstd", [128, 512], F32)
                zt = [sb(st, f"zt{i}", [128, 512], F32) for i in range(2)]
                sT = sb(st, "sT", [128, 8, 512], BF16)
                gA = [sb(st, f"gA{i}", [128, 4, 512], BF16) for i in range(2)]
                mst = [sb(st, f"mst{i}", [128, 4, 512], F32) for i in range(2)]
                slots = [sb(st, f"ring{i}", [128, KC, 512], BF16) for i in range(3)]
                pmean = ps(st, "pmean", [128, 512])
                psq = ps(st, "psq", [128, 512])
                mm = [ps(st, f"mm{i}", [128, 512], F32) for i in range(4)]
                ring = Ring(P, slots)
                cnt = dict(mm=0, g=0)

                def next_mm():
                    i = cnt['mm'] % len(mm)
                    cnt['mm'] += 1
                    return mm[i], ('mm', i)

                P.dma('sp', 'cw', cw[:], p_cw[l], w=['cw'])
                P.dma('sp', 'cv', cv[:], p_cv[l], w=['cv'])

                def load_u(ti):
                    c0, n = atiles[ti]
                    nt = n * 128
                    t0 = c0 * 128
                    b = ti % 2
                    P.dma('sp', f'uin{b}', uin[b][:, :, 0:nt + 32], fm(uT[:, t0:t0 + nt + 32]), w=[('uin', b)])

                def a2_tile(ti, c0, n):
                    nt = n * 128
                    t0 = c0 * 128
                    b = ti % 2
                    if ti + 1 < len(atiles):
                        load_u(ti + 1)
                    ub = uin[b]
                    for wtap in range(CW):
                        for j in range(8):
                            if wtap == 0:
                                P.op('dve', I('tensor_scalar', out=acc[:, j, 0:nt], in0=ub[:, j, 1:1 + nt],
                                              scalar1=cw[:, j, 0:1], scalar2=cv[:, 0, j:j + 1], op0=ALU.mult, op1=ALU.add),
                                     r=[('uin', b), 'cw', 'cv'], w=[('acc', j)])
                            else:
                                P.op('dve', I('scalar_tensor_tensor', out=acc[:, j, 0:nt], in0=ub[:, j, 1 + wtap:1 + wtap + nt],
                                              scalar=cw[:, j, wtap:wtap + 1], in1=acc[:, j, 0:nt], op0=ALU.mult, op1=ALU.add),
                                     r=[('uin', b), 'cw', ('acc', j)], w=[('acc', j)])
                    for j in range(8):
                        yb = j % 2
                        P.op('act', I('activation', out=ysq[yb][:, 0:nt], in_=acc[:, j, 0:nt], func=AF.Square),
                             r=[('acc', j)], w=[('ysq', yb)])
                        P.op('pe', I('matmul', out=pmean[:, 0:nt], lhsT=ones1k[:], rhs=acc[:, j, 0:nt],
                                     start=(j == 0), stop=(j == 7)), r=[('acc', j), 'ones1k'], w=['pmean'])
                        P.op('pe', I('matmul', out=psq[:, 0:nt], lhsT=ones1k[:], rhs=ysq[yb][:, 0:nt],
                                     start=(j == 0), stop=(j == 7)), r=[('ysq', yb), 'ones1k'], w=['psq'])
                    P.op('act', I('copy', out=mean[:, 0:nt], in_=pmean[:, 0:nt]), r=['pmean'], w=['mean'])
                    P.op('dve', I('tensor_tensor', out=var[:, 0:nt], in0=mean[:, 0:nt], in1=mean[:, 0:nt], op=ALU.mult),
                         r=['mean'], w=['var'])
                    P.op('dve', I('tensor_tensor', out=var[:, 0:nt], in0=psq[:, 0:nt], in1=var[:, 0:nt], op=ALU.subtract),
                         r=['psq', 'var'], w=['var'])
                    P.op('act', I('activation', out=var[:, 0:nt], in_=var[:, 0:nt], func=AF.Sqrt, bias=epsc[:]),
                         r=['var', 'epsc'], w=['var'])
                    P.op('dve', I('reciprocal', out=rstd[:, 0:nt], in_=var[:, 0:nt]), r=['var'], w=['rstd'])
                    for j in range(8):
                        zb = j % 2
                        P.op('dve', I('tensor_tensor', out=zt[zb][:, 0:nt], in0=acc[:, j, 0:nt], in1=mean[:, 0:nt],
                                      op=ALU.subtract), r=[('acc', j), 'mean'], w=[('zt', zb)])
                        P.op('dve', I('tensor_tensor', out=zt[zb][:, 0:nt], in0=zt[zb][:, 0:nt], in1=rstd[:, 0:nt],
                                      op=ALU.mult), r=[('zt', zb), 'rstd'], w=[('zt', zb)])
                        P.op('act', I('activation', out=sT[:, j, 0:nt], in_=zt[zb][:, 0:nt], func=AF.Silu,
                                      scale=cv[:, 1, j:j + 1], bias=cv[:, 2, j:j + 1]),
                             r=[('zt', zb), 'cv'], w=['sT'])
                    for blk in range(4):
                        gb = cnt['g'] % 2
                        cnt['g'] += 1
                        P.dma('sp', f'gA{gb}', gA[gb][:, :, 0:nt], fm(gmT[blk * 512:(blk + 1) * 512, t0:t0 + nt]),
                              w=[('gA', gb)])
                        slot, skey = ring.load([(w_conv_out[l][:, blk * 512:(blk + 1) * 512], 8, 0, 512)])
                        for ct in range(4):
                            pt, pk = next_mm()
                            P.op('pe', [I('matmul', out=pt[:, 0:nt], lhsT=slot[:, kc, ct * 128:(ct + 1) * 128],
                                          rhs=sT[:, kc, 0:nt], start=(kc == 0), stop=(kc == 7)) for kc in range(8)],
                                 r=[skey, 'sT'], w=[pk])
                            P.op('dve', I('tensor_tensor', out=mst[gb][:, ct, 0:nt], in0=pt[:, 0:nt], in1=gA[gb][:, ct, 0:nt],
                                          op=ALU.mult), r=[pk, ('gA', gb)], w=[('mst', gb)])
                        P.dma('sp', f'mst{gb}', fm(mixT[blk * 512:(blk + 1) * 512, t0:t0 + nt]), mst[gb][:, :, 0:nt],
                              r=[('mst', gb)])

                load_u(0)
                for ti, (c0, n) in enumerate(atiles):
                    a2_tile(ti, c0, n)
                P.barrier()
                P.emit()

            with ExitStack() as st:
                S = {}
                for d in range(2):
                    S[d] = dict(
                        C=sb(st, f"C{d}", [128, 8, 512], F32),
                        Cb=sb(st, f"Cb{d}", [128, 8, 512], BF16),
                        n=sb(st, f"n{d}", [128, 8], F32),
                        nB=sb(st, f"nB{d}", [128, 8, 128], BF16),
                        qT=[sb(st, f"qT{d}{i}", [128, 8, 128], BF16) for i in range(2)],
                        kT=[sb(st, f"kT{d}{i}", [128, 8, 128], BF16) for i in range(2)],
                        kt=[sb(st, f"kt{d}{i}", [128, 1024], BF16) for i in range(2)],
                        vt=[sb(st, f"vt{d}{i}", [128, 2048], BF16) for i in range(2)],
                        g=[sb(st, f"g{d}{i}", [128, 16], F32) for i in range(2)],
                        lfB=sb(st, f"lfB{d}", [128, 4, 128], F32),
                        bias=sb(st, f"bias{d}", [128, 4], F32),
                        E=sb(st, f"E{d}", [128, 4, 128], F32),
                        Dm=sb(st, f"Dm{d}", [128, 4, 128], F32),
                        qt=sb(st, f"qt{d}", [128, 8, 128], BF16),
                        wk=sb(st, f"wk{d}", [128, 1024], BF16),
                        sw=sb(st, f"sw{d}", [128, 4, 128], BF16),
                        aden=sb(st, f"aden{d}", [128, 4, 128], F32),
                        rden=sb(st, f"rden{d}", [128, 4, 128], F32),
                        hst=[sb(st, f"hst{d}{i}", [128, 16, 128], F32) for i in range(2)],
                    )
                pmisc = ps(st, "pmisc", [128, 16])
                pBu = ps(st, "pBu", [128, 4, 128])
                pBm = ps(st, "pBm", [128, 4, 128])
                psT = ps(st, "psT", [128, 4, 128])
                pden = ps(st, "pden", [128, 4, 128])
                pnum = [ps(st, f"pnum{i}", [128, 4, 128]) for i in range(2)]
                pdC = ps(st, "pdC", [128, 512])
                cnt = dict(num=0)

                for d in range(2):
                    sd = S[d]
                    P.op('dve', I('memset', ap=sd['C'][:], constant=0.0), w=[('C', d, j) for j in range(8)])
                    P.op('dve', I('memset', ap=sd['Cb'][:], constant=0.0), w=[('Cb', d)])
                    P.op('dve', I('memset', ap=sd['n'][:], constant=0.0), w=[('n', d)])
                    P.op('dve', I('memset', ap=sd['nB'][:], constant=0.0), w=[('nB', d)])

                def scan_load(d, step):
                    c = step if d == 0 else NCH - 1 - step
                    b = step % 2
                    sd = S[d]
                    r0 = c * 128
                    P.dma('sp', f'sq{d}{b}', sd['qT'][b][:].rearrange("p j t -> p (j t)"), qTc[c], w=[('qT', d, b)])
                    P.dma('sp', f'sk{d}{b}', sd['kT'][b][:].rearrange("p j t -> p (j t)"), kTc[c], w=[('kT', d, b)])
                    P.dma('sp', f'st{d}{b}', sd['kt'][b][:], ktok[r0:r0 + 128, :], w=[('kt', d, b)])
                    P.dma('sp', f'sv{d}{b}', sd['vt'][b][:], vtok[r0:r0 + 128, :], w=[('vt', d, b)])
                    P.dma('sp', f'sg{d}{b}', sd['g'][b][:], gts[r0:r0 + 128, :], w=[('g', d, b)])

                def scan_step(d, step):
                    c = step if d == 0 else NCH - 1 - step
                    b = step % 2
                    sd = S[d]
                    go = 0 if d == 0 else 8
                    tl = 127 if d == 0 else 0
                    qT, kT, kt, vt, g = sd['qT'][b], sd['kT'][b], sd['kt'][b], sd['vt'][b], sd['g'][b]
                    kq, kk, kkt, kv, kg = ('qT', d, b), ('kT', d, b), ('kt', d, b), ('vt', d, b), ('g', d, b)
                    ig = g[:, go:go + 4]
                    lf = g[:, go + 4:go + 8]
                    TRI = tri[:, d, :]
                    NEGM = tri[:, 2 + d, :]
                    C, Cb, n, nB = sd['C'], sd['Cb'], sd['n'], sd['nB']
                    lfB, bias, E, Dm, qt, wk, sw, aden, rden = (sd[k] for k in ('lfB', 'bias', 'E', 'Dm', 'qt', 'wk', 'sw', 'aden', 'rden'))
                    hst = sd['hst'][b]
                    P.op('pe', I('matmul', out=pmisc[:, 0:4], lhsT=TRI, rhs=lf, start=True, stop=True),
                         r=[kg, 'tri'], w=['pmisc_cs'])
                    P.op('dve', I('tensor_copy', out=lfB[:], in_=lf.unsqueeze(2).broadcast_to([128, 4, 128])),
                         r=[kg], w=[('lfB', d)])
                    P.op('dve', I('tensor_tensor', out=bias[:], in0=ig, in1=pmisc[:, 0:4], op=ALU.subtract),
                         r=[kg, 'pmisc_cs'], w=[('bias', d)])
                    fu = []
                    fmm = []
                    for h in range(4):
                        fu.append(I('matmul', out=pBu[:, h, :], lhsT=lfB[:, h, :], rhs=TRI, start=True, stop=True))
                        fmm.append(I('matmul', out=pBm[:, h, :], lhsT=lfB[:, h, :], rhs=TRI, start=True, stop=False))
                        fmm.append(I('matmul', out=pBm[:, h, :], lhsT=identf[:], rhs=NEGM, start=False, stop=True))
                    P.op('pe', fu, r=[('lfB', d), 'tri'], w=['pBu'])
                    P.op('pe', fmm, r=[('lfB', d), 'tri', 'identf'], w=['pBm'])
                    P.op('act', I('activation', out=E[:], in_=pBu[:], func=AF.Exp), r=['pBu'], w=[('E', d)])
                    for h in range(4):
                        P.op('act', I('activation', out=Dm[:, h, :], in_=pBm[:, h, :], func=AF.Exp, bias=bias[:, h:h + 1]),
                             r=['pBm', ('bias', d)], w=[('Dm', d)])
                    P.op('dve', I('tensor_tensor', out=qt[:].rearrange("p (h c) t -> p h c t", h=4),
                                  in0=qT[:].rearrange("p (h c) t -> p h c t", h=4),
                                  in1=E[:].unsqueeze(2).broadcast_to([128, 4, 2, 128]), op=ALU.mult),
                         r=[kq, ('E', d)], w=[('qt', d)])
                    P.op('dve', I('tensor_tensor', out=wk[:].rearrange("p (h f) -> p h f", h=4),
                                  in0=kt[:].rearrange("p (h f) -> p h f", h=4),
                                  in1=Dm[:, :, tl:tl + 1].broadcast_to([128, 4, 256]), op=ALU.mult),
                         r=[kkt, ('Dm', d)], w=[('wk', d)])
                    P.op('pe', [I('matmul', out=psT[:, h, :], lhsT=kT[:, h * 2 + cc, :], rhs=qT[:, h * 2 + cc, :],
                                  start=(cc == 0), stop=(cc == 1)) for h in range(4) for cc in range(2)],
                         r=[kk, kq], w=['psT'])
                    P.op('dve', I('tensor_tensor', out=sw[:], in0=psT[:], in1=Dm[:], op=ALU.mult),
                         r=['psT', ('Dm', d)], w=[('sw', d)])
                    fd = []
                    for h in range(4):
                        fd.append(I('matmul', out=pden[:, h, :], lhsT=onesb[:], rhs=sw[:, h, :], start=True, stop=False))
                        for cc in range(2):
                            fd.append(I('matmul', out=pden[:, h, :], lhsT=nB[:, h * 2 + cc, :], rhs=qt[:, h * 2 + cc, :],
                                        start=False, stop=(cc == 1)))
                    P.op('pe', fd, r=[('sw', d), ('qt', d), ('nB', d), 'onesb'], w=['pden'])
                    P.op('act', I('activation', out=aden[:], in_=pden[:], func=AF.Abs), r=['pden'], w=[('aden', d)])
                    P.op('dve', I('tensor_scalar', out=aden[:], in0=aden[:], scalar1=1.0, scalar2=None, op0=ALU.max),
                         r=[('aden', d)], w=[('aden', d)])
                    P.op('dve', I('reciprocal', out=rden[:], in_=aden[:]), r=[('aden', d)], w=[('rden', d)])
                    for h in range(4):
                        pi = cnt['num'] % 2
                        cnt['num'] += 1
                        pn = pnum[pi]
                        pnk = ('pnum', pi)
                        fn = []
                        for et in range(4):
                            fn.append(I('matmul', out=pn[:, et, :], lhsT=vt[:, h * 512 + et * 128:h * 512 + (et + 1) * 128],
                                        rhs=sw[:, h, :], start=True, stop=False))
                            for cc in range(2):
                                fn.append(I('matmul', out=pn[:, et, :], lhsT=Cb[:, h * 2 + cc, et * 128:(et + 1) * 128],
                                            rhs=qt[:, h * 2 + cc, :], start=False, stop=(cc == 1)))
                        P.op('pe', fn, r=[kv, ('sw', d), ('Cb', d), ('qt', d)], w=[pnk])
                        P.op('dve', I('tensor_tensor', out=hst[:, h * 4:(h + 1) * 4, :], in0=pn[:],
                                      in1=rden[:, h:h + 1, :].broadcast_to([128, 4, 128]), op=ALU.mult),
                             r=[pnk, ('rden', d)], w=[('hst', d, b)])
                    hdst = hfT if d == 0 else hbT
                    P.dma('sp', f'hst{d}{b}', fm(hdst[:, c * 128:(c + 1) * 128]), hst[:], r=[('hst', d, b)])
                    P.op('pe', [I('matmul', out=pmisc[:, 4 + j:5 + j], lhsT=wk[:, j * 128:(j + 1) * 128], rhs=onesb[:, 0:1],
                                  start=True, stop=True) for j in range(8)], r=[('wk', d), 'onesb'], w=['pmisc_dn'])
                    for h in range(4):
                        P.op('dve', I('scalar_tensor_tensor', out=n[:, h * 2:h * 2 + 2], in0=n[:, h * 2:h * 2 + 2],
                                      scalar=E[:, h, tl:tl + 1], in1=pmisc[:, 4 + h * 2:6 + h * 2], op0=ALU.mult, op1=ALU.add),
                             r=[('n', d), ('E', d), 'pmisc_dn'], w=[('n', d)])
                    for h in range(4):
                        for cc in range(2):
                            j = h * 2 + cc
                            P.op('pe', I('matmul', out=pdC[:, :], lhsT=wk[:, j * 128:(j + 1) * 128],
                                         rhs=vt[:, h * 512:(h + 1) * 512], start=True, stop=True),
                                 r=[('wk', d), kv], w=['pdC'])
                            P.op('dve', I('scalar_tensor_tensor', out=C[:, j, :], in0=C[:, j, :], scalar=E[:, h, tl:tl + 1],
                                          in1=pdC[:, :], op0=ALU.mult, op1=ALU.add),
                                 r=[('C', d, j), ('E', d), 'pdC'], w=[('C', d, j)])
                            P.op('act', I('copy', out=Cb[:, j, :], in_=C[:, j, :]), r=[('C', d, j)], w=[('Cb', d)])
                    P.op('dve', I('tensor_copy', out=nB[:], in_=n[:].unsqueeze(2).broadcast_to([128, 8, 128])),
                         r=[('n', d)], w=[('nB', d)])

                for d in range(2):
                    scan_load(d, 0)
                for step in range(NCH):
                    for d in range(2):
                        if step + 1 < NCH:
                            scan_load(d, step + 1)
                        scan_step(d, step)
                P.barrier()
                P.emit()

            with ExitStack() as st:
                hgn = sb(st, "hgn", [128, 16], F32)
                hf = sb(st, "hf", [128, 4, 512], F32)
                hb = sb(st, "hb", [128, 4, 512], F32)
                hs = sb(st, "hs", [128, 4, 512], F32)
                hsq = [sb(st, f"hsq{i}", [128, 512], F32) for i in range(2)]
                sdv = sb(st, "sdv", [128, 512], F32)
                rs = sb(st, "rs", [128, 512], F32)
                t1 = [sb(st, f"t1{i}", [128, 512], F32) for i in range(2)]
                og = [sb(st, f"og{i}", [128, 4, 512], BF16) for i in range(2)]
                hg = sb(st, "hg", [128, 16, 512], BF16)
                gBt = [sb(st, f"gBt{i}", [128, 4, 512], BF16) for i in range(2)]
                mxa = [sb(st, f"mxa{i}", [128, 4, 512], F32) for i in range(2)]
                mx = sb(st, "mx", [128, 16, 512], BF16)
                xres = [sb(st, f"xres{i}", [128, 512], F32) for i in range(4)]
                xm = [sb(st, f"xm{i}", [128, 512], F32) for i in range(4)]
                slots = [sb(st, f"ring{i}", [128, KC, 512], BF16) for i in range(3)]
                pms = ps(st, "pms", [128, 512])
                mm = [ps(st, f"mm{i}", [128, 512], F32) for i in range(6)]
                ring = Ring(P, slots)
                cnt = dict(mm=0, o=0, g=0, x=0)

                def next_mm():
                    i = cnt['mm'] % len(mm)
                    cnt['mm'] += 1
                    return mm[i], ('mm', i)

                P.dma('sp', 'hgn', hgn[:], p_hg[l], w=['hgn'])

                def c1_tile(ti, c0, n):
                    nt = n * 128
                    t0 = c0 * 128
                    for h in range(4):
                        ob = cnt['o'] % 2
                        cnt['o'] += 1
                        rows = slice(h * 512, (h + 1) * 512)
                        P.dma('sp', 'hf', hf[:, :, 0:nt], fm(hfT[rows, t0:t0 + nt]), w=['hf'])
                        P.dma('sp', 'hb', hb[:, :, 0:nt], fm(hbT[rows, t0:t0 + nt]), w=['hb'])
                        P.dma('sp', f'og{ob}', og[ob][:, :, 0:nt], fm(ogT[rows, t0:t0 + nt]), w=[('og', ob)])
                        P.op('dve', I('tensor_tensor', out=hs[:, :, 0:nt], in0=hf[:, :, 0:nt], in1=hb[:, :, 0:nt], op=ALU.add),
                             r=['hf', 'hb'], w=['hs'])
                        for et in range(4):
                            qb = et % 2
                            P.op('act', I('activation', out=hsq[qb][:, 0:nt], in_=hs[:, et, 0:nt], func=AF.Square),
                                 r=['hs'], w=[('hsq', qb)])
                            P.op('pe', I('matmul', out=pms[:, 0:nt], lhsT=ones512[:], rhs=hsq[qb][:, 0:nt],
                                         start=(et == 0), stop=(et == 3)), r=[('hsq', qb), 'ones512'], w=['pms'])
                        P.op('act', I('activation', out=sdv[:, 0:nt], in_=pms[:, 0:nt], func=AF.Sqrt, bias=epsc[:]),
                             r=['pms', 'epsc'], w=['sdv'])
                        P.op('dve', I('reciprocal', out=rs[:, 0:nt], in_=sdv[:, 0:nt]), r=['sdv'], w=['rs'])
                        for et in range(4):
                            tb = et % 2
                            P.op('dve', I('tensor_tensor', out=t1[tb][:, 0:nt], in0=hs[:, et, 0:nt], in1=rs[:, 0:nt], op=ALU.mult),
                                 r=['hs', 'rs'], w=[('t1', tb)])
                            P.op('dve', I('scalar_tensor_tensor', out=hg[:, h * 4 + et, 0:nt], in0=t1[tb][:, 0:nt],
                                          scalar=hgn[:, h * 4 + et:h * 4 + et + 1], in1=og[ob][:, et, 0:nt],
                                          op0=ALU.mult, op1=ALU.mult),
                                 r=[('t1', tb), 'hgn', ('og', ob)], w=['hg'])
                    for blk in range(4):
                        gb = cnt['g'] % 2
                        cnt['g'] += 1
                        P.dma('sp', f'gBt{gb}', gBt[gb][:, :, 0:nt], fm(gmT[D + blk * 512:D + (blk + 1) * 512, t0:t0 + nt]),
                              w=[('gBt', gb)])
                        P.dma('sp', f'mxa{gb}', mxa[gb][:, :, 0:nt], fm(mixT[blk * 512:(blk + 1) * 512, t0:t0 + nt]),
                              w=[('mxa', gb)])
                        slot, skey = ring.load([(w_mlstm_out[l][:, blk * 512:(blk + 1) * 512], KC, 0, 512)])
                        for ct in range(4):
                            pt, pk = next_mm()
                            P.op('pe', [I('matmul', out=pt[:, 0:nt], lhsT=slot[:, kc, ct * 128:(ct + 1) * 128], rhs=hg[:, kc, 0:nt],
                                          start=(kc == 0), stop=(kc == KC - 1)) for kc in range(KC)],
                                 r=[skey, 'hg'], w=[pk])
                            tb = ct % 2
                            P.op('dve', I('tensor_tensor', out=t1[tb][:, 0:nt], in0=pt[:, 0:nt], in1=gBt[gb][:, ct, 0:nt],
                                          op=ALU.mult), r=[pk, ('gBt', gb)], w=[('t1', tb)])
                            P.op('dve', I('tensor_tensor', out=mx[:, blk * 4 + ct, 0:nt], in0=t1[tb][:, 0:nt],
                                          in1=mxa[gb][:, ct, 0:nt], op=ALU.add),
                                 r=[('t1', tb), ('mxa', gb)], w=['mx'])
                    for blk in range(4):
                        slot, skey = ring.load([(w_out[l][:, blk * 512:(blk + 1) * 512], KC, 0, 512)])
                        for ci in range(n):
                            xb = cnt['x'] % 4
                            cnt['x'] += 1
                            r0 = t0 + ci * 128
                            P.dma('sp', f'xres{xb}', xres[xb][:], xsrc[r0:r0 + 128, blk * 512:(blk + 1) * 512], w=[('xres', xb)])
                            pt, pk = next_mm()
                            P.op('pe', [I('matmul', out=pt[:, :], lhsT=mx[:, kc, ci * 128:(ci + 1) * 128], rhs=slot[:, kc, :],
                                          start=(kc == 0), stop=(kc == KC - 1)) for kc in range(KC)],
                                 r=[skey, 'mx'], w=[pk])
                            P.op('dve', I('scalar_tensor_tensor', out=xm[xb][:], in0=pt[:, :], scalar=mcol[:, c0 + ci:c0 + ci + 1],
                                          in1=xres[xb][:], op0=ALU.mult, op1=ALU.add),
                                 r=[pk, ('xres', xb), 'mcol'], w=[('xm', xb)])
                            P.dma('sp', f'xm{xb}', xmid[1 + r0:1 + r0 + 128, blk * 512:(blk + 1) * 512], xm[xb][:],
                                  r=[('xm', xb)])

                for ti, (c0, n) in enumerate(atiles):
                    c1_tile(ti, c0, n)
                P.barrier()
                P.emit()

            with ExitStack() as st:
                gF = sb(st, "gF", [128, D], F32)
                gL = sb(st, "gL", [128, D], F32) if last else None
                fw = sb(st, "fw", [128, NF, 4], F32)
                xl = sb(st, "xl", [128, D], F32)
                xnb = sb(st, "xnb2", [128, D], BF16)
                sqj = sb(st, "sqj2", [128, D], BF16)
                stat = sb(st, "stat2", [128, 4], F32)
                xnT = sb(st, "xn2T", [128, KC, 512], BF16)
                gt = [sb(st, f"gt{i}", [128, 512], F32) for i in range(2)]
                cvt = [sb(st, f"cvt{i}", [128, 512], F32) for i in range(2)]
                hid = sb(st, "hid", [128, NF, 512], BF16)
                xrow = [sb(st, f"xrow{i}", [128, D], F32) for i in range(4)]
                slots = [sb(st, f"ring{i}", [128, KC, 512], BF16) for i in range(3)]
                tp = ps(st, "tp", [128, 1024], BF16)
                gbk = [ps(st, f"gbk{i}", [128, 512]) for i in range(7)]
                pg = [(gbk[0], ('gbk', 0)), (gbk[1], ('gbk', 1))]
                pv = [(gbk[2], ('gbk', 2)), (gbk[3], ('gbk', 3))]
                pd = [(gbk[4], ('gbk', 4)), (gbk[5], ('gbk', 5)), (gbk[6], ('gbk', 6)), (gbk[3], ('gbk', 3))]
                ring = Ring(P, slots)
                cnt = dict(pg=0)

                P.dma('sp', 'gF', gF[:], norm_ffn_g[l].partition_broadcast(128), w=['gvec'])
                if last:
                    P.dma('sp', 'gL', gL[:], norm_final_g.partition_broadcast(128), w=['gL'])
                P.dma('sp', 'fw', fw[:], p_fw[l], w=['fw'])

                ftiles = []
                s_ = 0
                while s_ < NT:
                    nv_ = min(510, NT - s_)
                    ftiles.append((s_, nv_))
                    s_ += nv_

                def c2_tile(s, nv):
                    ncol = nv + 2
                    nblk = (ncol + 127) // 128
                    for bi in range(nblk):
                        nr = min(128, ncol - bi * 128)
                        P.dma('sp', 'xl', xl[0:nr, :], xmid[s + bi * 128:s + bi * 128 + nr, :], w=['xl'])
                        rmsnorm_rows(xl, nr, 'xl', gF, 'gvec', xnb, 'xnb', stat, sqj)
                        for g in range(2):
                            P.op('pe', [I('transpose', out=tp[:, k * 128:k * 128 + nr],
                                          in_=xnb[0:nr, (g * 8 + k) * 128:(g * 8 + k + 1) * 128], identity=identb[0:nr, 0:nr])
                                        for k in range(8)], r=['xnb', 'identb'], w=['tp'])
                            dst = xnT[:, g * 8:(g + 1) * 8, bi * 128:bi * 128 + nr]
                            src = tp[:].rearrange("p (k t) -> p k t", k=8)[:, :, 0:nr]
                            if g == 0:
                                P.op('act', I('copy', out=dst, in_=src), r=['tp'], w=['xnT'])
                            else:
                                P.op('dve', I('tensor_copy', out=dst, in_=src), r=['tp'], w=['xnT'])
                    for j0 in range(0, NF, 2):
                        nj = min(2, NF - j0)
                        slot, skey = ring.load([(w_up[l][:, j0 * 128:(j0 + nj) * 128], KC, 0, nj * 128),
                                                (w_up[l][:, DFF + j0 * 128:DFF + (j0 + nj) * 128], KC, 256, nj * 128)])
                        for jj in range(nj):
                            j = j0 + jj
                            pb = cnt['pg'] % 2
                            cnt['pg'] += 1
                            pgt, pgk = pg[pb]
                            pvt, pvk = pv[pb]
                            P.op('pe', [I('matmul', out=pgt[:, 0:ncol], lhsT=slot[:, kc, jj * 128:(jj + 1) * 128], rhs=xnT[:, kc, 0:ncol],
                                          start=(kc == 0), stop=(kc == KC - 1)) for kc in range(KC)],
                                 r=[skey, 'xnT'], w=[pgk])
                            P.op('pe', [I('matmul', out=pvt[:, 0:ncol], lhsT=slot[:, kc, 256 + jj * 128:256 + (jj + 1) * 128],
                                          rhs=xnT[:, kc, 0:ncol], start=(kc == 0), stop=(kc == KC - 1)) for kc in range(KC)],
                                 r=[skey, 'xnT'], w=[pvk])
                            P.op('act', I('copy', out=gt[pb][:, 0:ncol], in_=pgt[:, 0:ncol]), r=[pgk], w=[('gt', pb)])
                            P.op('dve', I('tensor_scalar', out=cvt[pb][:, 0:nv], in0=gt[pb][:, 0:nv], scalar1=fw[:, j, 0:1],
                                          scalar2=fw[:, j, 3:4], op0=ALU.mult, op1=ALU.add),
                                 r=[('gt', pb), 'fw'], w=[('cvt', pb)])
                            for wt in (1, 2):
                                P.op('dve', I('scalar_tensor_tensor', out=cvt[pb][:, 0:nv], in0=gt[pb][:, wt:wt + nv],
                                              scalar=fw[:, j, wt:wt + 1], in1=cvt[pb][:, 0:nv], op0=ALU.mult, op1=ALU.add),
                                     r=[('gt', pb), 'fw', ('cvt', pb)], w=[('cvt', pb)])
                            P.op('act', I('activation', out=cvt[pb][:, 0:nv], in_=cvt[pb][:, 0:nv], func=AF.Gelu),
                                 r=[('cvt', pb)], w=[('cvt', pb)])
                            P.op('dve', I('tensor_tensor', out=hid[:, j, 0:nv], in0=cvt[pb][:, 0:nv], in1=pvt[:, 1:1 + nv],
                                          op=ALU.mult), r=[('cvt', pb), pvk], w=['hid'])
                    ntb = (nv + 127) // 128
                    kparts = [(0, 15), (15, 15), (30, 13)]
                    nrs = [min(128, nv - tbi * 128) for tbi in range(ntb)]
                    for tbi in range(ntb):
                        tok0 = s + tbi * 128
                        P.dma('sp', f'xrow{tbi}', xrow[tbi][0:nrs[tbi], :], xmid[1 + tok0:1 + tok0 + nrs[tbi], :], w=[('xrow', tbi)])
                    for blk in range(4):
                        for kp, (k0, nk) in enumerate(kparts):
                            slot, skey = ring.load([(w_down[l][k0 * 128:(k0 + nk) * 128, blk * 512:(blk + 1) * 512], nk, 0, 512)])
                            for tbi in range(ntb):
                                nr = nrs[tbi]
                                pdt, pdk = pd[tbi]
                                P.op('pe', [I('matmul', out=pdt[0:nr, :], lhsT=hid[:, k0 + kk, tbi * 128:tbi * 128 + nr],
                                              rhs=slot[:, kk, :], start=(kp == 0 and kk == 0), stop=(kp == 2 and kk == nk - 1))
                                            for kk in range(nk)], r=[skey, 'hid'], w=[pdk])
                        for tbi in range(ntb):
                            nr = nrs[tbi]
                            pdt, pdk = pd[tbi]
                            P.op('dve', I('tensor_tensor', out=xrow[tbi][0:nr, blk * 512:(blk + 1) * 512], in0=pdt[0:nr, :],
                                          in1=xrow[tbi][0:nr, blk * 512:(blk + 1) * 512], op=ALU.add),
                                 r=[pdk, ('xrow', tbi)], w=[('xrow', tbi)])
                    for tbi in range(ntb):
                        nr = nrs[tbi]
                        tok0 = s + tbi * 128
                        if last:
                            rmsnorm_rows(xrow[tbi], nr, ('xrow', tbi), gL, 'gL', xrow[tbi], ('xrow', tbi), stat, sqj)
                        P.dma('sp', f'xrow{tbi}', xdst[tok0:tok0 + nr, :], xrow[tbi][0:nr, :], r=[('xrow', tbi)])

                for (s, nv) in ftiles:
                    c2_tile(s, nv)
                P.barrier()
                P.emit()
        print("bass ops recorded:", P.nops)
    return nc


def _host_layout(inputs, depth=DEPTH):
    f = lambda a: np.ascontiguousarray(np.asarray(a, dtype=np.float32))
    cw = f(inputs['conv_dw_w'])
    p_cw = np.ascontiguousarray(cw.reshape(depth, CW, 8, 128).transpose(0, 3, 2, 1))
    cvs = np.stack([f(inputs['conv_dw_b']), f(inputs['conv_ln_g']), f(inputs['conv_ln_b'])], axis=1)
    p_cv = np.ascontiguousarray(cvs.reshape(depth, 3, 8, 128).transpose(0, 3, 1, 2))
    p_hg = np.ascontiguousarray(f(inputs['mlstm_head_g']).reshape(depth, 16, 128).transpose(0, 2, 1))
    fwb = np.concatenate([f(inputs['ffn_dw_w']), f(inputs['ffn_dw_b'])[:, None, :]], axis=1)
    p_fw = np.ascontiguousarray(fwb.reshape(depth, 4, NF, 128).transpose(0, 3, 2, 1))
    k = np.arange(128)
    triU = (k[:, None] <= k[None, :]).astype(np.float32)
    triL = (k[:, None] >= k[None, :]).astype(np.float32)
    c_tri = np.stack([triU, triL, (1 - triU) * NEG, (1 - triL) * NEG]).astype(np.float32)
    common = dict(
        w_in=f(inputs['w_in']), w_conv_out=f(inputs['w_conv_out']), w_mlstm_out=f(inputs['w_mlstm_out']),
        w_out=f(inputs['w_out']), w_up=f(inputs['w_up']), w_down=f(inputs['w_down']),
        norm_mix_g=f(inputs['norm_mix_g']), norm_ffn_g=f(inputs['norm_ffn_g']), norm_final_g=f(inputs['norm_final_g']),
        b_gates=f(inputs['b_gates']), p_cw=p_cw, p_cv=p_cv, p_hg=p_hg, p_fw=p_fw, c_tri=c_tri)
    return common


def run_segments(segs, NCH, inputs, depth=DEPTH):
    NT = NCH * 128
    common = _host_layout(inputs, depth)
    in_maps = []
    for sgm in segs:
        x = np.zeros((NT, D), np.float32)
        m = np.zeros((NT,), np.float32)
        if sgm is not None:
            x[:sgm.shape[0]] = sgm
            m[:sgm.shape[0]] = 1.0
        dct = dict(common)
        dct['x'] = x
        dct['msk'] = np.ascontiguousarray(m.reshape(NCH, 128).T)
        in_maps.append(dct)
    nc = build_program(NCH, depth)
    res = run_bass_kernel_spmd(nc, in_maps, core_ids=list(range(8)))
    return [r["y"] for r in res.results]


def kernel(x_prompt, x_sample, **params):
    x_prompt = np.asarray(x_prompt, dtype=np.float32)
    x_sample = np.asarray(x_sample, dtype=np.float32)
    segs = [x_prompt[0], x_prompt[1], x_sample[0], x_sample[1], None, None, None, None]
    NCH = x_sample.shape[1] // 128
    outs = run_segments(segs, NCH, params)
    y_prompt = np.stack([outs[0][:x_prompt.shape[1]], outs[1][:x_prompt.shape[1]]]).astype(np.float32)
    y_sample = np.stack([outs[2], outs[3]]).astype(np.float32)
    return (y_prompt, y_sample)
```

```python
import numpy as np
from contextlib import ExitStack

import concourse.bass as bass
import concourse.mybir as mybir
from concourse.bass_utils import run_bass_kernel_spmd

F32 = mybir.dt.float32
BF16 = mybir.dt.bfloat16
AF = mybir.ActivationFunctionType
ALU = mybir.AluOpType

D = 2048
KC = 16
CONV = 1024
CW = 31
HEADS = 4
DK = 256
DV = 512
DFF = 5504
NF = 43
DIN = 12304
DEPTH = 2
EPS = 1e-6
C_A, C_B, C_Q, C_K, C_V, C_O, C_G, C_M = 0, 1024, 2048, 3072, 4096, 6144, 8192, 8208
NEG = -30000.0

ENG = ['pe', 'act', 'dve', 'pool', 'sp']
BLK = {'pe': 'tensor', 'act': 'scalar', 'dve': 'vector', 'pool': 'gpsimd', 'sp': 'sync'}


class Prog:
    def __init__(self, nc, es):
        self.nc = nc
        self.es = es
        self.sem = {e: es.enter_context(nc.semaphore('s_' + e)) for e in ENG}
        self.cnt = {e: 0 for e in ENG}
        self.dpool = []
        self.dmap = {}
        self.ops = {e: [] for e in ENG}
        self.waited = {e: {} for e in ENG}
        self.res = {}
        self.xkeys = set()
        self.nops = 0

    def semh(self, k):
        if k in self.sem:
            return self.sem[k]
        return self.dpool[int(k[2:])][0]

    def _dsem(self, slot):
        if slot not in self.dmap:
            idx = len(self.dmap)
            if idx >= len(self.dpool):
                self.dpool.append([self.es.enter_context(self.nc.semaphore(f'dq{idx}')), 0])
            self.dmap[slot] = idx
        return self.dmap[slot]

    def _deps(self, r, w):
        evs = {}

        def add(k, v):
            if evs.get(k, 0) < v:
                evs[k] = v

        for key in r:
            st = self.res.get(key)
            if st:
                for k, v in st[0].items():
                    add(k, v)
        for key in w:
            st = self.res.get(key)
            if st:
                for k, v in st[0].items():
                    add(k, v)
                for k, v in st[1].items():
                    add(k, v)
        return evs

    def _commit(self, ev, r, w):
        k, v = ev
        for key in r:
            st = self.res.setdefault(key, [{}, {}])
            if st[1].get(k, 0) < v:
                st[1][k] = v
        for key in w:
            st = self.res.setdefault(key, [{}, {}])
            if st[0].get(k, 0) < v:
                st[0][k] = v

    def _waits(self, e, evs, skip_self_pe=True):
        waits = []
        for k, v in evs.items():
            if skip_self_pe and e == 'pe' and k == 'pe':
                continue
            if self.waited[e].get(k, 0) >= v:
                continue
            self.waited[e][k] = v
            waits.append((k, v))
        return waits

    def op(self, e, fns, r=(), w=()):
        if isinstance(fns, tuple):
            fns = [fns]
        xr = [k for k in r if k in self.xkeys]
        if xr:
            w = list(w) + xr
        waits = self._waits(e, self._deps(r, w))
        self.cnt[e] += 1
        ev = (e, self.cnt[e])
        self.ops[e].append((waits, fns, (e, 1)))
        self._commit(ev, r, w)
        self.nops += len(fns)

    def dma(self, e, slot, out, in_, r=(), w=()):
        waits = self._waits(e, self._deps(r, w), skip_self_pe=False)
        idx = self._dsem(slot)
        k = f'd:{idx}'
        self.dpool[idx][1] += 16
        ev = (k, self.dpool[idx][1])
        self.ops[e].append((waits, [('dma_start', dict(out=out, in_=in_))], (k, 16)))
        self._commit(ev, r, w)
        self.nops += 1

    def coll(self, src, dst):
        e = 'pool'
        idx = self._dsem('cc')
        k = f'd:{idx}'
        self.dpool[idx][1] += 1
        self.ops[e].append(([], [('collective_compute', dict(kind="AllGather", op=ALU.bypass,
                                                               replica_groups=[list(range(8))], ins=[src], outs=[dst]))],
                            (k, 1)))
        self.nops += 1

    def barrier(self):
        targets = {e: self.cnt[e] for e in ENG}
        for i, (h, c) in enumerate(self.dpool):
            targets[f'd:{i}'] = c
        for e in ENG:
            waits = []
            for k, v in targets.items():
                if v > 0 and self.waited[e].get(k, 0) < v:
                    self.waited[e][k] = v
                    waits.append((k, v))
            if waits:
                self.ops[e].append((waits, [], None))
        self.res = {}

    def emit(self):
        nc = self.nc
        with nc.Block() as block:
            for e in ENG:
                ops = self.ops[e]
                if not ops:
                    continue

                def body(eng, ops=ops):
                    for waits, fns, inc in ops:
                        for k, v in waits:
                            eng.wait_ge(self.semh(k), v)
                        ins = None
                        for (nm, kw) in fns:
                            ins = getattr(eng, nm)(**kw)
                        if inc is not None and ins is not None:
                            ins.then_inc(self.semh(inc[0]), inc[1])

                getattr(block, BLK[e])(body)
                self.ops[e] = []
        self.dmap = {}


def I(name, **kw):
    return (name, kw)


class Ring:
    def __init__(self, P, slots, name='w'):
        self.P = P
        self.slots = slots
        self.i = 0
        self.name = name

    def load(self, parts):
        s = self.i % len(self.slots)
        self.i += 1
        slot = self.slots[s]
        key = (self.name, s)
        for (src, nk, c0, ncols) in parts:
            self.P.dma('pool', f'{self.name}{s}', slot[:, 0:nk, c0:c0 + ncols],
                       src.rearrange("(kc p) c -> p kc c", p=128), w=[key])
        return slot, key


def tiles_of(nch, per=4):
    out = []
    c = 0
    while c < nch:
        n = min(per, nch - c)
        out.append((c, n))
        c += n
    return out


def build_program(NCH, depth=DEPTH):
    NT = NCH * 128
    nc = bass.Bass("TRN2", target_bir_lowering=False)

    def din(name, shape, dt=F32):
        return nc.dram_tensor(name, list(shape), dt, kind="ExternalInput").ap()

    def dscr(name, shape, dt=F32):
        return nc.dram_tensor(name, list(shape), dt).ap()

    x_in = din("x", [NT, D])
    msk = din("msk", [128, NCH])
    w_in = din("w_in", [depth, D, DIN])
    w_conv_out = din("w_conv_out", [depth, CONV, D])
    w_mlstm_out = din("w_mlstm_out", [depth, D, D])
    w_out = din("w_out", [depth, D, D])
    w_up = din("w_up", [depth, D, 2 * DFF])
    w_down = din("w_down", [depth, DFF, D])
    norm_mix_g = din("norm_mix_g", [depth, D])
    norm_ffn_g = din("norm_ffn_g", [depth, D])
    norm_final_g = din("norm_final_g", [D])
    b_gates = din("b_gates", [depth, 16])
    p_cw = din("p_cw", [depth, 128, 8, CW])
    p_cv = din("p_cv", [depth, 128, 3, 8])
    p_hg = din("p_hg", [depth, 128, 16])
    p_fw = din("p_fw", [depth, 128, NF, 4])
    c_tri = din("c_tri", [4, 128, 128])
    sel_in = din("sel", [128, 4, 8])
    y_out = nc.dram_tensor("y", [NT, D], F32, kind="ExternalOutput").ap()

    uT = dscr("uT", [CONV, NT + 32])
    qTc = dscr("qTc", [NCH, 128, 1024], BF16)
    kTc = dscr("kTc", [NCH, 128, 1024], BF16)
    ktok = dscr("ktok", [NT, 1024], BF16)
    vtok = dscr("vtok", [NT, 2048], BF16)
    gts = dscr("gts", [NT, 16])
    ogT = dscr("ogT", [D, NT], BF16)
    gmT = dscr("gmT", [2 * D, NT], BF16)
    mixT = dscr("mixT", [D, NT])
    hfT = dscr("hfT", [D, NT])
    hbT = dscr("hbT", [D, NT])
    xmid = dscr("xmid", [NT + 2, D])
    SW = 4096 + 8 + 4
    pkU_t = nc.dram_tensor("pkU", [CONV, 32], F32)
    gU_t = nc.dram_tensor("gU", [8 * CONV, 32], F32)
    pkS_t = nc.dram_tensor("pkS", [128, 2 * SW], F32)
    gS_t = nc.dram_tensor("gS", [8 * 128, 2 * SW], F32)
    pkX_t = nc.dram_tensor("pkX", [2, D], F32)
    gX_t = nc.dram_tensor("gX", [16, D], F32)
    pkU, gU, pkS, gS, pkX, gX = (t.ap() for t in (pkU_t, gU_t, pkS_t, gS_t, pkX_t, gX_t))
    x1 = dscr("x1", [NT, D])

    wb_in = dscr("wb_in", [depth, D, DIN], BF16)
    wb_conv_out = dscr("wb_conv_out", [depth, CONV, D], BF16)
    wb_mlstm_out = dscr("wb_mlstm_out", [depth, D, D], BF16)
    wb_out = dscr("wb_out", [depth, D, D], BF16)
    wb_up = dscr("wb_up", [depth, D, 2 * DFF], BF16)
    wb_down = dscr("wb_down", [depth, DFF, D], BF16)

    def fm(ap):
        return ap.rearrange("(j p) t -> p j t", p=128)

    es = ExitStack()
    with es:
        P = Prog(nc, es)
        P.xkeys.update(['gps', 'pmean', 'psq', 'pmisc', 'pBu', 'pBm', 'psT', 'pden', 'pdC', 'pms', 'tp'])
        P.xkeys.update([('mm', i) for i in range(8)] + [('tp', i) for i in range(4)] + [('pnum', i) for i in range(4)]
                       + [('gbk', i) for i in range(8)])

        uid = [0]

        def sb(st, name, shape, dt):
            uid[0] += 1
            return st.enter_context(nc.sbuf_tensor(f"{name}_{uid[0]}", list(shape), dt))

        def ps(st, name, shape, dt=F32):
            uid[0] += 1
            return st.enter_context(nc.psum_tensor(f"{name}_{uid[0]}", list(shape), dt))

        identf = sb(es, "identf", [128, 128], F32)
        identb = sb(es, "identb", [128, 128], BF16)
        ones1k = sb(es, "ones1k", [128, 128], F32)
        ones512 = sb(es, "ones512", [128, 128], F32)
        onesb = sb(es, "onesb", [128, 128], BF16)
        tri = sb(es, "tri", [128, 4, 128], F32)
        zer = sb(es, "zer", [128, 2048], F32)
        mcol = sb(es, "mcol", [128, NCH], F32)
        epsc = sb(es, "epsc", [128, 1], F32)
        sel = sb(es, "sel", [128, 4, 8], F32)

        P.op('pool', I('memset', ap=identf[:], constant=1.0), w=['identf'])
        P.op('pool', I('affine_select', out=identf[:], in_=identf[:], pattern=[[-1, 128]],
                       compare_op=ALU.is_equal, fill=0.0, base=0, channel_multiplier=1),
             r=['identf'], w=['identf'])
        P.op('dve', I('tensor_copy', out=identb[:], in_=identf[:]), r=['identf'], w=['identb'])
        P.op('dve', I('memset', ap=ones1k[:], constant=1.0 / 1024), w=['ones1k'])
        P.op('dve', I('memset', ap=ones512[:], constant=1.0 / 512), w=['ones512'])
        P.op('dve', I('memset', ap=onesb[:], constant=1.0), w=['onesb'])
        P.op('dve', I('memset', ap=zer[:], constant=0.0), w=['zer'])
        P.op('dve', I('memset', ap=epsc[:], constant=EPS), w=['epsc'])
        P.dma('sp', 'c0', tri[:], c_tri.rearrange("a p t -> p a t"), w=['tri'])
        P.dma('sp', 'c1', mcol[:], msk, w=['mcol'])
        P.dma('sp', 'c2', sel[:], sel_in, w=['sel'])
        P.dma('sp', 'z0', fm(uT[:, 0:16]), zer[:, 0:128].rearrange("p (j t) -> p j t", j=8), r=['zer'])
        P.dma('sp', 'z1', fm(uT[:, NT + 16:NT + 32]), zer[:, 0:128].rearrange("p (j t) -> p j t", j=8), r=['zer'])
        P.dma('sp', 'z2', xmid[0:1, :], zer[0:1, :], r=['zer'])
        P.dma('sp', 'z3', xmid[NT + 1:NT + 2, :], zer[0:1, :], r=['zer'])
        cvi = 0
        for l_ in range(depth):
            for (src, dst, nrows, rb) in ((w_in, wb_in, D, 64), (w_conv_out, wb_conv_out, CONV, 256),
                                          (w_mlstm_out, wb_mlstm_out, D, 256), (w_out, wb_out, D, 256),
                                          (w_up, wb_up, D, 64), (w_down, wb_down, DFF, 256)):
                r_ = 0
                while r_ < nrows:
                    n_ = min(rb, nrows - r_)
                    P.dma('pool', f'cv{cvi % 6}', dst[l_][r_:r_ + n_, :], src[l_][r_:r_ + n_, :], w=[('cv', cvi % 6)])
                    cvi += 1
                    r_ += n_
        P.barrier()
        P.emit()

        atiles = tiles_of(NCH, 4)

        def rmsnorm_rows(xt, nr, xk, gvec, gk, outt, outk, stat, sqj):
            P.op('dve', I('memset', ap=stat[:, 0:1], constant=0.0), w=['stat0'])
            P.op('act', I('activation', out=sqj[0:nr, :], in_=xt[0:nr, :], func=AF.Square, accum_out=stat[0:nr, 0:1]),
                 r=[xk, 'stat0'], w=['sqj', 'stat0'])
            P.op('act', I('activation', out=stat[0:nr, 1:2], in_=stat[0:nr, 0:1], func=AF.Sqrt, scale=1.0 / D,
                          bias=epsc[0:nr, :]), r=['stat0', 'epsc'], w=['stat1'])
            P.op('dve', I('reciprocal', out=stat[0:nr, 2:3], in_=stat[0:nr, 1:2]), r=['stat1'], w=['stat2'])
            P.op('dve', I('scalar_tensor_tensor', out=outt[0:nr, :], in0=xt[0:nr, :], scalar=stat[0:nr, 2:3],
                          in1=gvec[0:nr, :], op0=ALU.mult, op1=ALU.mult),
                 r=[xk, 'stat2', gk], w=[outk])

        def masked_sum(dst, dkey, srcs, k):
            for c in range(8):
                ap, key = srcs[c]
                if c == 0:
                    P.op('dve', I('tensor_scalar', out=dst, in0=ap, scalar1=sel[:, k, 0:1], scalar2=None, op0=ALU.mult),
                         r=[key, 'sel'], w=[dkey])
                else:
                    P.op('dve', I('scalar_tensor_tensor', out=dst, in0=ap, scalar=sel[:, k, c:c + 1], in1=dst,
                                  op0=ALU.mult, op1=ALU.add), r=[key, 'sel', dkey], w=[dkey])

        def exchange_u():
            with ExitStack() as st:
                G = sb(st, "xG", [128, 8, 8, 32], F32)
                hl = sb(st, "xhl", [128, 8, 16], F32)
                hr = sb(st, "xhr", [128, 8, 16], F32)
                P.dma('sp', 'x0', pkU[:, 0:16], uT[:, 16:32])
                P.dma('sp', 'x1', pkU[:, 16:32], uT[:, NT:NT + 16])
                P.barrier()
                P.coll(pkU_t.ap().opt(), gU_t.ap().opt())
                P.barrier()
                for c in range(8):
                    P.dma('sp', f'xg{c}', G[:, c], fm(gU[c * CONV:(c + 1) * CONV, :]), w=[('G', c)])
                masked_sum(hl[:], 'hl', [(G[:, c, :, 16:32], ('G', c)) for c in range(8)], 0)
                masked_sum(hr[:], 'hr', [(G[:, c, :, 0:16], ('G', c)) for c in range(8)], 1)
                P.dma('sp', 'x2', fm(uT[:, 0:16]), hl[:], r=['hl'])
                P.dma('sp', 'x3', fm(uT[:, NT + 16:NT + 32]), hr[:], r=['hr'])
                P.barrier()
                P.emit()

        def exchange_x():
            with ExitStack() as st:
                G = sb(st, "xGX", [128, 8, 2, 16], F32)
                hl = sb(st, "xxl", [128, 16], F32)
                hr = sb(st, "xxr", [128, 16], F32)
                P.dma('sp', 'x0', pkX[0:1, :], xmid[1:2, :])
                P.dma('sp', 'x1', pkX[1:2, :], xmid[NT:NT + 1, :])
                P.barrier()
                P.coll(pkX_t.ap().opt(), gX_t.ap().opt())
                P.barrier()
                for c in range(8):
                    P.dma('sp', f'xg{c}', G[:, c], gX[2 * c:2 * c + 2, :].rearrange("r (p f) -> p r f", p=128), w=[('G', c)])
                masked_sum(hl[:], 'hl', [(G[:, c, 1, :], ('G', c)) for c in range(8)], 0)
                masked_sum(hr[:], 'hr', [(G[:, c, 0, :], ('G', c)) for c in range(8)], 1)
                P.dma('sp', 'x2', xmid[0:1, :].rearrange("o (p f) -> p (o f)", p=128), hl[:], r=['hl'])
                P.dma('sp', 'x3', xmid[NT + 1:NT + 2, :].rearrange("o (p f) -> p (o f)", p=128), hr[:], r=['hr'])
                P.barrier()
                P.emit()

        for l in range(depth):
            xsrc = x_in if l == 0 else x1
            xdst = x1 if l < depth - 1 else y_out
            last = (l == depth - 1)
            W_in = wb_in[l]
            W_in32 = w_in[l]

            with ExitStack() as st:
                gB = sb(st, "gB", [128, D], F32)
                wg = sb(st, "wg", [128, KC, 16], BF16)
                bg = sb(st, "bg", [128, 16], F32)
                xin = [sb(st, f"xin{i}", [128, D], F32) for i in range(2)]
                xnb = sb(st, "xnb", [128, D], BF16)
                sqj = sb(st, "sqj", [128, D], BF16)
                stat = sb(st, "stat", [128, 4], F32)
                xnT = sb(st, "xnT", [128, KC, 512], BF16)
                sig = sb(st, "sig", [128, 4, 512], F32)
                ust = [sb(st, f"ust{i}", [128, 4, 512], F32) for i in range(2)]
                bst = [sb(st, f"bst{i}", [128, 4, 512], BF16) for i in range(2)]
                qst = sb(st, "qst", [128, 4, 8, 128], BF16)
                kst = sb(st, "kst", [128, 4, 8, 128], BF16)
                ktk = sb(st, "ktk", [128, 4, 1024], BF16)
                vst = sb(st, "vst", [128, 4, 2048], BF16)
                gst = sb(st, "gst", [128, 4, 16], F32)
                gtmp = sb(st, "gtmp", [128, 16], F32)
                slots = [sb(st, f"ring{i}", [128, KC, 512], BF16) for i in range(4)]
                tp = [ps(st, f"tp{i}", [128, 1024], BF16) for i in range(2)]
                mm = [ps(st, f"mm{i}", [128, 512], F32) for i in range(5)]
                gps = ps(st, "gps", [128, 16], F32)
                ring = Ring(P, slots)
                cnt = dict(mm=0, tp=0, bst=0)

                def next_mm():
                    i = cnt['mm'] % len(mm)
                    cnt['mm'] += 1
                    return mm[i], ('mm', i)

                def next_tp():
                    i = cnt['tp'] % len(tp)
                    cnt['tp'] += 1
                    return tp[i], ('tp', i)

                P.dma('sp', 'gB', gB[:], norm_mix_g[l].partition_broadcast(128), w=['gvec'])
                P.dma('sp', 'bg', bg[:], b_gates[l].partition_broadcast(128), w=['bg'])
                P.dma('pool', 'wg', wg[:], W_in32[:, C_G:C_G + 16].rearrange("(kc p) c -> p kc c", p=128), w=['wg'])

                def load_x(chunk):
                    b = chunk % 2
                    P.dma('sp', f'xin{b}', xin[b][:], xsrc[chunk * 128:(chunk + 1) * 128, :], w=[('xin', b)])

                def norm_chunk(chunk, ci):
                    b = chunk % 2
                    rmsnorm_rows(xin[b], 128, ('xin', b), gB, 'gvec', xnb, 'xnb', stat, sqj)
                    for g in range(2):
                        tpt, tpk = next_tp()
                        P.op('pe', [I('transpose', out=tpt[:, k * 128:(k + 1) * 128],
                                      in_=xnb[:, (g * 8 + k) * 128:(g * 8 + k + 1) * 128], identity=identb[:])
                                    for k in range(8)], r=['xnb', 'identb'], w=[tpk])
                        dst = xnT[:, g * 8:(g + 1) * 8, ci * 128:(ci + 1) * 128]
                        src = tpt[:].rearrange("p (k t) -> p k t", k=8)
                        if g == 0:
                            P.op('act', I('copy', out=dst, in_=src), r=[tpk], w=[('xnT', ci)])
                        else:
                            P.op('dve', I('tensor_copy', out=dst, in_=src), r=[tpk], w=[('xnT', ci)])

                def a1_tile(ti, c0, n):
                    nt = n * 128
                    t0 = c0 * 128
                    xk = [('xnT', ci) for ci in range(n)]
                    for ci in range(n):
                        if c0 + ci + 1 < NCH:
                            load_x(c0 + ci + 1)
                        norm_chunk(c0 + ci, ci)

                    def ws_block(col0, evac):
                        slot, skey = ring.load([(W_in[:, col0:col0 + 512], KC, 0, 512)])
                        for ct in range(4):
                            pt, pk = next_mm()
                            P.op('pe', [I('matmul', out=pt[:, 0:nt], lhsT=slot[:, kc, ct * 128:(ct + 1) * 128],
                                          rhs=xnT[:, kc, 0:nt], start=(kc == 0), stop=(kc == KC - 1))
                                        for kc in range(KC)], r=[skey] + xk, w=[pk])
                            evac(ct, pt, pk)

                    for i in range(2):
                        def ev_b(ct, pt, pk):
                            P.op('act', I('activation', out=sig[:, ct, 0:nt], in_=pt[:, 0:nt], func=AF.Sigmoid),
                                 r=[pk], w=[('sig', ct)])
                        ws_block(C_B + i * 512, ev_b)
                        us = ust[i]

                        def ev_a(ct, pt, pk):
                            P.op('dve', I('tensor_tensor', out=us[:, ct, 0:nt], in0=pt[:, 0:nt], in1=sig[:, ct, 0:nt],
                                          op=ALU.mult), r=[pk, ('sig', ct)], w=[('ust', i)])
                        ws_block(C_A + i * 512, ev_a)
                        P.dma('sp', f'ust{i}', fm(uT[i * 512:(i + 1) * 512, 16 + t0:16 + t0 + nt]), us[:, :, 0:nt],
                              r=[('ust', i)])
                    for (cbase, stg, sname, dst, scl) in ((C_Q, qst, 'qst', qTc, DK ** -0.5), (C_K, kst, 'kst', kTc, 1.0)):
                        for i in range(2):
                            def ev_q(ct, pt, pk):
                                P.op('act', I('mul', out=stg[:, 0:n, i * 4 + ct, :],
                                              in_=pt[:, 0:nt].rearrange("p (c t) -> p c t", c=n), mul=scl),
                                     r=[pk], w=[sname])
                            ws_block(cbase + i * 512, ev_q)
                        P.dma('sp', sname, dst[c0:c0 + n].rearrange("c p f -> p c f"),
                              stg[:, 0:n].rearrange("p c j t -> p c (j t)"), r=[sname])
                    for ci in range(n):
                        tpt, tpk = next_tp()
                        P.op('pe', [I('transpose', out=tpt[:, j * 128:(j + 1) * 128], in_=kst[:, ci, j, :], identity=identb[:])
                                    for j in range(8)], r=['kst', 'identb'], w=[tpk])
                        P.op('dve', I('tensor_copy', out=ktk[:, ci, :], in_=tpt[:]), r=[tpk], w=['ktk'])
                    P.dma('sp', 'ktk', ktok[t0:t0 + nt, :].rearrange("(c p) f -> p c f", p=128), ktk[:, 0:n, :], r=['ktk'])
                    for i in range(4):
                        slot, skey = ring.load([(W_in[:, C_V + i * 512:C_V + (i + 1) * 512], KC, 0, 512)])
                        for ci in range(n):
                            pt, pk = next_mm()
                            P.op('pe', [I('matmul', out=pt[:, :], lhsT=xnT[:, kc, ci * 128:(ci + 1) * 128], rhs=slot[:, kc, :],
                                          start=(kc == 0), stop=(kc == KC - 1)) for kc in range(KC)],
                                 r=[skey, ('xnT', ci)], w=[pk])
                            if (i + ci) % 2 == 0:
                                P.op('act', I('copy', out=vst[:, ci, i * 512:(i + 1) * 512], in_=pt[:, :]), r=[pk], w=['vst'])
                            else:
                                P.op('dve', I('tensor_copy', out=vst[:, ci, i * 512:(i + 1) * 512], in_=pt[:, :]), r=[pk], w=['vst'])
                    P.dma('sp', 'vst', vtok[t0:t0 + nt, :].rearrange("(c p) f -> p c f", p=128), vst[:, 0:n, :], r=['vst'])
                    for ci in range(n):
                        P.op('pe', [I('matmul', out=gps[:, :], lhsT=xnT[:, kc, ci * 128:(ci + 1) * 128], rhs=wg[:, kc, :],
                                      start=(kc == 0), stop=(kc == KC - 1)) for kc in range(KC)],
                             r=['wg', ('xnT', ci)], w=['gps'])
                        P.op('dve', I('tensor_tensor', out=gst[:, ci, :], in0=gps[:, :], in1=bg[:], op=ALU.add),
                             r=['gps', 'bg'], w=['gst'])
                        gv = gst[:, ci, :].rearrange("p (a b) -> p a b", a=2)[:, :, 4:8]
                        tv = gtmp[:, :].rearrange("p (a b) -> p a b", a=2)[:, :, 4:8]
                        P.op('act', I('activation', out=tv, in_=gv, func=AF.Exp, scale=-1.0), r=['gst'], w=['gtmp'])
                        P.op('act', I('activation', out=tv, in_=tv, func=AF.Ln, bias=1.0), r=['gtmp'], w=['gtmp'])
                        P.op('dve', I('tensor_scalar', out=gv, in0=tv, scalar1=-1.0, scalar2=None, op0=ALU.mult),
                             r=['gtmp'], w=['gst'])
                    P.dma('sp', 'gst', gts[t0:t0 + nt, :].rearrange("(c p) g -> p c g", p=128), gst[:, 0:n, :], r=['gst'])
                    for (cbase, nblk, dst) in ((C_O, 4, ogT), (C_M, 8, gmT)):
                        for i in range(nblk):
                            bb = cnt['bst'] % 2
                            cnt['bst'] += 1
                            bs = bst[bb]

                            def ev_s(ct, pt, pk):
                                P.op('act', I('activation', out=bs[:, ct, 0:nt], in_=pt[:, 0:nt], func=AF.Sigmoid),
                                     r=[pk], w=[('bst', bb)])
                            ws_block(cbase + i * 512, ev_s)
                            P.dma('sp', f'bst{bb}', fm(dst[i * 512:(i + 1) * 512, t0:t0 + nt]), bs[:, :, 0:nt], r=[('bst', bb)])

                load_x(0)
                for ti, (c0, n) in enumerate(atiles):
                    a1_tile(ti, c0, n)
                P.barrier()
                P.emit()
            exchange_u()

            with ExitStack() as st:
                cw = sb(st, "cw", [128, 8, CW], F32)
                cv = sb(st, "cv", [128, 3, 8], F32)
                uin = [sb(st, f"uin{i}", [128, 8, 544], F32) for i in range(2)]
                acc = sb(st, "acc", [128, 8, 512], F32)
                ysq = [sb(st, f"ysq{i}", [128, 512], F32) for i in range(2)]
                mean = sb(st, "mean", [128, 512], F32)
                var = sb(st, "var", [128, 512], F32)
                rstd = sb(st, "rstd", [128, 512], F32)
                zt = [sb(st, f"zt{i}", [128, 512], F32) for i in range(2)]
                sT = sb(st, "sT", [128, 8, 512], BF16)
                gA = [sb(st, f"gA{i}", [128, 4, 512], BF16) for i in range(2)]
                mst = [sb(st, f"mst{i}", [128, 4, 512], F32) for i in range(2)]
                slots = [sb(st, f"ring{i}", [128, KC, 512], BF16) for i in range(3)]
                pmean = ps(st, "pmean", [128, 512])
                psq = ps(st, "psq", [128, 512])
                mm = [ps(st, f"mm{i}", [128, 512], F32) for i in range(4)]
                ring = Ring(P, slots)
                cnt = dict(mm=0, g=0)

                def next_mm():
                    i = cnt['mm'] % len(mm)
                    cnt['mm'] += 1
                    return mm[i], ('mm', i)

                P.dma('sp', 'cw', cw[:], p_cw[l], w=['cw'])
                P.dma('sp', 'cv', cv[:], p_cv[l], w=['cv'])

                def load_u(ti):
                    c0, n = atiles[ti]
                    nt = n * 128
                    t0 = c0 * 128
                    b = ti % 2
                    P.dma('sp', f'uin{b}', uin[b][:, :, 0:nt + 32], fm(uT[:, t0:t0 + nt + 32]), w=[('uin', b)])

                def a2_tile(ti, c0, n):
                    nt = n * 128
                    t0 = c0 * 128
                    b = ti % 2
                    if ti + 1 < len(atiles):
                        load_u(ti + 1)
                    ub = uin[b]
                    for wtap in range(CW):
                        for j in range(8):
                            if wtap == 0:
                                P.op('dve', I('tensor_scalar', out=acc[:, j, 0:nt], in0=ub[:, j, 1:1 + nt],
                                              scalar1=cw[:, j, 0:1], scalar2=cv[:, 0, j:j + 1], op0=ALU.mult, op1=ALU.add),
                                     r=[('uin', b), 'cw', 'cv'], w=[('acc', j)])
                            else:
                                P.op('dve', I('scalar_tensor_tensor', out=acc[:, j, 0:nt], in0=ub[:, j, 1 + wtap:1 + wtap + nt],
                                              scalar=cw[:, j, wtap:wtap + 1], in1=acc[:, j, 0:nt], op0=ALU.mult, op1=ALU.add),
                                     r=[('uin', b), 'cw', ('acc', j)], w=[('acc', j)])
                    for j in range(8):
                        yb = j % 2
                        P.op('act', I('activation', out=ysq[yb][:, 0:nt], in_=acc[:, j, 0:nt], func=AF.Square),
                             r=[('acc', j)], w=[('ysq', yb)])
                        P.op('pe', I('matmul', out=pmean[:, 0:nt], lhsT=ones1k[:], rhs=acc[:, j, 0:nt],
                                     start=(j == 0), stop=(j == 7)), r=[('acc', j), 'ones1k'], w=['pmean'])
                        P.op('pe', I('matmul', out=psq[:, 0:nt], lhsT=ones1k[:], rhs=ysq[yb][:, 0:nt],
                                     start=(j == 0), stop=(j == 7)), r=[('ysq', yb), 'ones1k'], w=['psq'])
                    P.op('act', I('copy', out=mean[:, 0:nt], in_=pmean[:, 0:nt]), r=['pmean'], w=['mean'])
                    P.op('dve', I('tensor_tensor', out=var[:, 0:nt], in0=mean[:, 0:nt], in1=mean[:, 0:nt], op=ALU.mult),
                         r=['mean'], w=['var'])
                    P.op('dve', I('tensor_tensor', out=var[:, 0:nt], in0=psq[:, 0:nt], in1=var[:, 0:nt], op=ALU.subtract),
                         r=['psq', 'var'], w=['var'])
                    P.op('act', I('activation', out=var[:, 0:nt], in_=var[:, 0:nt], func=AF.Sqrt, bias=epsc[:]),
                         r=['var', 'epsc'], w=['var'])
                    P.op('dve', I('reciprocal', out=rstd[:, 0:nt], in_=var[:, 0:nt]), r=['var'], w=['rstd'])
                    for j in range(8):
                        zb = j % 2
                        P.op('dve', I('tensor_tensor', out=zt[zb][:, 0:nt], in0=acc[:, j, 0:nt], in1=mean[:, 0:nt],
                                      op=ALU.subtract), r=[('acc', j), 'mean'], w=[('zt', zb)])
                        P.op('dve', I('tensor_tensor', out=zt[zb][:, 0:nt], in0=zt[zb][:, 0:nt], in1=rstd[:, 0:nt],
                                      op=ALU.mult), r=[('zt', zb), 'rstd'], w=[('zt', zb)])
                        P.op('act', I('activation', out=sT[:, j, 0:nt], in_=zt[zb][:, 0:nt], func=AF.Silu,
                                      scale=cv[:, 1, j:j + 1], bias=cv[:, 2, j:j + 1]),
                             r=[('zt', zb), 'cv'], w=['sT'])
                    for blk in range(4):
                        gb = cnt['g'] % 2
                        cnt['g'] += 1
                        P.dma('sp', f'gA{gb}', gA[gb][:, :, 0:nt], fm(gmT[blk * 512:(blk + 1) * 512, t0:t0 + nt]),
                              w=[('gA', gb)])
                        slot, skey = ring.load([(wb_conv_out[l][:, blk * 512:(blk + 1) * 512], 8, 0, 512)])
                        for ct in range(4):
                            pt, pk = next_mm()
                            P.op('pe', [I('matmul', out=pt[:, 0:nt], lhsT=slot[:, kc, ct * 128:(ct + 1) * 128],
                                          rhs=sT[:, kc, 0:nt], start=(kc == 0), stop=(kc == 7)) for kc in range(8)],
                                 r=[skey, 'sT'], w=[pk])
                            P.op('dve', I('tensor_tensor', out=mst[gb][:, ct, 0:nt], in0=pt[:, 0:nt], in1=gA[gb][:, ct, 0:nt],
                                          op=ALU.mult), r=[pk, ('gA', gb)], w=[('mst', gb)])
                        P.dma('sp', f'mst{gb}', fm(mixT[blk * 512:(blk + 1) * 512, t0:t0 + nt]), mst[gb][:, :, 0:nt],
                              r=[('mst', gb)])

                load_u(0)
                for ti, (c0, n) in enumerate(atiles):
                    a2_tile(ti, c0, n)
                P.barrier()
                P.emit()

            with ExitStack() as st:
                S = {}
                for d in range(2):
                    S[d] = dict(
                        C=sb(st, f"C{d}", [128, 8, 512], F32),
                        Cb=sb(st, f"Cb{d}", [128, 8, 512], BF16),
                        n=sb(st, f"n{d}", [128, 8], F32),
                        nB=sb(st, f"nB{d}", [128, 8, 128], BF16),
                        qT=[sb(st, f"qT{d}{i}", [128, 8, 128], BF16) for i in range(2)],
                        kT=[sb(st, f"kT{d}{i}", [128, 8, 128], BF16) for i in range(2)],
                        kt=[sb(st, f"kt{d}{i}", [128, 1024], BF16) for i in range(2)],
                        vt=[sb(st, f"vt{d}{i}", [128, 2048], BF16) for i in range(2)],
                        g=[sb(st, f"g{d}{i}", [128, 16], F32) for i in range(2)],
                        lfB=sb(st, f"lfB{d}", [128, 4, 128], F32),
                        bias=sb(st, f"bias{d}", [128, 4], F32),
                        E=sb(st, f"E{d}", [128, 4, 128], F32),
                        Dm=sb(st, f"Dm{d}", [128, 4, 128], F32),
                        qt=sb(st, f"qt{d}", [128, 8, 128], BF16),
                        wk=sb(st, f"wk{d}", [128, 1024], BF16),
                        sw=sb(st, f"sw{d}", [128, 4, 128], BF16),
                        aden=sb(st, f"aden{d}", [128, 4, 128], F32),
                        rden=sb(st, f"rden{d}", [128, 4, 128], F32),
                        hst=[sb(st, f"hst{d}{i}", [128, 16, 128], F32) for i in range(2)],
                        ld=sb(st, f"ld{d}", [128, 4], F32),
                    )
                XH = 2048 + 12
                xa1 = sb(st, "xa1", [128, XH], F32)
                xa2 = sb(st, "xa2", [128, XH], F32)
                xlb = [sb(st, f"xlb{i}", [128, XH], F32) for i in range(2)]
                xdec = sb(st, "xdec", [128, 4], F32)
                pmisc = ps(st, "pmisc", [128, 16])
                pBu = ps(st, "pBu", [128, 4, 128])
                pBm = ps(st, "pBm", [128, 4, 128])
                psT = ps(st, "psT", [128, 4, 128])
                pden = ps(st, "pden", [128, 4, 128])
                pnum = [ps(st, f"pnum{i}", [128, 4, 128]) for i in range(2)]
                pdC = ps(st, "pdC", [128, 512])
                cnt = dict(num=0)

                for d in range(2):
                    sd = S[d]
                    P.op('dve', I('memset', ap=sd['C'][:], constant=0.0), w=[('C', d, j) for j in range(8)])
                    P.op('dve', I('memset', ap=sd['Cb'][:], constant=0.0), w=[('Cb', d)])
                    P.op('dve', I('memset', ap=sd['n'][:], constant=0.0), w=[('n', d)])
                    P.op('dve', I('memset', ap=sd['nB'][:], constant=0.0), w=[('nB', d)])
                    P.op('dve', I('memset', ap=sd['ld'][:], constant=0.0), w=[('ld', d)])

                def scan_load(d, step, full=True):
                    c = step if d == 0 else NCH - 1 - step
                    b = step % 2
                    sd = S[d]
                    r0 = c * 128
                    if full:
                        P.dma('sp', f'sq{d}{b}', sd['qT'][b][:].rearrange("p j t -> p (j t)"), qTc[c], w=[('qT', d, b)])
                        P.dma('sp', f'sk{d}{b}', sd['kT'][b][:].rearrange("p j t -> p (j t)"), kTc[c], w=[('kT', d, b)])
                    P.dma('sp', f'st{d}{b}', sd['kt'][b][:], ktok[r0:r0 + 128, :], w=[('kt', d, b)])
                    P.dma('sp', f'sv{d}{b}', sd['vt'][b][:], vtok[r0:r0 + 128, :], w=[('vt', d, b)])
                    P.dma('sp', f'sg{d}{b}', sd['g'][b][:], gts[r0:r0 + 128, :], w=[('g', d, b)])

                def scan_step(d, step, full=True):
                    c = step if d == 0 else NCH - 1 - step
                    b = step % 2
                    sd = S[d]
                    go = 0 if d == 0 else 8
                    tl = 127 if d == 0 else 0
                    qT, kT, kt, vt, g = sd['qT'][b], sd['kT'][b], sd['kt'][b], sd['vt'][b], sd['g'][b]
                    kq, kk, kkt, kv, kg = ('qT', d, b), ('kT', d, b), ('kt', d, b), ('vt', d, b), ('g', d, b)
                    ig = g[:, go:go + 4]
                    lf = g[:, go + 4:go + 8]
                    TRI = tri[:, d, :]
                    NEGM = tri[:, 2 + d, :]
                    C, Cb, n, nB = sd['C'], sd['Cb'], sd['n'], sd['nB']
                    lfB, bias, E, Dm, qt, wk, sw, aden, rden = (sd[k] for k in ('lfB', 'bias', 'E', 'Dm', 'qt', 'wk', 'sw', 'aden', 'rden'))
                    hst = sd['hst'][b]
                    P.op('pe', I('matmul', out=pmisc[:, 0:4], lhsT=TRI, rhs=lf, start=True, stop=True),
                         r=[kg, 'tri'], w=['pmisc'])
                    P.op('dve', I('tensor_copy', out=lfB[:], in_=lf.unsqueeze(2).broadcast_to([128, 4, 128])),
                         r=[kg], w=[('lfB', d)])
                    P.op('dve', I('tensor_tensor', out=bias[:], in0=ig, in1=pmisc[:, 0:4], op=ALU.subtract),
                         r=[kg, 'pmisc'], w=[('bias', d)])
                    fu = []
                    fmm = []
                    for h in range(4):
                        fu.append(I('matmul', out=pBu[:, h, :], lhsT=lfB[:, h, :], rhs=TRI, start=True, stop=True))
                        fmm.append(I('matmul', out=pBm[:, h, :], lhsT=lfB[:, h, :], rhs=TRI, start=True, stop=False))
                        fmm.append(I('matmul', out=pBm[:, h, :], lhsT=identf[:], rhs=NEGM, start=False, stop=True))
                    P.op('pe', fu, r=[('lfB', d), 'tri'], w=['pBu'])
                    P.op('pe', fmm, r=[('lfB', d), 'tri', 'identf'], w=['pBm'])
                    P.op('act', I('activation', out=E[:], in_=pBu[:], func=AF.Exp), r=['pBu'], w=[('E', d)])
                    for h in range(4):
                        P.op('act', I('activation', out=Dm[:, h, :], in_=pBm[:, h, :], func=AF.Exp, bias=bias[:, h:h + 1]),
                             r=['pBm', ('bias', d)], w=[('Dm', d)])
                    P.op('dve', I('tensor_tensor', out=wk[:].rearrange("p (h f) -> p h f", h=4),
                                  in0=kt[:].rearrange("p (h f) -> p h f", h=4),
                                  in1=Dm[:, :, tl:tl + 1].broadcast_to([128, 4, 256]), op=ALU.mult),
                         r=[kkt, ('Dm', d)], w=[('wk', d)])
                    if full:
                        P.op('dve', I('tensor_tensor', out=qt[:].rearrange("p (h c) t -> p h c t", h=4),
                                      in0=qT[:].rearrange("p (h c) t -> p h c t", h=4),
                                      in1=E[:].unsqueeze(2).broadcast_to([128, 4, 2, 128]), op=ALU.mult),
                             r=[kq, ('E', d)], w=[('qt', d)])
                        P.op('pe', [I('matmul', out=psT[:, h, :], lhsT=kT[:, h * 2 + cc, :], rhs=qT[:, h * 2 + cc, :],
                                      start=(cc == 0), stop=(cc == 1)) for h in range(4) for cc in range(2)],
                             r=[kk, kq], w=['psT'])
                        P.op('dve', I('tensor_tensor', out=sw[:], in0=psT[:], in1=Dm[:], op=ALU.mult),
                             r=['psT', ('Dm', d)], w=[('sw', d)])
                        fd = []
                        for h in range(4):
                            fd.append(I('matmul', out=pden[:, h, :], lhsT=onesb[:], rhs=sw[:, h, :], start=True, stop=False))
                            for cc in range(2):
                                fd.append(I('matmul', out=pden[:, h, :], lhsT=nB[:, h * 2 + cc, :], rhs=qt[:, h * 2 + cc, :],
                                            start=False, stop=(cc == 1)))
                        P.op('pe', fd, r=[('sw', d), ('qt', d), ('nB', d), 'onesb'], w=['pden'])
                        P.op('act', I('activation', out=aden[:], in_=pden[:], func=AF.Abs), r=['pden'], w=[('aden', d)])
                        P.op('dve', I('tensor_scalar', out=aden[:], in0=aden[:], scalar1=1.0, scalar2=None, op0=ALU.max),
                             r=[('aden', d)], w=[('aden', d)])
                        P.op('dve', I('reciprocal', out=rden[:], in_=aden[:]), r=[('aden', d)], w=[('rden', d)])
                        for h in range(4):
                            pi = cnt['num'] % 2
                            cnt['num'] += 1
                            pn = pnum[pi]
                            pnk = ('pnum', pi)
                            fn = []
                            for et in range(4):
                                fn.append(I('matmul', out=pn[:, et, :], lhsT=vt[:, h * 512 + et * 128:h * 512 + (et + 1) * 128],
                                            rhs=sw[:, h, :], start=True, stop=False))
                                for cc in range(2):
                                    fn.append(I('matmul', out=pn[:, et, :], lhsT=Cb[:, h * 2 + cc, et * 128:(et + 1) * 128],
                                                rhs=qt[:, h * 2 + cc, :], start=False, stop=(cc == 1)))
                            P.op('pe', fn, r=[kv, ('sw', d), ('Cb', d), ('qt', d)], w=[pnk])
                            P.op('dve', I('tensor_tensor', out=hst[:, h * 4:(h + 1) * 4, :], in0=pn[:],
                                          in1=rden[:, h:h + 1, :].broadcast_to([128, 4, 128]), op=ALU.mult),
                                 r=[pnk, ('rden', d)], w=[('hst', d, b)])
                        hdst = hfT if d == 0 else hbT
                        P.dma('sp', f'hst{d}{b}', fm(hdst[:, c * 128:(c + 1) * 128]), hst[:], r=[('hst', d, b)])
                    else:
                        P.op('dve', I('tensor_tensor', out=sd['ld'][:], in0=sd['ld'][:], in1=pBu[:, :, tl], op=ALU.add),
                             r=[('ld', d), 'pBu'], w=[('ld', d)])
                    P.op('pe', [I('matmul', out=pmisc[:, 4 + j:5 + j], lhsT=wk[:, j * 128:(j + 1) * 128], rhs=onesb[:, 0:1],
                                  start=True, stop=True) for j in range(8)], r=[('wk', d), 'onesb'], w=['pmisc'])
                    for h in range(4):
                        P.op('dve', I('scalar_tensor_tensor', out=n[:, h * 2:h * 2 + 2], in0=n[:, h * 2:h * 2 + 2],
                                      scalar=E[:, h, tl:tl + 1], in1=pmisc[:, 4 + h * 2:6 + h * 2], op0=ALU.mult, op1=ALU.add),
                             r=[('n', d), ('E', d), 'pmisc'], w=[('n', d)])
                    for h in range(4):
                        for cc in range(2):
                            j = h * 2 + cc
                            P.op('pe', I('matmul', out=pdC[:, :], lhsT=wk[:, j * 128:(j + 1) * 128],
                                         rhs=vt[:, h * 512:(h + 1) * 512], start=True, stop=True),
                                 r=[('wk', d), kv], w=['pdC'])
                            P.op('dve', I('scalar_tensor_tensor', out=C[:, j, :], in0=C[:, j, :], scalar=E[:, h, tl:tl + 1],
                                          in1=pdC[:, :], op0=ALU.mult, op1=ALU.add),
                                 r=[('C', d, j), ('E', d), 'pdC'], w=[('C', d, j)])
                            if full:
                                P.op('act', I('copy', out=Cb[:, j, :], in_=C[:, j, :]), r=[('C', d, j)], w=[('Cb', d)])
                    if full:
                        P.op('dve', I('tensor_copy', out=nB[:], in_=n[:].unsqueeze(2).broadcast_to([128, 8, 128])),
                             r=[('n', d)], w=[('nB', d)])

                for d in range(2):
                    scan_load(d, 0, False)
                for step in range(NCH):
                    for d in range(2):
                        if step + 1 < NCH:
                            scan_load(d, step + 1, False)
                        scan_step(d, step, False)
                for d in range(2):
                    sd = S[d]
                    P.dma('sp', f'xs{d}0', pkS[:, d * SW:d * SW + 4096], sd['C'][:].rearrange("p j e -> p (j e)"),
                          r=[('C', d, j) for j in range(8)])
                    P.dma('sp', f'xs{d}1', pkS[:, d * SW + 4096:d * SW + 4104], sd['n'][:], r=[('n', d)])
                    P.dma('sp', f'xs{d}2', pkS[:, d * SW + 4104:d * SW + 4108], sd['ld'][:], r=[('ld', d)])
                P.barrier()
                P.coll(pkS_t.ap().opt(), gS_t.ap().opt())
                P.barrier()
                li = 0
                for d in range(2):
                    sd = S[d]
                    k1, k2 = (0, 2) if d == 0 else (1, 3)
                    Cf = sd['C'][:].rearrange("p j e -> p (j e)")
                    for half in (1, 0):
                        col0 = d * SW + (2048 if half == 1 else 0)
                        ncol = XH if half == 1 else 2048
                        srcs = []
                        for c8 in range(8):
                            lb = li % 2
                            li += 1
                            srcs.append((c8, lb))
                        for (dst, dkey, kk) in ((xa1, 'xa1', k1), (xa2, 'xa2', k2)):
                            for c8 in range(8):
                                lb = c8 % 2
                                P.dma('sp', f'xlb{lb}', xlb[lb][:, 0:ncol], gS[c8 * 128:(c8 + 1) * 128, col0:col0 + ncol],
                                      w=[('xlb', lb)])
                                if c8 == 0:
                                    P.op('dve', I('tensor_scalar', out=dst[:, 0:ncol], in0=xlb[lb][:, 0:ncol],
                                                  scalar1=sel[:, kk, 0:1], scalar2=None, op0=ALU.mult),
                                         r=[('xlb', lb), 'sel'], w=[dkey])
                                else:
                                    P.op('dve', I('scalar_tensor_tensor', out=dst[:, 0:ncol], in0=xlb[lb][:, 0:ncol],
                                                  scalar=sel[:, kk, c8:c8 + 1], in1=dst[:, 0:ncol], op0=ALU.mult, op1=ALU.add),
                                         r=[('xlb', lb), 'sel', dkey], w=[dkey])
                        if half == 1:
                            P.op('act', I('activation', out=xdec[:], in_=xa1[:, 2056:2060], func=AF.Exp), r=['xa1'], w=['xdec'])
                            hs_ = (2, 3)
                            P.op('dve', I('tensor_tensor', out=sd['n'][:].rearrange("p (h c) -> p h c", h=4),
                                          in0=xa2[:, 2048:2056].rearrange("p (h c) -> p h c", h=4),
                                          in1=xdec[:].unsqueeze(2).broadcast_to([128, 4, 2]), op=ALU.mult),
                                 r=['xa2', 'xdec'], w=[('n', d)])
                            P.op('dve', I('tensor_tensor', out=sd['n'][:], in0=sd['n'][:], in1=xa1[:, 2048:2056], op=ALU.add),
                                 r=[('n', d), 'xa1'], w=[('n', d)])
                        else:
                            hs_ = (0, 1)
                        for hi, h in enumerate(hs_):
                            P.op('dve', I('scalar_tensor_tensor', out=Cf[:, h * 1024:(h + 1) * 1024],
                                          in0=xa2[:, hi * 1024:(hi + 1) * 1024], scalar=xdec[:, h:h + 1],
                                          in1=xa1[:, hi * 1024:(hi + 1) * 1024], op0=ALU.mult, op1=ALU.add),
                                 r=['xa1', 'xa2', 'xdec'], w=[('C', d, 2 * h), ('C', d, 2 * h + 1)])
                    P.op('act', I('copy', out=sd['Cb'][:], in_=sd['C'][:]), r=[('C', d, j) for j in range(8)], w=[('Cb', d)])
                    P.op('dve', I('tensor_copy', out=sd['nB'][:], in_=sd['n'][:].unsqueeze(2).broadcast_to([128, 8, 128])),
                         r=[('n', d)], w=[('nB', d)])
                for d in range(2):
                    scan_load(d, 0)
                for step in range(NCH):
                    for d in range(2):
                        if step + 1 < NCH:
                            scan_load(d, step + 1)
                        scan_step(d, step)
                P.barrier()
                P.emit()

            with ExitStack() as st:
                hgn = sb(st, "hgn", [128, 16], F32)
                hf = sb(st, "hf", [128, 4, 512], F32)
                hb = sb(st, "hb", [128, 4, 512], F32)
                hs = sb(st, "hs", [128, 4, 512], F32)
                hsq = [sb(st, f"hsq{i}", [128, 512], F32) for i in range(2)]
                sdv = sb(st, "sdv", [128, 512], F32)
                rs = sb(st, "rs", [128, 512], F32)
                t1 = [sb(st, f"t1{i}", [128, 512], F32) for i in range(2)]
                og = [sb(st, f"og{i}", [128, 4, 512], BF16) for i in range(2)]
                hg = sb(st, "hg", [128, 16, 512], BF16)
                gBt = [sb(st, f"gBt{i}", [128, 4, 512], BF16) for i in range(2)]
                mxa = [sb(st, f"mxa{i}", [128, 4, 512], F32) for i in range(2)]
                mx = sb(st, "mx", [128, 16, 512], BF16)
                xres = [sb(st, f"xres{i}", [128, 512], F32) for i in range(4)]
                xm = [sb(st, f"xm{i}", [128, 512], F32) for i in range(4)]
                slots = [sb(st, f"ring{i}", [128, KC, 512], BF16) for i in range(3)]
                pms = ps(st, "pms", [128, 512])
                mm = [ps(st, f"mm{i}", [128, 512], F32) for i in range(6)]
                ring = Ring(P, slots)
                cnt = dict(mm=0, o=0, g=0, x=0)

                def next_mm():
                    i = cnt['mm'] % len(mm)
                    cnt['mm'] += 1
                    return mm[i], ('mm', i)

                P.dma('sp', 'hgn', hgn[:], p_hg[l], w=['hgn'])

                def c1_tile(ti, c0, n):
                    nt = n * 128
                    t0 = c0 * 128
                    for h in range(4):
                        ob = cnt['o'] % 2
                        cnt['o'] += 1
                        rows = slice(h * 512, (h + 1) * 512)
                        P.dma('sp', 'hf', hf[:, :, 0:nt], fm(hfT[rows, t0:t0 + nt]), w=['hf'])
                        P.dma('sp', 'hb', hb[:, :, 0:nt], fm(hbT[rows, t0:t0 + nt]), w=['hb'])
                        P.dma('sp', f'og{ob}', og[ob][:, :, 0:nt], fm(ogT[rows, t0:t0 + nt]), w=[('og', ob)])
                        P.op('dve', I('tensor_tensor', out=hs[:, :, 0:nt], in0=hf[:, :, 0:nt], in1=hb[:, :, 0:nt], op=ALU.add),
                             r=['hf', 'hb'], w=['hs'])
                        for et in range(4):
                            qb = et % 2
                            P.op('act', I('activation', out=hsq[qb][:, 0:nt], in_=hs[:, et, 0:nt], func=AF.Square),
                                 r=['hs'], w=[('hsq', qb)])
                            P.op('pe', I('matmul', out=pms[:, 0:nt], lhsT=ones512[:], rhs=hsq[qb][:, 0:nt],
                                         start=(et == 0), stop=(et == 3)), r=[('hsq', qb), 'ones512'], w=['pms'])
                        P.op('act', I('activation', out=sdv[:, 0:nt], in_=pms[:, 0:nt], func=AF.Sqrt, bias=epsc[:]),
                             r=['pms', 'epsc'], w=['sdv'])
                        P.op('dve', I('reciprocal', out=rs[:, 0:nt], in_=sdv[:, 0:nt]), r=['sdv'], w=['rs'])
                        for et in range(4):
                            tb = et % 2
                            P.op('dve', I('tensor_tensor', out=t1[tb][:, 0:nt], in0=hs[:, et, 0:nt], in1=rs[:, 0:nt], op=ALU.mult),
                                 r=['hs', 'rs'], w=[('t1', tb)])
                            P.op('dve', I('scalar_tensor_tensor', out=hg[:, h * 4 + et, 0:nt], in0=t1[tb][:, 0:nt],
                                          scalar=hgn[:, h * 4 + et:h * 4 + et + 1], in1=og[ob][:, et, 0:nt],
                                          op0=ALU.mult, op1=ALU.mult),
                                 r=[('t1', tb), 'hgn', ('og', ob)], w=['hg'])
                    for blk in range(4):
                        gb = cnt['g'] % 2
                        cnt['g'] += 1
                        P.dma('sp', f'gBt{gb}', gBt[gb][:, :, 0:nt], fm(gmT[D + blk * 512:D + (blk + 1) * 512, t0:t0 + nt]),
                              w=[('gBt', gb)])
                        P.dma('sp', f'mxa{gb}', mxa[gb][:, :, 0:nt], fm(mixT[blk * 512:(blk + 1) * 512, t0:t0 + nt]),
                              w=[('mxa', gb)])
                        slot, skey = ring.load([(wb_mlstm_out[l][:, blk * 512:(blk + 1) * 512], KC, 0, 512)])
                        for ct in range(4):
                            pt, pk = next_mm()
                            P.op('pe', [I('matmul', out=pt[:, 0:nt], lhsT=slot[:, kc, ct * 128:(ct + 1) * 128], rhs=hg[:, kc, 0:nt],
                                          start=(kc == 0), stop=(kc == KC - 1)) for kc in range(KC)],
                                 r=[skey, 'hg'], w=[pk])
                            tb = ct % 2
                            P.op('dve', I('tensor_tensor', out=t1[tb][:, 0:nt], in0=pt[:, 0:nt], in1=gBt[gb][:, ct, 0:nt],
                                          op=ALU.mult), r=[pk, ('gBt', gb)], w=[('t1', tb)])
                            P.op('dve', I('tensor_tensor', out=mx[:, blk * 4 + ct, 0:nt], in0=t1[tb][:, 0:nt],
                                          in1=mxa[gb][:, ct, 0:nt], op=ALU.add),
                                 r=[('t1', tb), ('mxa', gb)], w=['mx'])
                    for blk in range(4):
                        slot, skey = ring.load([(wb_out[l][:, blk * 512:(blk + 1) * 512], KC, 0, 512)])
                        for ci in range(n):
                            xb = cnt['x'] % 4
                            cnt['x'] += 1
                            r0 = t0 + ci * 128
                            P.dma('sp', f'xres{xb}', xres[xb][:], xsrc[r0:r0 + 128, blk * 512:(blk + 1) * 512], w=[('xres', xb)])
                            pt, pk = next_mm()
                            P.op('pe', [I('matmul', out=pt[:, :], lhsT=mx[:, kc, ci * 128:(ci + 1) * 128], rhs=slot[:, kc, :],
                                          start=(kc == 0), stop=(kc == KC - 1)) for kc in range(KC)],
                                 r=[skey, 'mx'], w=[pk])
                            P.op('dve', I('scalar_tensor_tensor', out=xm[xb][:], in0=pt[:, :], scalar=mcol[:, c0 + ci:c0 + ci + 1],
                                          in1=xres[xb][:], op0=ALU.mult, op1=ALU.add),
                                 r=[pk, ('xres', xb), 'mcol'], w=[('xm', xb)])
                            P.dma('sp', f'xm{xb}', xmid[1 + r0:1 + r0 + 128, blk * 512:(blk + 1) * 512], xm[xb][:],
                                  r=[('xm', xb)])

                for ti, (c0, n) in enumerate(atiles):
                    c1_tile(ti, c0, n)
                P.barrier()
                P.emit()
            exchange_x()

            with ExitStack() as st:
                gF = sb(st, "gF", [128, D], F32)
                gL = sb(st, "gL", [128, D], F32) if last else None
                fw = sb(st, "fw", [128, NF, 4], F32)
                xl = sb(st, "xl", [128, D], F32)
                xnb = sb(st, "xnb2", [128, D], BF16)
                sqj = sb(st, "sqj2", [128, D], BF16)
                stat = sb(st, "stat2", [128, 4], F32)
                xnT = sb(st, "xn2T", [128, KC, 512], BF16)
                gt = [sb(st, f"gt{i}", [128, 512], F32) for i in range(2)]
                cvt = [sb(st, f"cvt{i}", [128, 512], F32) for i in range(2)]
                hid = sb(st, "hid", [128, NF, 512], BF16)
                xrow = [sb(st, f"xrow{i}", [128, D], F32) for i in range(4)]
                slots = [sb(st, f"ring{i}", [128, KC, 512], BF16) for i in range(3)]
                tp = ps(st, "tp", [128, 1024], BF16)
                gbk = [ps(st, f"gbk{i}", [128, 512]) for i in range(7)]
                pg = [(gbk[0], ('gbk', 0)), (gbk[1], ('gbk', 1))]
                pv = [(gbk[2], ('gbk', 2)), (gbk[3], ('gbk', 3))]
                pd = [(gbk[4], ('gbk', 4)), (gbk[5], ('gbk', 5)), (gbk[6], ('gbk', 6)), (gbk[3], ('gbk', 3))]
                ring = Ring(P, slots)
                cnt = dict(pg=0)

                P.dma('sp', 'gF', gF[:], norm_ffn_g[l].partition_broadcast(128), w=['gvec'])
                if last:
                    P.dma('sp', 'gL', gL[:], norm_final_g.partition_broadcast(128), w=['gL'])
                P.dma('sp', 'fw', fw[:], p_fw[l], w=['fw'])

                ftiles = []
                s_ = 0
                while s_ < NT:
                    nv_ = min(510, NT - s_)
                    ftiles.append((s_, nv_))
                    s_ += nv_

                def c2_tile(s, nv):
                    ncol = nv + 2
                    nblk = (ncol + 127) // 128
                    for bi in range(nblk):
                        nr = min(128, ncol - bi * 128)
                        P.dma('sp', 'xl', xl[0:nr, :], xmid[s + bi * 128:s + bi * 128 + nr, :], w=['xl'])
                        rmsnorm_rows(xl, nr, 'xl', gF, 'gvec', xnb, 'xnb', stat, sqj)
                        for g in range(2):
                            P.op('pe', [I('transpose', out=tp[:, k * 128:k * 128 + nr],
                                          in_=xnb[0:nr, (g * 8 + k) * 128:(g * 8 + k + 1) * 128], identity=identb[0:nr, 0:nr])
                                        for k in range(8)], r=['xnb', 'identb'], w=['tp'])
                            dst = xnT[:, g * 8:(g + 1) * 8, bi * 128:bi * 128 + nr]
                            src = tp[:].rearrange("p (k t) -> p k t", k=8)[:, :, 0:nr]
                            if g == 0:
                                P.op('act', I('copy', out=dst, in_=src), r=['tp'], w=['xnT'])
                            else:
                                P.op('dve', I('tensor_copy', out=dst, in_=src), r=['tp'], w=['xnT'])
                    for j0 in range(0, NF, 2):
                        nj = min(2, NF - j0)
                        slot, skey = ring.load([(wb_up[l][:, j0 * 128:(j0 + nj) * 128], KC, 0, nj * 128),
                                                (wb_up[l][:, DFF + j0 * 128:DFF + (j0 + nj) * 128], KC, 256, nj * 128)])
                        for jj in range(nj):
                            j = j0 + jj
                            pb = cnt['pg'] % 2
                            cnt['pg'] += 1
                            pgt, pgk = pg[pb]
                            pvt, pvk = pv[pb]
                            P.op('pe', [I('matmul', out=pgt[:, 0:ncol], lhsT=slot[:, kc, jj * 128:(jj + 1) * 128], rhs=xnT[:, kc, 0:ncol],
                                          start=(kc == 0), stop=(kc == KC - 1)) for kc in range(KC)],
                                 r=[skey, 'xnT'], w=[pgk])
                            P.op('pe', [I('matmul', out=pvt[:, 0:ncol], lhsT=slot[:, kc, 256 + jj * 128:256 + (jj + 1) * 128],
                                          rhs=xnT[:, kc, 0:ncol], start=(kc == 0), stop=(kc == KC - 1)) for kc in range(KC)],
                                 r=[skey, 'xnT'], w=[pvk])
                            P.op('act', I('copy', out=gt[pb][:, 0:ncol], in_=pgt[:, 0:ncol]), r=[pgk], w=[('gt', pb)])
                            P.op('dve', I('tensor_scalar', out=cvt[pb][:, 0:nv], in0=gt[pb][:, 0:nv], scalar1=fw[:, j, 0:1],
                                          scalar2=fw[:, j, 3:4], op0=ALU.mult, op1=ALU.add),
                                 r=[('gt', pb), 'fw'], w=[('cvt', pb)])
                            for wt in (1, 2):
                                P.op('dve', I('scalar_tensor_tensor', out=cvt[pb][:, 0:nv], in0=gt[pb][:, wt:wt + nv],
                                              scalar=fw[:, j, wt:wt + 1], in1=cvt[pb][:, 0:nv], op0=ALU.mult, op1=ALU.add),
                                     r=[('gt', pb), 'fw', ('cvt', pb)], w=[('cvt', pb)])
                            P.op('act', I('activation', out=cvt[pb][:, 0:nv], in_=cvt[pb][:, 0:nv], func=AF.Gelu),
                                 r=[('cvt', pb)], w=[('cvt', pb)])
                            P.op('dve', I('tensor_tensor', out=hid[:, j, 0:nv], in0=cvt[pb][:, 0:nv], in1=pvt[:, 1:1 + nv],
                                          op=ALU.mult), r=[('cvt', pb), pvk], w=['hid'])
                    ntb = (nv + 127) // 128
                    kparts = [(0, 15), (15, 15), (30, 13)]
                    nrs = [min(128, nv - tbi * 128) for tbi in range(ntb)]
                    for tbi in range(ntb):
                        tok0 = s + tbi * 128
                        P.dma('sp', f'xrow{tbi}', xrow[tbi][0:nrs[tbi], :], xmid[1 + tok0:1 + tok0 + nrs[tbi], :], w=[('xrow', tbi)])
                    for blk in range(4):
                        for kp, (k0, nk) in enumerate(kparts):
                            slot, skey = ring.load([(wb_down[l][k0 * 128:(k0 + nk) * 128, blk * 512:(blk + 1) * 512], nk, 0, 512)])
                            for tbi in range(ntb):
                                nr = nrs[tbi]
                                pdt, pdk = pd[tbi]
                                P.op('pe', [I('matmul', out=pdt[0:nr, :], lhsT=hid[:, k0 + kk, tbi * 128:tbi * 128 + nr],
                                              rhs=slot[:, kk, :], start=(kp == 0 and kk == 0), stop=(kp == 2 and kk == nk - 1))
                                            for kk in range(nk)], r=[skey, 'hid'], w=[pdk])
                        for tbi in range(ntb):
                            nr = nrs[tbi]
                            pdt, pdk = pd[tbi]
                            P.op('dve', I('tensor_tensor', out=xrow[tbi][0:nr, blk * 512:(blk + 1) * 512], in0=pdt[0:nr, :],
                                          in1=xrow[tbi][0:nr, blk * 512:(blk + 1) * 512], op=ALU.add),
                                 r=[pdk, ('xrow', tbi)], w=[('xrow', tbi)])
                    for tbi in range(ntb):
                        nr = nrs[tbi]
                        tok0 = s + tbi * 128
                        if last:
                            rmsnorm_rows(xrow[tbi], nr, ('xrow', tbi), gL, 'gL', xrow[tbi], ('xrow', tbi), stat, sqj)
                        P.dma('sp', f'xrow{tbi}', xdst[tok0:tok0 + nr, :], xrow[tbi][0:nr, :], r=[('xrow', tbi)])

                for (s, nv) in ftiles:
                    c2_tile(s, nv)
                P.barrier()
                P.emit()
        print("bass ops recorded:", P.nops)
    return nc


def _host_layout(inputs, depth=DEPTH):
    f = lambda a: np.ascontiguousarray(np.asarray(a, dtype=np.float32))
    cw = f(inputs['conv_dw_w'])
    p_cw = np.ascontiguousarray(cw.reshape(depth, CW, 8, 128).transpose(0, 3, 2, 1))
    cvs = np.stack([f(inputs['conv_dw_b']), f(inputs['conv_ln_g']), f(inputs['conv_ln_b'])], axis=1)
    p_cv = np.ascontiguousarray(cvs.reshape(depth, 3, 8, 128).transpose(0, 3, 1, 2))
    p_hg = np.ascontiguousarray(f(inputs['mlstm_head_g']).reshape(depth, 16, 128).transpose(0, 2, 1))
    fwb = np.concatenate([f(inputs['ffn_dw_w']), f(inputs['ffn_dw_b'])[:, None, :]], axis=1)
    p_fw = np.ascontiguousarray(fwb.reshape(depth, 4, NF, 128).transpose(0, 3, 2, 1))
    k = np.arange(128)
    triU = (k[:, None] <= k[None, :]).astype(np.float32)
    triL = (k[:, None] >= k[None, :]).astype(np.float32)
    c_tri = np.stack([triU, triL, (1 - triU) * NEG, (1 - triL) * NEG]).astype(np.float32)
    common = dict(
        w_in=f(inputs['w_in']), w_conv_out=f(inputs['w_conv_out']), w_mlstm_out=f(inputs['w_mlstm_out']),
        w_out=f(inputs['w_out']), w_up=f(inputs['w_up']), w_down=f(inputs['w_down']),
        norm_mix_g=f(inputs['norm_mix_g']), norm_ffn_g=f(inputs['norm_ffn_g']), norm_final_g=f(inputs['norm_final_g']),
        b_gates=f(inputs['b_gates']), p_cw=p_cw, p_cv=p_cv, p_hg=p_hg, p_fw=p_fw, c_tri=c_tri)
    return common


def run_segments(segs, NCH, inputs, chains=None, depth=DEPTH):
    NT = NCH * 128
    common = _host_layout(inputs, depth)
    if chains is None:
        chains = [[i] for i in range(8)]
    sel = np.zeros((8, 4, 8), np.float32)
    for ch in chains:
        for i, c in enumerate(ch):
            if i >= 1:
                sel[c, 0, ch[i - 1]] = 1.0
            if i + 1 < len(ch):
                sel[c, 1, ch[i + 1]] = 1.0
                assert segs[c].shape[0] == NT
            if i >= 2:
                sel[c, 2, ch[i - 2]] = 1.0
            if i + 2 < len(ch):
                sel[c, 3, ch[i + 2]] = 1.0
    in_maps = []
    for c, sgm in enumerate(segs):
        x = np.zeros((NT, D), np.float32)
        m = np.zeros((NT,), np.float32)
        if sgm is not None:
            x[:sgm.shape[0]] = sgm
            m[:sgm.shape[0]] = 1.0
        dct = dict(common)
        dct['x'] = x
        dct['msk'] = np.ascontiguousarray(m.reshape(NCH, 128).T)
        dct['sel'] = np.ascontiguousarray(np.broadcast_to(sel[c][None], (128, 4, 8)))
        in_maps.append(dct)
    nc = build_program(NCH, depth)
    res = run_bass_kernel_spmd(nc, in_maps, core_ids=list(range(8)))
    return [r["y"] for r in res.results]


NCH_CORE = 43


def kernel(x_prompt, x_sample, **params):
    x_prompt = np.asarray(x_prompt, dtype=np.float32)
    x_sample = np.asarray(x_sample, dtype=np.float32)
    NT = NCH_CORE * 128
    S = x_sample.shape[1]
    cuts = [0, NT, 2 * NT, S]
    segs = [x_sample[b, cuts[i]:cuts[i + 1]] for b in range(2) for i in range(3)] + [x_prompt[0], x_prompt[1]]
    chains = [[0, 1, 2], [3, 4, 5], [6], [7]]
    outs = run_segments(segs, NCH_CORE, params, chains)
    y_sample = np.stack([np.concatenate([outs[b * 3 + i][:cuts[i + 1] - cuts[i]] for i in range(3)], axis=0)
                         for b in range(2)]).astype(np.float32)
    Pn = x_prompt.shape[1]
    y_prompt = np.stack([outs[6][:Pn], outs[7][:Pn]]).astype(np.float32)
    return (y_prompt, y_sample)
```

```python
import numpy as np
from contextlib import ExitStack

import concourse.bass as bass
import concourse.mybir as mybir
from concourse.bass_utils import run_bass_kernel_spmd

F32 = mybir.dt.float32
BF16 = mybir.dt.bfloat16
AF = mybir.ActivationFunctionType
ALU = mybir.AluOpType

D = 2048
KC = 16
CONV = 1024
CW = 31
HEADS = 4
DK = 256
DV = 512
DFF = 5504
NF = 43
DIN = 12304
DEPTH = 2
EPS = 1e-6
C_A, C_B, C_Q, C_K, C_V, C_O, C_G, C_M = 0, 1024, 2048, 3072, 4096, 6144, 8192, 8208
NEG = -30000.0

ENG = ['pe', 'act', 'dve', 'pool', 'sp']
BLK = {'pe': 'tensor', 'act': 'scalar', 'dve': 'vector', 'pool': 'gpsimd', 'sp': 'sync'}


class Prog:
    def __init__(self, nc, es):
        self.nc = nc
        self.es = es
        self.sem = {e: es.enter_context(nc.semaphore('s_' + e)) for e in ENG}
        self.cnt = {e: 0 for e in ENG}
        self.dpool = []
        self.dmap = {}
        self.ops = {e: [] for e in ENG}
        self.waited = {e: {} for e in ENG}
        self.res = {}
        self.xkeys = set()
        self.nops = 0

    def semh(self, k):
        if k in self.sem:
            return self.sem[k]
        return self.dpool[int(k[2:])][0]

    def _dsem(self, slot):
        if slot not in self.dmap:
            idx = len(self.dmap)
            if idx >= len(self.dpool):
                self.dpool.append([self.es.enter_context(self.nc.semaphore(f'dq{idx}')), 0])
            self.dmap[slot] = idx
        return self.dmap[slot]

    def _deps(self, r, w):
        evs = {}

        def add(k, v):
            if evs.get(k, 0) < v:
                evs[k] = v

        for key in r:
            st = self.res.get(key)
            if st:
                for k, v in st[0].items():
                    add(k, v)
        for key in w:
            st = self.res.get(key)
            if st:
                for k, v in st[0].items():
                    add(k, v)
                for k, v in st[1].items():
                    add(k, v)
        return evs

    def _commit(self, ev, r, w):
        k, v = ev
        for key in r:
            st = self.res.setdefault(key, [{}, {}])
            if st[1].get(k, 0) < v:
                st[1][k] = v
        for key in w:
            st = self.res.setdefault(key, [{}, {}])
            if st[0].get(k, 0) < v:
                st[0][k] = v

    def _waits(self, e, evs, skip_self_pe=True):
        waits = []
        for k, v in evs.items():
            if skip_self_pe and e == 'pe' and k == 'pe':
                continue
            if self.waited[e].get(k, 0) >= v:
                continue
            self.waited[e][k] = v
            waits.append((k, v))
        return waits

    def op(self, e, fns, r=(), w=()):
        if isinstance(fns, tuple):
            fns = [fns]
        xr = [k for k in r if k in self.xkeys]
        if xr:
            w = list(w) + xr
        waits = self._waits(e, self._deps(r, w))
        self.cnt[e] += 1
        ev = (e, self.cnt[e])
        self.ops[e].append((waits, fns, (e, 1)))
        self._commit(ev, r, w)
        self.nops += len(fns)

    def dma(self, e, slot, out, in_, r=(), w=()):
        waits = self._waits(e, self._deps(r, w), skip_self_pe=False)
        idx = self._dsem(slot)
        k = f'd:{idx}'
        self.dpool[idx][1] += 16
        ev = (k, self.dpool[idx][1])
        self.ops[e].append((waits, [('dma_start', dict(out=out, in_=in_))], (k, 16)))
        self._commit(ev, r, w)
        self.nops += 1

    def coll(self, src, dst):
        e = 'pool'
        idx = self._dsem('cc')
        k = f'd:{idx}'
        self.dpool[idx][1] += 1
        self.ops[e].append(([], [('collective_compute', dict(kind="AllGather", op=ALU.bypass,
                                                               replica_groups=[list(range(8))], ins=[src], outs=[dst]))],
                            (k, 1)))
        self.nops += 1

    def barrier(self):
        targets = {e: self.cnt[e] for e in ENG}
        for i, (h, c) in enumerate(self.dpool):
            targets[f'd:{i}'] = c
        for e in ENG:
            waits = []
            for k, v in targets.items():
                if v > 0 and self.waited[e].get(k, 0) < v:
                    self.waited[e][k] = v
                    waits.append((k, v))
            if waits:
                self.ops[e].append((waits, [], None))
        self.res = {}

    def emit(self):
        nc = self.nc
        with nc.Block() as block:
            for e in ENG:
                ops = self.ops[e]
                if not ops:
                    continue

                def body(eng, ops=ops):
                    for waits, fns, inc in ops:
                        for k, v in waits:
                            eng.wait_ge(self.semh(k), v)
                        ins = None
                        for (nm, kw) in fns:
                            ins = getattr(eng, nm)(**kw)
                        if inc is not None and ins is not None:
                            ins.then_inc(self.semh(inc[0]), inc[1])

                getattr(block, BLK[e])(body)
                self.ops[e] = []
        self.dmap = {}


def I(name, **kw):
    return (name, kw)


class Ring:
    def __init__(self, P, slots, name='w'):
        self.P = P
        self.slots = slots
        self.i = 0
        self.name = name

    def load(self, parts):
        s = self.i % len(self.slots)
        self.i += 1
        slot = self.slots[s]
        key = (self.name, s)
        for (src, nk, c0, ncols) in parts:
            self.P.dma('pool', f'{self.name}{s}', slot[:, 0:nk, c0:c0 + ncols],
                       src.rearrange("(kc p) c -> p kc c", p=128), w=[key])
        return slot, key


def tiles_of(nch, per=4):
    out = []
    c = 0
    while c < nch:
        n = min(per, nch - c)
        out.append((c, n))
        c += n
    return out


def build_program(NCH, depth=DEPTH):
    NT = NCH * 128
    nc = bass.Bass("TRN2", target_bir_lowering=False)

    def din(name, shape, dt=F32):
        return nc.dram_tensor(name, list(shape), dt, kind="ExternalInput").ap()

    def dscr(name, shape, dt=F32):
        return nc.dram_tensor(name, list(shape), dt).ap()

    x_in = din("x", [NT, D])
    msk = din("msk", [128, NCH])
    w_in = din("w_in", [depth, D, DIN])
    w_conv_out = din("w_conv_out", [depth, CONV, D])
    w_mlstm_out = din("w_mlstm_out", [depth, D, D])
    w_out = din("w_out", [depth, D, D])
    w_up = din("w_up", [depth, D, 2 * DFF])
    w_down = din("w_down", [depth, DFF, D])
    norm_mix_g = din("norm_mix_g", [depth, D])
    norm_ffn_g = din("norm_ffn_g", [depth, D])
    norm_final_g = din("norm_final_g", [D])
    b_gates = din("b_gates", [depth, 16])
    p_cw = din("p_cw", [depth, 128, 8, CW])
    p_cv = din("p_cv", [depth, 128, 3, 8])
    p_hg = din("p_hg", [depth, 128, 16])
    p_fw = din("p_fw", [depth, 128, NF, 4])
    c_tri = din("c_tri", [4, 128, 128])
    sel_in = din("sel", [128, 4, 8])
    y_out = nc.dram_tensor("y", [NT, D], F32, kind="ExternalOutput").ap()

    uT = dscr("uT", [CONV, NT + 32])
    qTc = dscr("qTc", [NCH, 128, 1024], BF16)
    kTc = dscr("kTc", [NCH, 128, 1024], BF16)
    ktok = dscr("ktok", [NT, 1024], BF16)
    vtok = dscr("vtok", [NT, 2048], BF16)
    gts = dscr("gts", [NT, 16])
    ogT = dscr("ogT", [D, NT], BF16)
    gmT = dscr("gmT", [2 * D, NT], BF16)
    mixT = dscr("mixT", [D, NT])
    hfT = dscr("hfT", [D, NT])
    hbT = dscr("hbT", [D, NT])
    xmid = dscr("xmid", [NT + 2, D])
    SW = 4096 + 8 + 4
    pkU_t = nc.dram_tensor("pkU", [CONV, 32], F32)
    gU_t = nc.dram_tensor("gU", [8 * CONV, 32], F32)
    pkS_t = nc.dram_tensor("pkS", [128, 2 * SW], F32)
    gS_t = nc.dram_tensor("gS", [8 * 128, 2 * SW], F32)
    pkX_t = nc.dram_tensor("pkX", [2, D], F32)
    gX_t = nc.dram_tensor("gX", [16, D], F32)
    pkU, gU, pkS, gS, pkX, gX = (t.ap() for t in (pkU_t, gU_t, pkS_t, gS_t, pkX_t, gX_t))
    x1 = dscr("x1", [NT, D])

    def fm(ap):
        return ap.rearrange("(j p) t -> p j t", p=128)

    es = ExitStack()
    with es:
        P = Prog(nc, es)
        P.xkeys.update(['gps', 'pmean', 'psq', 'pmisc', 'pBu', 'pBm', 'psT', 'pden', 'pdC', 'pms', 'tp'])
        P.xkeys.update([('mm', i) for i in range(8)] + [('tp', i) for i in range(4)] + [('pnum', i) for i in range(4)]
                       + [('gbk', i) for i in range(8)])

        uid = [0]

        def sb(st, name, shape, dt):
            uid[0] += 1
            return st.enter_context(nc.sbuf_tensor(f"{name}_{uid[0]}", list(shape), dt))

        def ps(st, name, shape, dt=F32):
            uid[0] += 1
            return st.enter_context(nc.psum_tensor(f"{name}_{uid[0]}", list(shape), dt))

        identf = sb(es, "identf", [128, 128], F32)
        identb = sb(es, "identb", [128, 128], BF16)
        onesf = sb(es, "onesf", [128, 128], F32)
        ones1k = sb(es, "ones1k", [128, 128], F32)
        ones512 = sb(es, "ones512", [128, 128], F32)
        onesb = sb(es, "onesb", [128, 128], BF16)
        tri = sb(es, "tri", [128, 4, 128], F32)
        zer = sb(es, "zer", [128, 2048], F32)
        mcol = sb(es, "mcol", [128, NCH], F32)
        epsc = sb(es, "epsc", [128, 1], F32)
        sel = sb(es, "sel", [128, 4, 8], F32)

        P.op('pool', I('memset', ap=identf[:], constant=1.0), w=['identf'])
        P.op('pool', I('affine_select', out=identf[:], in_=identf[:], pattern=[[-1, 128]],
                       compare_op=ALU.is_equal, fill=0.0, base=0, channel_multiplier=1),
             r=['identf'], w=['identf'])
        P.op('dve', I('tensor_copy', out=identb[:], in_=identf[:]), r=['identf'], w=['identb'])
        P.op('dve', I('memset', ap=onesf[:], constant=1.0), w=['onesf'])
        P.op('dve', I('memset', ap=ones1k[:], constant=1.0 / 1024), w=['ones1k'])
        P.op('dve', I('memset', ap=ones512[:], constant=1.0 / 512), w=['ones512'])
        P.op('dve', I('memset', ap=onesb[:], constant=1.0), w=['onesb'])
        P.op('dve', I('memset', ap=zer[:], constant=0.0), w=['zer'])
        P.op('dve', I('memset', ap=epsc[:], constant=EPS), w=['epsc'])
        P.dma('sp', 'c0', tri[:], c_tri.rearrange("a p t -> p a t"), w=['tri'])
        P.dma('sp', 'c1', mcol[:], msk, w=['mcol'])
        P.dma('sp', 'c2', sel[:], sel_in, w=['sel'])
        P.dma('sp', 'z0', fm(uT[:, 0:16]), zer[:, 0:128].rearrange("p (j t) -> p j t", j=8), r=['zer'])
        P.dma('sp', 'z1', fm(uT[:, NT + 16:NT + 32]), zer[:, 0:128].rearrange("p (j t) -> p j t", j=8), r=['zer'])
        P.dma('sp', 'z2', xmid[0:1, :], zer[0:1, :], r=['zer'])
        P.dma('sp', 'z3', xmid[NT + 1:NT + 2, :], zer[0:1, :], r=['zer'])
        P.barrier()
        P.emit()

        atiles = tiles_of(NCH, 4)

        def rmsnorm_rows(xt, nr, xk, gvec, gk, outt, outk, stat, sqj):
            P.op('dve', I('memset', ap=stat[:, 0:1], constant=0.0), w=['stat0'])
            P.op('act', I('activation', out=sqj[0:nr, :], in_=xt[0:nr, :], func=AF.Square, accum_out=stat[0:nr, 0:1]),
                 r=[xk, 'stat0'], w=['sqj', 'stat0'])
            P.op('act', I('activation', out=stat[0:nr, 1:2], in_=stat[0:nr, 0:1], func=AF.Sqrt, scale=1.0 / D,
                          bias=epsc[0:nr, :]), r=['stat0', 'epsc'], w=['stat1'])
            P.op('dve', I('reciprocal', out=stat[0:nr, 2:3], in_=stat[0:nr, 1:2]), r=['stat1'], w=['stat2'])
            P.op('dve', I('scalar_tensor_tensor', out=outt[0:nr, :], in0=xt[0:nr, :], scalar=stat[0:nr, 2:3],
                          in1=gvec[0:nr, :], op0=ALU.mult, op1=ALU.mult),
                 r=[xk, 'stat2', gk], w=[outk])

        def masked_sum(dst, dkey, srcs, k):
            for c in range(8):
                ap, key = srcs[c]
                if c == 0:
                    P.op('dve', I('tensor_scalar', out=dst, in0=ap, scalar1=sel[:, k, 0:1], scalar2=None, op0=ALU.mult),
                         r=[key, 'sel'], w=[dkey])
                else:
                    P.op('dve', I('scalar_tensor_tensor', out=dst, in0=ap, scalar=sel[:, k, c:c + 1], in1=dst,
                                  op0=ALU.mult, op1=ALU.add), r=[key, 'sel', dkey], w=[dkey])

        def exchange_u():
            with ExitStack() as st:
                G = sb(st, "xG", [128, 8, 8, 32], F32)
                hl = sb(st, "xhl", [128, 8, 16], F32)
                hr = sb(st, "xhr", [128, 8, 16], F32)
                P.dma('sp', 'x0', pkU[:, 0:16], uT[:, 16:32])
                P.dma('sp', 'x1', pkU[:, 16:32], uT[:, NT:NT + 16])
                P.barrier()
                P.coll(pkU_t.ap().opt(), gU_t.ap().opt())
                P.barrier()
                for c in range(8):
                    P.dma('sp', f'xg{c}', G[:, c], fm(gU[c * CONV:(c + 1) * CONV, :]), w=[('G', c)])
                masked_sum(hl[:], 'hl', [(G[:, c, :, 16:32], ('G', c)) for c in range(8)], 0)
                masked_sum(hr[:], 'hr', [(G[:, c, :, 0:16], ('G', c)) for c in range(8)], 1)
                P.dma('sp', 'x2', fm(uT[:, 0:16]), hl[:], r=['hl'])
                P.dma('sp', 'x3', fm(uT[:, NT + 16:NT + 32]), hr[:], r=['hr'])
                P.barrier()
                P.emit()

        def exchange_x():
            with ExitStack() as st:
                G = sb(st, "xGX", [128, 8, 2, 16], F32)
                hl = sb(st, "xxl", [128, 16], F32)
                hr = sb(st, "xxr", [128, 16], F32)
                P.dma('sp', 'x0', pkX[0:1, :], xmid[1:2, :])
                P.dma('sp', 'x1', pkX[1:2, :], xmid[NT:NT + 1, :])
                P.barrier()
                P.coll(pkX_t.ap().opt(), gX_t.ap().opt())
                P.barrier()
                for c in range(8):
                    P.dma('sp', f'xg{c}', G[:, c], gX[2 * c:2 * c + 2, :].rearrange("r (p f) -> p r f", p=128), w=[('G', c)])
                masked_sum(hl[:], 'hl', [(G[:, c, 1, :], ('G', c)) for c in range(8)], 0)
                masked_sum(hr[:], 'hr', [(G[:, c, 0, :], ('G', c)) for c in range(8)], 1)
                P.dma('sp', 'x2', xmid[0:1, :].rearrange("o (p f) -> p (o f)", p=128), hl[:], r=['hl'])
                P.dma('sp', 'x3', xmid[NT + 1:NT + 2, :].rearrange("o (p f) -> p (o f)", p=128), hr[:], r=['hr'])
                P.barrier()
                P.emit()

        for l in range(depth):
            xsrc = x_in if l == 0 else x1
            xdst = x1 if l < depth - 1 else y_out
            last = (l == depth - 1)
            W_in = w_in[l]
            W_in32 = w_in[l]

            with ExitStack() as st:
                gB = sb(st, "gB", [128, D], F32)
                wg = sb(st, "wg", [128, KC, 16], BF16)
                bg = sb(st, "bg", [128, 16], F32)
                xin = [sb(st, f"xin{i}", [128, D], F32) for i in range(2)]
                xnb = sb(st, "xnb", [128, D], BF16)
                sqj = sb(st, "sqj", [128, D], BF16)
                stat = sb(st, "stat", [128, 4], F32)
                xnT = sb(st, "xnT", [128, KC, 512], BF16)
                sig = sb(st, "sig", [128, 4, 512], F32)
                ust = [sb(st, f"ust{i}", [128, 4, 512], F32) for i in range(2)]
                bst = [sb(st, f"bst{i}", [128, 4, 512], BF16) for i in range(2)]
                qst = sb(st, "qst", [128, 4, 8, 128], BF16)
                kst = sb(st, "kst", [128, 4, 8, 128], BF16)
                ktk = sb(st, "ktk", [128, 4, 1024], BF16)
                vst = sb(st, "vst", [128, 4, 2048], BF16)
                gst = sb(st, "gst", [128, 4, 16], F32)
                gtmp = sb(st, "gtmp", [128, 16], F32)
                slots = [sb(st, f"ring{i}", [128, KC, 512], BF16) for i in range(4)]
                tp = [ps(st, f"tp{i}", [128, 1024], BF16) for i in range(2)]
                mm = [ps(st, f"mm{i}", [128, 512], F32) for i in range(5)]
                gps = ps(st, "gps", [128, 16], F32)
                ring = Ring(P, slots)
                cnt = dict(mm=0, tp=0, bst=0)

                def next_mm():
                    i = cnt['mm'] % len(mm)
                    cnt['mm'] += 1
                    return mm[i], ('mm', i)

                def next_tp():
                    i = cnt['tp'] % len(tp)
                    cnt['tp'] += 1
                    return tp[i], ('tp', i)

                P.dma('sp', 'gB', gB[:], norm_mix_g[l].partition_broadcast(128), w=['gvec'])
                P.dma('sp', 'bg', bg[:], b_gates[l].partition_broadcast(128), w=['bg'])
                P.dma('pool', 'wg', wg[:], W_in32[:, C_G:C_G + 16].rearrange("(kc p) c -> p kc c", p=128), w=['wg'])

                def load_x(chunk):
                    b = chunk % 2
                    P.dma('sp', f'xin{b}', xin[b][:], xsrc[chunk * 128:(chunk + 1) * 128, :], w=[('xin', b)])

                def norm_chunk(chunk, ci):
                    b = chunk % 2
                    rmsnorm_rows(xin[b], 128, ('xin', b), gB, 'gvec', xnb, 'xnb', stat, sqj)
                    for g in range(2):
                        tpt, tpk = next_tp()
                        P.op('pe', [I('transpose', out=tpt[:, k * 128:(k + 1) * 128],
                                      in_=xnb[:, (g * 8 + k) * 128:(g * 8 + k + 1) * 128], identity=identb[:])
                                    for k in range(8)], r=['xnb', 'identb'], w=[tpk])
                        dst = xnT[:, g * 8:(g + 1) * 8, ci * 128:(ci + 1) * 128]
                        src = tpt[:].rearrange("p (k t) -> p k t", k=8)
                        if g == 0:
                            P.op('act', I('copy', out=dst, in_=src), r=[tpk], w=[('xnT', ci)])
                        else:
                            P.op('dve', I('tensor_copy', out=dst, in_=src), r=[tpk], w=[('xnT', ci)])

                def a1_tile(ti, c0, n):
                    nt = n * 128
                    t0 = c0 * 128
                    xk = [('xnT', ci) for ci in range(n)]
                    for ci in range(n):
                        if c0 + ci + 1 < NCH:
                            load_x(c0 + ci + 1)
                        norm_chunk(c0 + ci, ci)

                    def ws_block(col0, evac):
                        slot, skey = ring.load([(W_in[:, col0:col0 + 512], KC, 0, 512)])
                        for ct in range(4):
                            pt, pk = next_mm()
                            P.op('pe', [I('matmul', out=pt[:, 0:nt], lhsT=slot[:, kc, ct * 128:(ct + 1) * 128],
                                          rhs=xnT[:, kc, 0:nt], start=(kc == 0), stop=(kc == KC - 1))
                                        for kc in range(KC)], r=[skey] + xk, w=[pk])
                            evac(ct, pt, pk)

                    for i in range(2):
                        def ev_b(ct, pt, pk):
                            P.op('act', I('activation', out=sig[:, ct, 0:nt], in_=pt[:, 0:nt], func=AF.Sigmoid),
                                 r=[pk], w=[('sig', ct)])
                        ws_block(C_B + i * 512, ev_b)
                        us = ust[i]

                        def ev_a(ct, pt, pk):
                            P.op('dve', I('tensor_tensor', out=us[:, ct, 0:nt], in0=pt[:, 0:nt], in1=sig[:, ct, 0:nt],
                                          op=ALU.mult), r=[pk, ('sig', ct)], w=[('ust', i)])
                        ws_block(C_A + i * 512, ev_a)
                        P.dma('sp', f'ust{i}', fm(uT[i * 512:(i + 1) * 512, 16 + t0:16 + t0 + nt]), us[:, :, 0:nt],
                              r=[('ust', i)])
                    for (cbase, stg, sname, dst, scl) in ((C_Q, qst, 'qst', qTc, DK ** -0.5), (C_K, kst, 'kst', kTc, 1.0)):
                        for i in range(2):
                            def ev_q(ct, pt, pk):
                                P.op('act', I('mul', out=stg[:, 0:n, i * 4 + ct, :],
                                              in_=pt[:, 0:nt].rearrange("p (c t) -> p c t", c=n), mul=scl),
                                     r=[pk], w=[sname])
                            ws_block(cbase + i * 512, ev_q)
                        P.dma('sp', sname, dst[c0:c0 + n].rearrange("c p f -> p c f"),
                              stg[:, 0:n].rearrange("p c j t -> p c (j t)"), r=[sname])
                    for ci in range(n):
                        tpt, tpk = next_tp()
                        P.op('pe', [I('transpose', out=tpt[:, j * 128:(j + 1) * 128], in_=kst[:, ci, j, :], identity=identb[:])
                                    for j in range(8)], r=['kst', 'identb'], w=[tpk])
                        P.op('dve', I('tensor_copy', out=ktk[:, ci, :], in_=tpt[:]), r=[tpk], w=['ktk'])
                    P.dma('sp', 'ktk', ktok[t0:t0 + nt, :].rearrange("(c p) f -> p c f", p=128), ktk[:, 0:n, :], r=['ktk'])
                    for i in range(4):
                        slot, skey = ring.load([(W_in[:, C_V + i * 512:C_V + (i + 1) * 512], KC, 0, 512)])
                        for ci in range(n):
                            pt, pk = next_mm()
                            P.op('pe', [I('matmul', out=pt[:, :], lhsT=xnT[:, kc, ci * 128:(ci + 1) * 128], rhs=slot[:, kc, :],
                                          start=(kc == 0), stop=(kc == KC - 1)) for kc in range(KC)],
                                 r=[skey, ('xnT', ci)], w=[pk])
                            if (i + ci) % 2 == 0:
                                P.op('act', I('copy', out=vst[:, ci, i * 512:(i + 1) * 512], in_=pt[:, :]), r=[pk], w=['vst'])
                            else:
                                P.op('dve', I('tensor_copy', out=vst[:, ci, i * 512:(i + 1) * 512], in_=pt[:, :]), r=[pk], w=['vst'])
                    P.dma('sp', 'vst', vtok[t0:t0 + nt, :].rearrange("(c p) f -> p c f", p=128), vst[:, 0:n, :], r=['vst'])
                    for ci in range(n):
                        P.op('pe', [I('matmul', out=gps[:, :], lhsT=xnT[:, kc, ci * 128:(ci + 1) * 128], rhs=wg[:, kc, :],
                                      start=(kc == 0), stop=(kc == KC - 1)) for kc in range(KC)],
                             r=['wg', ('xnT', ci)], w=['gps'])
                        P.op('dve', I('tensor_tensor', out=gst[:, ci, :], in0=gps[:, :], in1=bg[:], op=ALU.add),
                             r=['gps', 'bg'], w=['gst'])
                        gv = gst[:, ci, :].rearrange("p (a b) -> p a b", a=2)[:, :, 4:8]
                        tv = gtmp[:, :].rearrange("p (a b) -> p a b", a=2)[:, :, 4:8]
                        P.op('act', I('activation', out=tv, in_=gv, func=AF.Exp, scale=-1.0), r=['gst'], w=['gtmp'])
                        P.op('act', I('activation', out=tv, in_=tv, func=AF.Ln, bias=1.0), r=['gtmp'], w=['gtmp'])
                        P.op('dve', I('tensor_scalar', out=gv, in0=tv, scalar1=-1.0, scalar2=None, op0=ALU.mult),
                             r=['gtmp'], w=['gst'])
                    P.dma('sp', 'gst', gts[t0:t0 + nt, :].rearrange("(c p) g -> p c g", p=128), gst[:, 0:n, :], r=['gst'])
                    for (cbase, nblk, dst) in ((C_O, 4, ogT), (C_M, 8, gmT)):
                        for i in range(nblk):
                            bb = cnt['bst'] % 2
                            cnt['bst'] += 1
                            bs = bst[bb]

                            def ev_s(ct, pt, pk):
                                P.op('act', I('activation', out=bs[:, ct, 0:nt], in_=pt[:, 0:nt], func=AF.Sigmoid),
                                     r=[pk], w=[('bst', bb)])
                            ws_block(cbase + i * 512, ev_s)
                            P.dma('sp', f'bst{bb}', fm(dst[i * 512:(i + 1) * 512, t0:t0 + nt]), bs[:, :, 0:nt], r=[('bst', bb)])

                load_x(0)
                for ti, (c0, n) in enumerate(atiles):
                    a1_tile(ti, c0, n)
                P.barrier()
                P.emit()
            exchange_u()

            with ExitStack() as st:
                cw = sb(st, "cw", [128, 8, CW], F32)
                cv = sb(st, "cv", [128, 3, 8], F32)
                uin = [sb(st, f"uin{i}", [128, 8, 544], F32) for i in range(2)]
                acc = sb(st, "acc", [128, 8, 512], F32)
                ub16 = sb(st, "ub16", [128, 8, 544], BF16)
                dg = [sb(st, f"dg{i}", [128, CW, 128], BF16) for i in range(2)]
                ysq = [sb(st, f"ysq{i}", [128, 512], F32) for i in range(2)]
                mean = sb(st, "mean", [128, 512], F32)
                var = sb(st, "var", [128, 512], F32)
                rstd = sb(st, "rstd", [128, 512], F32)
                zt = [sb(st, f"zt{i}", [128, 512], F32) for i in range(2)]
                sT = sb(st, "sT", [128, 8, 512], BF16)
                gA = [sb(st, f"gA{i}", [128, 4, 512], BF16) for i in range(2)]
                mst = [sb(st, f"mst{i}", [128, 4, 512], F32) for i in range(2)]
                slots = [sb(st, f"ring{i}", [128, KC, 512], BF16) for i in range(3)]
                pmean = ps(st, "pmean", [128, 512])
                psq = ps(st, "psq", [128, 512])
                mm = [ps(st, f"mm{i}", [128, 512], F32) for i in range(4)]
                ring = Ring(P, slots)
                cnt = dict(mm=0, g=0)

                def next_mm():
                    i = cnt['mm'] % len(mm)
                    cnt['mm'] += 1
                    return mm[i], ('mm', i)

                P.dma('sp', 'cw', cw[:], p_cw[l], w=['cw'])
                P.dma('sp', 'cv', cv[:], p_cv[l], w=['cv'])

                def load_u(ti):
                    c0, n = atiles[ti]
                    nt = n * 128
                    t0 = c0 * 128
                    b = ti % 2
                    P.dma('sp', f'uin{b}', uin[b][:, :, 0:nt + 32], fm(uT[:, t0:t0 + nt + 32]), w=[('uin', b)])

                def a2_tile(ti, c0, n):
                    nt = n * 128
                    t0 = c0 * 128
                    b = ti % 2
                    if ti + 1 < len(atiles):
                        load_u(ti + 1)
                    ub = uin[b]
                    for j in range(8):
                        P.op('act', I('copy', out=ub16[:, j, 0:nt + 32], in_=ub[:, j, 0:nt + 32]), r=[('uin', b)], w=[('ub16', j)])
                    for j in range(8):
                        db = j % 2
                        P.op('dve', I('tensor_tensor', out=dg[db][:], in0=identb[:].unsqueeze(1).broadcast_to([128, CW, 128]),
                                      in1=cw[:, j, :].unsqueeze(2).broadcast_to([128, CW, 128]), op=ALU.mult),
                             r=['identb', 'cw'], w=[('dg', db)])
                        pt, pk = next_mm()
                        P.op('pe', [I('matmul', out=pt[:, 0:nt], lhsT=dg[db][:, wtap, :], rhs=ub16[:, j, 1 + wtap:1 + wtap + nt],
                                      start=(wtap == 0), stop=(wtap == CW - 1)) for wtap in range(CW)],
                             r=[('dg', db), ('ub16', j)], w=[pk])
                        P.op('act', I('activation', out=acc[:, j, 0:nt], in_=pt[:, 0:nt], func=AF.Identity, bias=cv[:, 0, j:j + 1]),
                             r=[pk, 'cv'], w=[('acc', j)])
                    for j in range(8):
                        yb = j % 2
                        P.op('act', I('activation', out=ysq[yb][:, 0:nt], in_=acc[:, j, 0:nt], func=AF.Square),
                             r=[('acc', j)], w=[('ysq', yb)])
                        P.op('pe', I('matmul', out=pmean[:, 0:nt], lhsT=ones1k[:], rhs=acc[:, j, 0:nt],
                                     start=(j == 0), stop=(j == 7)), r=[('acc', j), 'ones1k'], w=['pmean'])
                        P.op('pe', I('matmul', out=psq[:, 0:nt], lhsT=ones1k[:], rhs=ysq[yb][:, 0:nt],
                                     start=(j == 0), stop=(j == 7)), r=[('ysq', yb), 'ones1k'], w=['psq'])
                    P.op('act', I('copy', out=mean[:, 0:nt], in_=pmean[:, 0:nt]), r=['pmean'], w=['mean'])
                    P.op('dve', I('tensor_tensor', out=var[:, 0:nt], in0=mean[:, 0:nt], in1=mean[:, 0:nt], op=ALU.mult),
                         r=['mean'], w=['var'])
                    P.op('dve', I('tensor_tensor', out=var[:, 0:nt], in0=psq[:, 0:nt], in1=var[:, 0:nt], op=ALU.subtract),
                         r=['psq', 'var'], w=['var'])
                    P.op('act', I('activation', out=var[:, 0:nt], in_=var[:, 0:nt], func=AF.Sqrt, bias=epsc[:]),
                         r=['var', 'epsc'], w=['var'])
                    P.op('dve', I('reciprocal', out=rstd[:, 0:nt], in_=var[:, 0:nt]), r=['var'], w=['rstd'])
                    for j in range(8):
                        zb = j % 2
                        P.op('dve', I('tensor_tensor', out=zt[zb][:, 0:nt], in0=acc[:, j, 0:nt], in1=mean[:, 0:nt],
                                      op=ALU.subtract), r=[('acc', j), 'mean'], w=[('zt', zb)])
                        P.op('dve', I('tensor_tensor', out=zt[zb][:, 0:nt], in0=zt[zb][:, 0:nt], in1=rstd[:, 0:nt],
                                      op=ALU.mult), r=[('zt', zb), 'rstd'], w=[('zt', zb)])
                        P.op('act', I('activation', out=sT[:, j, 0:nt], in_=zt[zb][:, 0:nt], func=AF.Silu,
                                      scale=cv[:, 1, j:j + 1], bias=cv[:, 2, j:j + 1]),
                             r=[('zt', zb), 'cv'], w=['sT'])
                    for blk in range(4):
                        gb = cnt['g'] % 2
                        cnt['g'] += 1
                        P.dma('sp', f'gA{gb}', gA[gb][:, :, 0:nt], fm(gmT[blk * 512:(blk + 1) * 512, t0:t0 + nt]),
                              w=[('gA', gb)])
                        slot, skey = ring.load([(w_conv_out[l][:, blk * 512:(blk + 1) * 512], 8, 0, 512)])
                        for ct in range(4):
                            pt, pk = next_mm()
                            P.op('pe', [I('matmul', out=pt[:, 0:nt], lhsT=slot[:, kc, ct * 128:(ct + 1) * 128],
                                          rhs=sT[:, kc, 0:nt], start=(kc == 0), stop=(kc == 7)) for kc in range(8)],
                                 r=[skey, 'sT'], w=[pk])
                            P.op('dve', I('tensor_tensor', out=mst[gb][:, ct, 0:nt], in0=pt[:, 0:nt], in1=gA[gb][:, ct, 0:nt],
                                          op=ALU.mult), r=[pk, ('gA', gb)], w=[('mst', gb)])
                        P.dma('sp', f'mst{gb}', fm(mixT[blk * 512:(blk + 1) * 512, t0:t0 + nt]), mst[gb][:, :, 0:nt],
                              r=[('mst', gb)])

                load_u(0)
                for ti, (c0, n) in enumerate(atiles):
                    a2_tile(ti, c0, n)
                P.barrier()
                P.emit()

            with ExitStack() as st:
                S = {}
                for d in range(2):
                    S[d] = dict(
                        C=sb(st, f"C{d}", [128, 8, 512], F32),
                        Cb=sb(st, f"Cb{d}", [128, 8, 512], BF16),
                        n=sb(st, f"n{d}", [128, 8], F32),
                        nB=sb(st, f"nB{d}", [128, 8, 128], BF16),
                        qT=[sb(st, f"qT{d}{i}", [128, 8, 128], BF16) for i in range(2)],
                        kT=[sb(st, f"kT{d}{i}", [128, 8, 128], BF16) for i in range(2)],
                        kt=[sb(st, f"kt{d}{i}", [128, 1024], BF16) for i in range(2)],
                        vt=[sb(st, f"vt{d}{i}", [128, 2048], BF16) for i in range(2)],
                        g=[sb(st, f"g{d}{i}", [128, 16], F32) for i in range(2)],
                        lfB=sb(st, f"lfB{d}", [128, 4, 128], F32),
                        bias=sb(st, f"bias{d}", [128, 4], F32),
                        E=sb(st, f"E{d}", [128, 4, 128], F32),
                        Dm=sb(st, f"Dm{d}", [128, 4, 128], F32),
                        qt=sb(st, f"qt{d}", [128, 8, 128], BF16),
                        wk=sb(st, f"wk{d}", [128, 1024], BF16),
                        sw=sb(st, f"sw{d}", [128, 4, 128], BF16),
                        aden=sb(st, f"aden{d}", [128, 4, 128], F32),
                        rden=sb(st, f"rden{d}", [128, 4, 128], F32),
                        hst=[sb(st, f"hst{d}{i}", [128, 16, 128], F32) for i in range(2)],
                        ld=sb(st, f"ld{d}", [128, 4], F32),
                        pp=sb(st, f"pp{d}", [128, 4, 4], F32),
                    )
                XH = 2048 + 12
                xa1 = sb(st, "xa1", [128, XH], F32)
                xa2 = sb(st, "xa2", [128, XH], F32)
                xlb = [sb(st, f"xlb{i}", [128, XH], F32) for i in range(2)]
                xdec = sb(st, "xdec", [128, 4], F32)
                pmisc = ps(st, "pmisc", [128, 16])
                pBu = ps(st, "pBu", [128, 4, 128])
                pBm = ps(st, "pBm", [128, 4, 128])
                psT = ps(st, "psT", [128, 4, 128])
                pden = ps(st, "pden", [128, 4, 128])
                pnum = [ps(st, f"pnum{i}", [128, 4, 128]) for i in range(2)]
                pdC = ps(st, "pdC", [128, 512])
                cnt = dict(num=0)

                for d in range(2):
                    sd = S[d]
                    P.op('dve', I('memset', ap=sd['C'][:], constant=0.0), w=[('C', d, j) for j in range(8)])
                    P.op('dve', I('memset', ap=sd['Cb'][:], constant=0.0), w=[('Cb', d)])
                    P.op('dve', I('memset', ap=sd['n'][:], constant=0.0), w=[('n', d)])
                    P.op('dve', I('memset', ap=sd['nB'][:], constant=0.0), w=[('nB', d)])
                    P.op('dve', I('memset', ap=sd['ld'][:], constant=0.0), w=[('ld', d)])

                def scan_load(d, step, full=True):
                    c = step if d == 0 else NCH - 1 - step
                    b = step % 2
                    sd = S[d]
                    r0 = c * 128
                    if full:
                        P.dma('sp', f'sq{d}{b}', sd['qT'][b][:].rearrange("p j t -> p (j t)"), qTc[c], w=[('qT', d, b)])
                        P.dma('sp', f'sk{d}{b}', sd['kT'][b][:].rearrange("p j t -> p (j t)"), kTc[c], w=[('kT', d, b)])
                    P.dma('sp', f'st{d}{b}', sd['kt'][b][:], ktok[r0:r0 + 128, :], w=[('kt', d, b)])
                    P.dma('sp', f'sv{d}{b}', sd['vt'][b][:], vtok[r0:r0 + 128, :], w=[('vt', d, b)])
                    P.dma('sp', f'sg{d}{b}', sd['g'][b][:], gts[r0:r0 + 128, :], w=[('g', d, b)])

                def scan_step(d, step, full=True):
                    c = step if d == 0 else NCH - 1 - step
                    b = step % 2
                    sd = S[d]
                    go = 0 if d == 0 else 8
                    tl = 127 if d == 0 else 0
                    qT, kT, kt, vt, g = sd['qT'][b], sd['kT'][b], sd['kt'][b], sd['vt'][b], sd['g'][b]
                    kq, kk, kkt, kv, kg = ('qT', d, b), ('kT', d, b), ('kt', d, b), ('vt', d, b), ('g', d, b)
                    ig = g[:, go:go + 4]
                    lf = g[:, go + 4:go + 8]
                    TRI = tri[:, d, :]
                    NEGM = tri[:, 2 + d, :]
                    C, Cb, n, nB = sd['C'], sd['Cb'], sd['n'], sd['nB']
                    lfB, bias, E, Dm, qt, wk, sw, aden, rden = (sd[k] for k in ('lfB', 'bias', 'E', 'Dm', 'qt', 'wk', 'sw', 'aden', 'rden'))
                    hst = sd['hst'][b]
                    P.op('pe', I('matmul', out=pmisc[:, 0:4], lhsT=TRI, rhs=lf, start=True, stop=True),
                         r=[kg, 'tri'], w=['pmisc'])
                    if full:
                        P.op('dve', I('tensor_copy', out=lfB[:], in_=lf.unsqueeze(2).broadcast_to([128, 4, 128])),
                             r=[kg], w=[('lfB', d)])
                    P.op('dve', I('tensor_tensor', out=bias[:], in0=ig, in1=pmisc[:, 0:4], op=ALU.subtract),
                         r=[kg, 'pmisc'], w=[('bias', d)])
                    if not full:
                        pp = sd['pp']
                        P.op('pe', I('matmul', out=pmisc[:, 12:16], lhsT=onesf[:], rhs=lf, start=True, stop=True),
                             r=[kg, 'onesf'], w=['pmisc'])
                        P.op('dve', I('tensor_tensor', out=pp[:, 0, :], in0=bias[:], in1=pmisc[:, 12:16], op=ALU.add),
                             r=[('bias', d), 'pmisc'], w=[('pp0', d)])
                        P.op('dve', I('tensor_copy', out=pp[:, 2, :], in_=pmisc[:, 12:16]), r=['pmisc'], w=[('pp2', d)])
                        P.op('act', I('activation', out=pp[:, 1, :], in_=pp[:, 0, :], func=AF.Exp), r=[('pp0', d)], w=[('pp1', d)])
                        P.op('act', I('activation', out=pp[:, 3, :], in_=pp[:, 2, :], func=AF.Exp), r=[('pp2', d)], w=[('pp3', d)])
                        P.op('dve', I('tensor_tensor', out=sd['ld'][:], in0=sd['ld'][:], in1=pp[:, 2, :], op=ALU.add),
                             r=[('ld', d), ('pp2', d)], w=[('ld', d)])
                        yield
                        P.op('dve', I('tensor_tensor', out=wk[:].rearrange("p (h f) -> p h f", h=4),
                                      in0=kt[:].rearrange("p (h f) -> p h f", h=4),
                                      in1=pp[:, 1, :].unsqueeze(2).broadcast_to([128, 4, 256]), op=ALU.mult),
                             r=[kkt, ('pp1', d)], w=[('wk', d)])
                    else:
                        yield
                        fu = []
                        fmm = []
                        for h in range(4):
                            fu.append(I('matmul', out=pBu[:, h, :], lhsT=lfB[:, h, :], rhs=TRI, start=True, stop=True))
                            fmm.append(I('matmul', out=pBm[:, h, :], lhsT=lfB[:, h, :], rhs=TRI, start=True, stop=False))
                            fmm.append(I('matmul', out=pBm[:, h, :], lhsT=identf[:], rhs=NEGM, start=False, stop=True))
                        P.op('pe', fu, r=[('lfB', d), 'tri'], w=['pBu'])
                        P.op('pe', fmm, r=[('lfB', d), 'tri', 'identf'], w=['pBm'])
                        P.op('act', I('activation', out=E[:], in_=pBu[:], func=AF.Exp), r=['pBu'], w=[('E', d)])
                        for h in range(4):
                            P.op('act', I('activation', out=Dm[:, h, :], in_=pBm[:, h, :], func=AF.Exp, bias=bias[:, h:h + 1]),
                                 r=['pBm', ('bias', d)], w=[('Dm', d)])
                        yield
                    if full:
                        P.op('dve', I('tensor_tensor', out=wk[:].rearrange("p (h f) -> p h f", h=4),
                                      in0=kt[:].rearrange("p (h f) -> p h f", h=4),
                                      in1=Dm[:, :, tl:tl + 1].broadcast_to([128, 4, 256]), op=ALU.mult),
                             r=[kkt, ('Dm', d)], w=[('wk', d)])
                    if full:
                        P.op('dve', I('tensor_tensor', out=qt[:].rearrange("p (h c) t -> p h c t", h=4),
                                      in0=qT[:].rearrange("p (h c) t -> p h c t", h=4),
                                      in1=E[:].unsqueeze(2).broadcast_to([128, 4, 2, 128]), op=ALU.mult),
                             r=[kq, ('E', d)], w=[('qt', d)])
                        yield
                        P.op('pe', [I('matmul', out=psT[:, h, :], lhsT=kT[:, h * 2 + cc, :], rhs=qT[:, h * 2 + cc, :],
                                      start=(cc == 0), stop=(cc == 1)) for h in range(4) for cc in range(2)],
                             r=[kk, kq], w=['psT'])
                        P.op('dve', I('tensor_tensor', out=sw[:], in0=psT[:], in1=Dm[:], op=ALU.mult),
                             r=['psT', ('Dm', d)], w=[('sw', d)])
                        yield
                        fd = []
                        for h in range(4):
                            fd.append(I('matmul', out=pden[:, h, :], lhsT=onesb[:], rhs=sw[:, h, :], start=True, stop=False))
                            for cc in range(2):
                                fd.append(I('matmul', out=pden[:, h, :], lhsT=nB[:, h * 2 + cc, :], rhs=qt[:, h * 2 + cc, :],
                                            start=False, stop=(cc == 1)))
                        P.op('pe', fd, r=[('sw', d), ('qt', d), ('nB', d), 'onesb'], w=['pden'])
                        P.op('act', I('activation', out=aden[:], in_=pden[:], func=AF.Abs), r=['pden'], w=[('aden', d)])
                        yield
                        P.op('dve', I('tensor_scalar', out=aden[:], in0=aden[:], scalar1=1.0, scalar2=None, op0=ALU.max),
                             r=[('aden', d)], w=[('aden', d)])
                        P.op('dve', I('reciprocal', out=rden[:], in_=aden[:]), r=[('aden', d)], w=[('rden', d)])
                        yield
                        for h in range(4):
                            if h == 2:
                                yield
                            pi = cnt['num'] % 2
                            cnt['num'] += 1
                            pn = pnum[pi]
                            pnk = ('pnum', pi)
                            fn = []
                            for et in range(4):
                                fn.append(I('matmul', out=pn[:, et, :], lhsT=vt[:, h * 512 + et * 128:h * 512 + (et + 1) * 128],
                                            rhs=sw[:, h, :], start=True, stop=False))
                                for cc in range(2):
                                    fn.append(I('matmul', out=pn[:, et, :], lhsT=Cb[:, h * 2 + cc, et * 128:(et + 1) * 128],
                                                rhs=qt[:, h * 2 + cc, :], start=False, stop=(cc == 1)))
                            P.op('pe', fn, r=[kv, ('sw', d), ('Cb', d), ('qt', d)], w=[pnk])
                            P.op('dve', I('tensor_tensor', out=hst[:, h * 4:(h + 1) * 4, :], in0=pn[:],
                                          in1=rden[:, h:h + 1, :].broadcast_to([128, 4, 128]), op=ALU.mult),
                                 r=[pnk, ('rden', d)], w=[('hst', d, b)])
                        hdst = hfT if d == 0 else hbT
                        P.dma('sp', f'hst{d}{b}', fm(hdst[:, c * 128:(c + 1) * 128]), hst[:], r=[('hst', d, b)])
                    yield
                    P.op('pe', [I('matmul', out=pmisc[:, 4 + j:5 + j], lhsT=wk[:, j * 128:(j + 1) * 128], rhs=onesb[:, 0:1],
                                  start=True, stop=True) for j in range(8)], r=[('wk', d), 'onesb'], w=['pmisc'])
                    dcol = (lambda h: E[:, h, tl:tl + 1]) if full else (lambda h: sd['pp'][:, 3, h:h + 1])
                    dkey = ('E', d) if full else ('pp3', d)
                    for h in range(4):
                        P.op('dve', I('scalar_tensor_tensor', out=n[:, h * 2:h * 2 + 2], in0=n[:, h * 2:h * 2 + 2],
                                      scalar=dcol(h), in1=pmisc[:, 4 + h * 2:6 + h * 2], op0=ALU.mult, op1=ALU.add),
                             r=[('n', d), dkey, 'pmisc'], w=[('n', d)])
                    for h in range(4):
                        yield
                        for cc in range(2):
                            j = h * 2 + cc
                            P.op('pe', I('matmul', out=pdC[:, :], lhsT=wk[:, j * 128:(j + 1) * 128],
                                         rhs=vt[:, h * 512:(h + 1) * 512], start=True, stop=True),
                                 r=[('wk', d), kv], w=['pdC'])
                            P.op('dve', I('scalar_tensor_tensor', out=C[:, j, :], in0=C[:, j, :], scalar=dcol(h),
                                          in1=pdC[:, :], op0=ALU.mult, op1=ALU.add),
                                 r=[('C', d, j), dkey, 'pdC'], w=[('C', d, j)])
                            if full:
                                P.op('act', I('copy', out=Cb[:, j, :], in_=C[:, j, :]), r=[('C', d, j)], w=[('Cb', d)])
                    if full:
                        P.op('dve', I('tensor_copy', out=nB[:], in_=n[:].unsqueeze(2).broadcast_to([128, 8, 128])),
                             r=[('n', d)], w=[('nB', d)])

                for d in range(2):
                    scan_load(d, 0, False)
                def run_pair(step, full):
                    gens = [scan_step(d, step, full) for d in range(2)]
                    while gens:
                        for g_ in list(gens):
                            try:
                                next(g_)
                            except StopIteration:
                                gens.remove(g_)

                for step in range(NCH):
                    for d in range(2):
                        if step + 1 < NCH:
                            scan_load(d, step + 1, False)
                    run_pair(step, False)
                for d in range(2):
                    sd = S[d]
                    P.dma('sp', f'xs{d}0', pkS[:, d * SW:d * SW + 4096], sd['C'][:].rearrange("p j e -> p (j e)"),
                          r=[('C', d, j) for j in range(8)])
                    P.dma('sp', f'xs{d}1', pkS[:, d * SW + 4096:d * SW + 4104], sd['n'][:], r=[('n', d)])
                    P.dma('sp', f'xs{d}2', pkS[:, d * SW + 4104:d * SW + 4108], sd['ld'][:], r=[('ld', d)])
                P.barrier()
                P.coll(pkS_t.ap().opt(), gS_t.ap().opt())
                P.barrier()
                li = 0
                for d in range(2):
                    sd = S[d]
                    k1, k2 = (0, 2) if d == 0 else (1, 3)
                    Cf = sd['C'][:].rearrange("p j e -> p (j e)")
                    for half in (1, 0):
                        col0 = d * SW + (2048 if half == 1 else 0)
                        ncol = XH if half == 1 else 2048
                        srcs = []
                        for c8 in range(8):
                            lb = li % 2
                            li += 1
                            srcs.append((c8, lb))
                        for (dst, dkey, kk) in ((xa1, 'xa1', k1), (xa2, 'xa2', k2)):
                            for c8 in range(8):
                                lb = c8 % 2
                                P.dma('sp', f'xlb{lb}', xlb[lb][:, 0:ncol], gS[c8 * 128:(c8 + 1) * 128, col0:col0 + ncol],
                                      w=[('xlb', lb)])
                                if c8 == 0:
                                    P.op('dve', I('tensor_scalar', out=dst[:, 0:ncol], in0=xlb[lb][:, 0:ncol],
                                                  scalar1=sel[:, kk, 0:1], scalar2=None, op0=ALU.mult),
                                         r=[('xlb', lb), 'sel'], w=[dkey])
                                else:
                                    P.op('dve', I('scalar_tensor_tensor', out=dst[:, 0:ncol], in0=xlb[lb][:, 0:ncol],
                                                  scalar=sel[:, kk, c8:c8 + 1], in1=dst[:, 0:ncol], op0=ALU.mult, op1=ALU.add),
                                         r=[('xlb', lb), 'sel', dkey], w=[dkey])
                        if half == 1:
                            P.op('act', I('activation', out=xdec[:], in_=xa1[:, 2056:2060], func=AF.Exp), r=['xa1'], w=['xdec'])
                            hs_ = (2, 3)
                            P.op('dve', I('tensor_tensor', out=sd['n'][:].rearrange("p (h c) -> p h c", h=4),
                                          in0=xa2[:, 2048:2056].rearrange("p (h c) -> p h c", h=4),
                                          in1=xdec[:].unsqueeze(2).broadcast_to([128, 4, 2]), op=ALU.mult),
                                 r=['xa2', 'xdec'], w=[('n', d)])
                            P.op('dve', I('tensor_tensor', out=sd['n'][:], in0=sd['n'][:], in1=xa1[:, 2048:2056], op=ALU.add),
                                 r=[('n', d), 'xa1'], w=[('n', d)])
                        else:
                            hs_ = (0, 1)
                        for hi, h in enumerate(hs_):
                            P.op('dve', I('scalar_tensor_tensor', out=Cf[:, h * 1024:(h + 1) * 1024],
                                          in0=xa2[:, hi * 1024:(hi + 1) * 1024], scalar=xdec[:, h:h + 1],
                                          in1=xa1[:, hi * 1024:(hi + 1) * 1024], op0=ALU.mult, op1=ALU.add),
                                 r=['xa1', 'xa2', 'xdec'], w=[('C', d, 2 * h), ('C', d, 2 * h + 1)])
                    P.op('act', I('copy', out=sd['Cb'][:], in_=sd['C'][:]), r=[('C', d, j) for j in range(8)], w=[('Cb', d)])
                    P.op('dve', I('tensor_copy', out=sd['nB'][:], in_=sd['n'][:].unsqueeze(2).broadcast_to([128, 8, 128])),
                         r=[('n', d)], w=[('nB', d)])
                for d in range(2):
                    scan_load(d, 0)
                for step in range(NCH):
                    for d in range(2):
                        if step + 1 < NCH:
                            scan_load(d, step + 1)
                    run_pair(step, True)
                P.barrier()
                P.emit()

            with ExitStack() as st:
                hgn = sb(st, "hgn", [128, 16], F32)
                hf = sb(st, "hf", [128, 4, 512], F32)
                hb = sb(st, "hb", [128, 4, 512], F32)
                hs = sb(st, "hs", [128, 4, 512], F32)
                hsq = [sb(st, f"hsq{i}", [128, 512], F32) for i in range(2)]
                sdv = sb(st, "sdv", [128, 512], F32)
                rs = sb(st, "rs", [128, 512], F32)
                t1 = [sb(st, f"t1{i}", [128, 512], F32) for i in range(2)]
                og = [sb(st, f"og{i}", [128, 4, 512], BF16) for i in range(2)]
                hg = sb(st, "hg", [128, 16, 512], BF16)
                gBt = [sb(st, f"gBt{i}", [128, 4, 512], BF16) for i in range(2)]
                mxa = [sb(st, f"mxa{i}", [128, 4, 512], F32) for i in range(2)]
                mx = sb(st, "mx", [128, 16, 512], BF16)
                xres = [sb(st, f"xres{i}", [128, 512], F32) for i in range(4)]
                xm = [sb(st, f"xm{i}", [128, 512], F32) for i in range(4)]
                slots = [sb(st, f"ring{i}", [128, KC, 512], BF16) for i in range(3)]
                pms = ps(st, "pms", [128, 512])
                mm = [ps(st, f"mm{i}", [128, 512], F32) for i in range(6)]
                ring = Ring(P, slots)
                cnt = dict(mm=0, o=0, g=0, x=0)

                def next_mm():
                    i = cnt['mm'] % len(mm)
                    cnt['mm'] += 1
                    return mm[i], ('mm', i)

                P.dma('sp', 'hgn', hgn[:], p_hg[l], w=['hgn'])

                def c1_tile(ti, c0, n):
                    nt = n * 128
                    t0 = c0 * 128
                    for h in range(4):
                        ob = cnt['o'] % 2
                        cnt['o'] += 1
                        rows = slice(h * 512, (h + 1) * 512)
                        P.dma('sp', 'hf', hf[:, :, 0:nt], fm(hfT[rows, t0:t0 + nt]), w=['hf'])
                        P.dma('sp', 'hb', hb[:, :, 0:nt], fm(hbT[rows, t0:t0 + nt]), w=['hb'])
                        P.dma('sp', f'og{ob}', og[ob][:, :, 0:nt], fm(ogT[rows, t0:t0 + nt]), w=[('og', ob)])
                        P.op('dve', I('tensor_tensor', out=hs[:, :, 0:nt], in0=hf[:, :, 0:nt], in1=hb[:, :, 0:nt], op=ALU.add),
                             r=['hf', 'hb'], w=['hs'])
                        for et in range(4):
                            qb = et % 2
                            P.op('act', I('activation', out=hsq[qb][:, 0:nt], in_=hs[:, et, 0:nt], func=AF.Square),
                                 r=['hs'], w=[('hsq', qb)])
                            P.op('pe', I('matmul', out=pms[:, 0:nt], lhsT=ones512[:], rhs=hsq[qb][:, 0:nt],
                                         start=(et == 0), stop=(et == 3)), r=[('hsq', qb), 'ones512'], w=['pms'])
                        P.op('act', I('activation', out=sdv[:, 0:nt], in_=pms[:, 0:nt], func=AF.Sqrt, bias=epsc[:]),
                             r=['pms', 'epsc'], w=['sdv'])
                        P.op('dve', I('reciprocal', out=rs[:, 0:nt], in_=sdv[:, 0:nt]), r=['sdv'], w=['rs'])
                        for et in range(4):
                            tb = et % 2
                            P.op('dve', I('tensor_tensor', out=t1[tb][:, 0:nt], in0=hs[:, et, 0:nt], in1=rs[:, 0:nt], op=ALU.mult),
                                 r=['hs', 'rs'], w=[('t1', tb)])
                            P.op('dve', I('scalar_tensor_tensor', out=hg[:, h * 4 + et, 0:nt], in0=t1[tb][:, 0:nt],
                                          scalar=hgn[:, h * 4 + et:h * 4 + et + 1], in1=og[ob][:, et, 0:nt],
                                          op0=ALU.mult, op1=ALU.mult),
                                 r=[('t1', tb), 'hgn', ('og', ob)], w=['hg'])
                    for blk in range(4):
                        gb = cnt['g'] % 2
                        cnt['g'] += 1
                        P.dma('sp', f'gBt{gb}', gBt[gb][:, :, 0:nt], fm(gmT[D + blk * 512:D + (blk + 1) * 512, t0:t0 + nt]),
                              w=[('gBt', gb)])
                        P.dma('sp', f'mxa{gb}', mxa[gb][:, :, 0:nt], fm(mixT[blk * 512:(blk + 1) * 512, t0:t0 + nt]),
                              w=[('mxa', gb)])
                        slot, skey = ring.load([(w_mlstm_out[l][:, blk * 512:(blk + 1) * 512], KC, 0, 512)])
                        for ct in range(4):
                            pt, pk = next_mm()
                            P.op('pe', [I('matmul', out=pt[:, 0:nt], lhsT=slot[:, kc, ct * 128:(ct + 1) * 128], rhs=hg[:, kc, 0:nt],
                                          start=(kc == 0), stop=(kc == KC - 1)) for kc in range(KC)],
                                 r=[skey, 'hg'], w=[pk])
                            tb = ct % 2
                            P.op('dve', I('tensor_tensor', out=t1[tb][:, 0:nt], in0=pt[:, 0:nt], in1=gBt[gb][:, ct, 0:nt],
                                          op=ALU.mult), r=[pk, ('gBt', gb)], w=[('t1', tb)])
                            P.op('dve', I('tensor_tensor', out=mx[:, blk * 4 + ct, 0:nt], in0=t1[tb][:, 0:nt],
                                          in1=mxa[gb][:, ct, 0:nt], op=ALU.add),
                                 r=[('t1', tb), ('mxa', gb)], w=['mx'])
                    for blk in range(4):
                        slot, skey = ring.load([(w_out[l][:, blk * 512:(blk + 1) * 512], KC, 0, 512)])
                        for ci in range(n):
                            xb = cnt['x'] % 4
                            cnt['x'] += 1
                            r0 = t0 + ci * 128
                            P.dma('sp', f'xres{xb}', xres[xb][:], xsrc[r0:r0 + 128, blk * 512:(blk + 1) * 512], w=[('xres', xb)])
                            pt, pk = next_mm()
                            P.op('pe', [I('matmul', out=pt[:, :], lhsT=mx[:, kc, ci * 128:(ci + 1) * 128], rhs=slot[:, kc, :],
                                          start=(kc == 0), stop=(kc == KC - 1)) for kc in range(KC)],
                                 r=[skey, 'mx'], w=[pk])
                            P.op('dve', I('scalar_tensor_tensor', out=xm[xb][:], in0=pt[:, :], scalar=mcol[:, c0 + ci:c0 + ci + 1],
                                          in1=xres[xb][:], op0=ALU.mult, op1=ALU.add),
                                 r=[pk, ('xres', xb), 'mcol'], w=[('xm', xb)])
                            P.dma('sp', f'xm{xb}', xmid[1 + r0:1 + r0 + 128, blk * 512:(blk + 1) * 512], xm[xb][:],
                                  r=[('xm', xb)])

                for ti, (c0, n) in enumerate(atiles):
                    c1_tile(ti, c0, n)
                P.barrier()
                P.emit()
            exchange_x()

            with ExitStack() as st:
                gF = sb(st, "gF", [128, D], F32)
                gL = sb(st, "gL", [128, D], F32) if last else None
                fw = sb(st, "fw", [128, NF, 4], F32)
                xl = sb(st, "xl", [128, D], F32)
                xnb = sb(st, "xnb2", [128, D], BF16)
                sqj = sb(st, "sqj2", [128, D], BF16)
                stat = sb(st, "stat2", [128, 4], F32)
                xnT = sb(st, "xn2T", [128, KC, 512], BF16)
                gt = [sb(st, f"gt{i}", [128, 512], F32) for i in range(2)]
                cvt = [sb(st, f"cvt{i}", [128, 512], F32) for i in range(2)]
                hid = sb(st, "hid", [128, NF, 512], BF16)
                xrow = [sb(st, f"xrow{i}", [128, D], F32) for i in range(4)]
                slots = [sb(st, f"ring{i}", [128, KC, 512], BF16) for i in range(3)]
                tp = ps(st, "tp", [128, 1024], BF16)
                gbk = [ps(st, f"gbk{i}", [128, 512]) for i in range(7)]
                pg = [(gbk[0], ('gbk', 0)), (gbk[1], ('gbk', 1))]
                pv = [(gbk[2], ('gbk', 2)), (gbk[3], ('gbk', 3))]
                pd = [(gbk[4], ('gbk', 4)), (gbk[5], ('gbk', 5)), (gbk[6], ('gbk', 6)), (gbk[3], ('gbk', 3))]
                ring = Ring(P, slots)
                cnt = dict(pg=0)

                P.dma('sp', 'gF', gF[:], norm_ffn_g[l].partition_broadcast(128), w=['gvec'])
                if last:
                    P.dma('sp', 'gL', gL[:], norm_final_g.partition_broadcast(128), w=['gL'])
                P.dma('sp', 'fw', fw[:], p_fw[l], w=['fw'])

                ftiles = []
                s_ = 0
                while s_ < NT:
                    nv_ = min(510, NT - s_)
                    ftiles.append((s_, nv_))
                    s_ += nv_

                def c2_tile(s, nv):
                    ncol = nv + 2
                    nblk = (ncol + 127) // 128
                    for bi in range(nblk):
                        nr = min(128, ncol - bi * 128)
                        P.dma('sp', 'xl', xl[0:nr, :], xmid[s + bi * 128:s + bi * 128 + nr, :], w=['xl'])
                        rmsnorm_rows(xl, nr, 'xl', gF, 'gvec', xnb, 'xnb', stat, sqj)
                        for g in range(2):
                            P.op('pe', [I('transpose', out=tp[:, k * 128:k * 128 + nr],
                                          in_=xnb[0:nr, (g * 8 + k) * 128:(g * 8 + k + 1) * 128], identity=identb[0:nr, 0:nr])
                                        for k in range(8)], r=['xnb', 'identb'], w=['tp'])
                            dst = xnT[:, g * 8:(g + 1) * 8, bi * 128:bi * 128 + nr]
                            src = tp[:].rearrange("p (k t) -> p k t", k=8)[:, :, 0:nr]
                            if g == 0:
                                P.op('act', I('copy', out=dst, in_=src), r=['tp'], w=['xnT'])
                            else:
                                P.op('dve', I('tensor_copy', out=dst, in_=src), r=['tp'], w=['xnT'])
                    for j0 in range(0, NF, 2):
                        nj = min(2, NF - j0)
                        slot, skey = ring.load([(w_up[l][:, j0 * 128:(j0 + nj) * 128], KC, 0, nj * 128),
                                                (w_up[l][:, DFF + j0 * 128:DFF + (j0 + nj) * 128], KC, 256, nj * 128)])
                        for jj in range(nj):
                            j = j0 + jj
                            pb = cnt['pg'] % 2
                            cnt['pg'] += 1
                            pgt, pgk = pg[pb]
                            pvt, pvk = pv[pb]
                            P.op('pe', [I('matmul', out=pgt[:, 0:ncol], lhsT=slot[:, kc, jj * 128:(jj + 1) * 128], rhs=xnT[:, kc, 0:ncol],
                                          start=(kc == 0), stop=(kc == KC - 1)) for kc in range(KC)],
                                 r=[skey, 'xnT'], w=[pgk])
                            P.op('pe', [I('matmul', out=pvt[:, 0:ncol], lhsT=slot[:, kc, 256 + jj * 128:256 + (jj + 1) * 128],
                                          rhs=xnT[:, kc, 0:ncol], start=(kc == 0), stop=(kc == KC - 1)) for kc in range(KC)],
                                 r=[skey, 'xnT'], w=[pvk])
                            P.op('act', I('copy', out=gt[pb][:, 0:ncol], in_=pgt[:, 0:ncol]), r=[pgk], w=[('gt', pb)])
                            P.op('dve', I('tensor_scalar', out=cvt[pb][:, 0:nv], in0=gt[pb][:, 0:nv], scalar1=fw[:, j, 0:1],
                                          scalar2=fw[:, j, 3:4], op0=ALU.mult, op1=ALU.add),
                                 r=[('gt', pb), 'fw'], w=[('cvt', pb)])
                            for wt in (1, 2):
                                P.op('dve', I('scalar_tensor_tensor', out=cvt[pb][:, 0:nv], in0=gt[pb][:, wt:wt + nv],
                                              scalar=fw[:, j, wt:wt + 1], in1=cvt[pb][:, 0:nv], op0=ALU.mult, op1=ALU.add),
                                     r=[('gt', pb), 'fw', ('cvt', pb)], w=[('cvt', pb)])
                            P.op('act', I('activation', out=cvt[pb][:, 0:nv], in_=cvt[pb][:, 0:nv], func=AF.Gelu),
                                 r=[('cvt', pb)], w=[('cvt', pb)])
                            P.op('dve', I('tensor_tensor', out=hid[:, j, 0:nv], in0=cvt[pb][:, 0:nv], in1=pvt[:, 1:1 + nv],
                                          op=ALU.mult), r=[('cvt', pb), pvk], w=['hid'])
                    ntb = (nv + 127) // 128
                    kparts = [(0, 15), (15, 15), (30, 13)]
                    nrs = [min(128, nv - tbi * 128) for tbi in range(ntb)]
                    for tbi in range(ntb):
                        tok0 = s + tbi * 128
                        P.dma('sp', f'xrow{tbi}', xrow[tbi][0:nrs[tbi], :], xmid[1 + tok0:1 + tok0 + nrs[tbi], :], w=[('xrow', tbi)])
                    for blk in range(4):
                        for kp, (k0, nk) in enumerate(kparts):
                            slot, skey = ring.load([(w_down[l][k0 * 128:(k0 + nk) * 128, blk * 512:(blk + 1) * 512], nk, 0, 512)])
                            for tbi in range(ntb):
                                nr = nrs[tbi]
                                pdt, pdk = pd[tbi]
                                P.op('pe', [I('matmul', out=pdt[0:nr, :], lhsT=hid[:, k0 + kk, tbi * 128:tbi * 128 + nr],
                                              rhs=slot[:, kk, :], start=(kp == 0 and kk == 0), stop=(kp == 2 and kk == nk - 1))
                                            for kk in range(nk)], r=[skey, 'hid'], w=[pdk])
                        for tbi in range(ntb):
                            nr = nrs[tbi]
                            pdt, pdk = pd[tbi]
                            P.op('dve', I('tensor_tensor', out=xrow[tbi][0:nr, blk * 512:(blk + 1) * 512], in0=pdt[0:nr, :],
                                          in1=xrow[tbi][0:nr, blk * 512:(blk + 1) * 512], op=ALU.add),
                                 r=[pdk, ('xrow', tbi)], w=[('xrow', tbi)])
                    for tbi in range(ntb):
                        nr = nrs[tbi]
                        tok0 = s + tbi * 128
                        if last:
                            rmsnorm_rows(xrow[tbi], nr, ('xrow', tbi), gL, 'gL', xrow[tbi], ('xrow', tbi), stat, sqj)
                        P.dma('sp', f'xrow{tbi}', xdst[tok0:tok0 + nr, :], xrow[tbi][0:nr, :], r=[('xrow', tbi)])

                for (s, nv) in ftiles:
                    c2_tile(s, nv)
                P.barrier()
                P.emit()
        print("bass ops recorded:", P.nops)
    return nc


def _host_layout(inputs, depth=DEPTH):
    f = lambda a: np.ascontiguousarray(np.asarray(a, dtype=np.float32))
    cw = f(inputs['conv_dw_w'])
    p_cw = np.ascontiguousarray(cw.reshape(depth, CW, 8, 128).transpose(0, 3, 2, 1))
    cvs = np.stack([f(inputs['conv_dw_b']), f(inputs['conv_ln_g']), f(inputs['conv_ln_b'])], axis=1)
    p_cv = np.ascontiguousarray(cvs.reshape(depth, 3, 8, 128).transpose(0, 3, 1, 2))
    p_hg = np.ascontiguousarray(f(inputs['mlstm_head_g']).reshape(depth, 16, 128).transpose(0, 2, 1))
    fwb = np.concatenate([f(inputs['ffn_dw_w']), f(inputs['ffn_dw_b'])[:, None, :]], axis=1)
    p_fw = np.ascontiguousarray(fwb.reshape(depth, 4, NF, 128).transpose(0, 3, 2, 1))
    k = np.arange(128)
    triU = (k[:, None] <= k[None, :]).astype(np.float32)
    triL = (k[:, None] >= k[None, :]).astype(np.float32)
    c_tri = np.stack([triU, triL, (1 - triU) * NEG, (1 - triL) * NEG]).astype(np.float32)
    common = dict(
        w_in=f(inputs['w_in']), w_conv_out=f(inputs['w_conv_out']), w_mlstm_out=f(inputs['w_mlstm_out']),
        w_out=f(inputs['w_out']), w_up=f(inputs['w_up']), w_down=f(inputs['w_down']),
        norm_mix_g=f(inputs['norm_mix_g']), norm_ffn_g=f(inputs['norm_ffn_g']), norm_final_g=f(inputs['norm_final_g']),
        b_gates=f(inputs['b_gates']), p_cw=p_cw, p_cv=p_cv, p_hg=p_hg, p_fw=p_fw, c_tri=c_tri)
    return common


def run_segments(segs, NCH, inputs, chains=None, depth=DEPTH):
    NT = NCH * 128
    common = _host_layout(inputs, depth)
    if chains is None:
        chains = [[i] for i in range(8)]
    sel = np.zeros((8, 4, 8), np.float32)
    for ch in chains:
        for i, c in enumerate(ch):
            if i >= 1:
                sel[c, 0, ch[i - 1]] = 1.0
            if i + 1 < len(ch):
                sel[c, 1, ch[i + 1]] = 1.0
                assert segs[c].shape[0] == NT
            if i >= 2:
                sel[c, 2, ch[i - 2]] = 1.0
            if i + 2 < len(ch):
                sel[c, 3, ch[i + 2]] = 1.0
    in_maps = []
    for c, sgm in enumerate(segs):
        x = np.zeros((NT, D), np.float32)
        m = np.zeros((NT,), np.float32)
        if sgm is not None:
            x[:sgm.shape[0]] = sgm
            m[:sgm.shape[0]] = 1.0
        dct = dict(common)
        dct['x'] = x
        dct['msk'] = np.ascontiguousarray(m.reshape(NCH, 128).T)
        dct['sel'] = np.ascontiguousarray(np.broadcast_to(sel[c][None], (128, 4, 8)))
        in_maps.append(dct)
    nc = build_program(NCH, depth)
    res = run_bass_kernel_spmd(nc, in_maps, core_ids=list(range(8)))
    return [r["y"] for r in res.results]


NCH_CORE = 43


def kernel(x_prompt, x_sample, **params):
    x_prompt = np.asarray(x_prompt, dtype=np.float32)
    x_sample = np.asarray(x_sample, dtype=np.float32)
    NT = NCH_CORE * 128
    S = x_sample.shape[1]
    cuts = [0, NT, 2 * NT, S]
    segs = [x_sample[b, cuts[i]:cuts[i + 1]] for b in range(2) for i in range(3)] + [x_prompt[0], x_prompt[1]]
    chains = [[0, 1, 2], [3, 4, 5], [6], [7]]
    outs = run_segments(segs, NCH_CORE, params, chains)
    y_sample = np.stack([np.concatenate([outs[b * 3 + i][:cuts[i + 1] - cuts[i]] for i in range(3)], axis=0)
                         for b in range(2)]).astype(np.float32)
    Pn = x_prompt.shape[1]
    y_prompt = np.stack([outs[6][:Pn], outs[7][:Pn]]).astype(np.float32)
    return (y_prompt, y_sample)
```
